# Optimizing a Trainium2 kernel written in Bass

```python
import jax, jax.numpy as jnp
from jax import lax
import numpy as np

D_MODEL = 1024
BATCH = 4
SEQ = 8192
DEPTH = 4

CHUNK = 64
EXPAND = 2
D_MIX = EXPAND * D_MODEL
D_A = D_MIX // 2
D_B = D_MIX - D_A
D_C = D_MIX // 2
D_D = D_MIX - D_C
POOL_WINDOWS = (2, 4, 8, 16)
N_POOL_GROUPS = len(POOL_WINDOWS)
POOL_GROUP = D_A // N_POOL_GROUPS
SHORT_CONV = 3
SGU_BLOCK = 128
SGU_HEADS = 4
SGU_HEAD_DIM = D_C // SGU_HEADS
CONF_CONV = 31
IN_EVEN = 2 * D_A + 4 * D_B
IN_ODD = 3 * D_C + 3 * D_D
N_EVEN = (DEPTH + 1) // 2
N_ODD = DEPTH // 2
DEEPNORM_ALPHA = (2 * DEPTH) ** 0.25
DEEPNORM_BETA = (8 * DEPTH) ** -0.25
LN_EPS = 1e-5

kernel_name = "hybrid_pool_conv_sgu_conformer_deepnorm"


def layer_norm(x, g, b):
    xf = x.astype(jnp.float32)
    mu = jnp.mean(xf, axis=-1, keepdims=True)
    var = jnp.mean(jnp.square(xf - mu), axis=-1, keepdims=True)
    return ((xf - mu) * lax.rsqrt(var + LN_EPS) * g.astype(jnp.float32) + b.astype(jnp.float32)).astype(x.dtype)


def split_cols(z, sizes):
    points = list(np.cumsum(sizes)[:-1])
    return jnp.split(z, points, axis=-1)


def causal_depthwise_conv(z, w, b):
    k = w.shape[0]
    y = lax.conv_general_dilated(
        z, w[:, None, :].astype(z.dtype), window_strides=(1,), padding=[(k - 1, 0)],
        dimension_numbers=('NWC', 'WIO', 'NWC'), feature_group_count=z.shape[-1])
    return y + b.astype(z.dtype)


def multi_scale_pool(z):
    s = z.shape[1]
    zf = z.astype(jnp.float32)
    cs0 = jnp.pad(jnp.cumsum(zf, axis=1), ((0, 0), (1, 0), (0, 0)))
    pos = jnp.arange(1, s + 1, dtype=jnp.float32)
    means = []
    for g, w in enumerate(POOL_WINDOWS):
        cg = cs0[..., g * POOL_GROUP:(g + 1) * POOL_GROUP]
        lagged = jnp.pad(cg[:, :s + 1 - w], ((0, 0), (w - 1, 0), (0, 0)))
        count = jnp.minimum(pos, float(w))[None, :, None]
        means.append((cg[:, 1:] - lagged) / count)
    return (jnp.concatenate(means, axis=-1) - zf).astype(z.dtype)


def sgu_mask():
    idx = jnp.arange(SGU_BLOCK)
    return (idx[None, :] // CHUNK) <= (idx[:, None] // CHUNK)


def pool_conv_layer(x, w_in, w_out, pool_w, pool_scale, sconv_w, sconv_b):
    bsz, s, _ = x.shape
    xa, ga, h, bg, cg, gb = split_cols(x @ w_in, [D_A, D_A, D_B, D_B, D_B, D_B])
    pooled = multi_scale_pool(xa).reshape(bsz, s, N_POOL_GROUPS, POOL_GROUP)
    ya = jnp.einsum('bsgc,gcd->bsgd', pooled, pool_w).reshape(bsz, s, D_A) * pool_scale
    ya = ya * jax.nn.silu(ga)
    yb = bg * causal_depthwise_conv(cg * h, sconv_w, sconv_b)
    yb = yb * jax.nn.silu(gb)
    return jnp.concatenate([ya, yb], axis=-1) @ w_out


def sgu_conformer_layer(x, w_in, w_out, sgu_ln_g, sgu_ln_b, sgu_w, sgu_b,
                        dconv_w, dconv_b, dnorm_g, dnorm_b):
    bsz, s, _ = x.shape
    u, v, gc, a, bglu, gd = split_cols(x @ w_in, [D_C, D_C, D_C, D_D, D_D, D_D])
    v = layer_norm(v, sgu_ln_g, sgu_ln_b)
    v = v.reshape(bsz, s // SGU_BLOCK, SGU_BLOCK, SGU_HEADS, SGU_HEAD_DIM)
    ws = jnp.where(sgu_mask()[None], sgu_w, 0.0).astype(v.dtype)
    sv = jnp.einsum('hij,bnjhc->bnihc', ws, v) + jnp.transpose(sgu_b)[:, :, None].astype(v.dtype)
    yc = u * sv.reshape(bsz, s, D_C) * jax.nn.silu(gc)
    z = a * jax.nn.sigmoid(bglu)
    z = causal_depthwise_conv(z, dconv_w, dconv_b)
    z = jax.nn.silu(layer_norm(z, dnorm_g, dnorm_b))
    yd = z * jax.nn.silu(gd)
    return jnp.concatenate([yc, yd], axis=-1) @ w_out


def setup_inputs(seed: int = 0) -> dict:
    key = jax.random.key(seed)
    ks = jax.random.split(key, 20)
    f32 = jnp.float32
    nrm = lambda k, shp, sc: (jax.random.normal(k, shp, f32) * sc).astype(f32)
    return {
        "x": nrm(ks[0], (BATCH, SEQ, D_MODEL), 1.0),
        "ln_g": 1.0 + nrm(ks[1], (DEPTH, D_MODEL), 0.02),
        "ln_b": nrm(ks[2], (DEPTH, D_MODEL), 0.02),
        "w_in_even": nrm(ks[3], (N_EVEN, D_MODEL, IN_EVEN), D_MODEL ** -0.5),
        "w_out_even": nrm(ks[4], (N_EVEN, D_MIX, D_MODEL), DEEPNORM_BETA * D_MIX ** -0.5),
        "pool_w": nrm(ks[5], (N_EVEN, N_POOL_GROUPS, POOL_GROUP, POOL_GROUP), POOL_GROUP ** -0.5),
        "pool_scale": 1.0 + nrm(ks[6], (N_EVEN, D_A), 0.1),
        "sconv_w": nrm(ks[7], (N_EVEN, SHORT_CONV, D_B), SHORT_CONV ** -0.5),
        "sconv_b": nrm(ks[8], (N_EVEN, D_B), 0.02),
        "w_in_odd": nrm(ks[9], (N_ODD, D_MODEL, IN_ODD), D_MODEL ** -0.5),
        "w_out_odd": nrm(ks[10], (N_ODD, D_MIX, D_MODEL), DEEPNORM_BETA * D_MIX ** -0.5),
        "sgu_ln_g": 1.0 + nrm(ks[11], (N_ODD, D_C), 0.02),
        "sgu_ln_b": nrm(ks[12], (N_ODD, D_C), 0.02),
        "sgu_w": nrm(ks[13], (N_ODD, SGU_HEADS, SGU_BLOCK, SGU_BLOCK), SGU_BLOCK ** -0.5),
        "sgu_b": 1.0 + nrm(ks[14], (N_ODD, SGU_HEADS, SGU_BLOCK), 0.01),
        "dconv_w": nrm(ks[15], (N_ODD, CONF_CONV, D_D), CONF_CONV ** -0.5),
        "dconv_b": nrm(ks[16], (N_ODD, D_D), 0.02),
        "dnorm_g": 1.0 + nrm(ks[17], (N_ODD, D_D), 0.02),
        "dnorm_b": nrm(ks[18], (N_ODD, D_D), 0.02),
    }


def reference(x, ln_g, ln_b, w_in_even, w_out_even, pool_w, pool_scale, sconv_w, sconv_b,
              w_in_odd, w_out_odd, sgu_ln_g, sgu_ln_b, sgu_w, sgu_b,
              dconv_w, dconv_b, dnorm_g, dnorm_b):
    for layer in range(DEPTH):
        i = layer // 2
        if layer % 2 == 0:
            y = pool_conv_layer(x, w_in_even[i], w_out_even[i], pool_w[i], pool_scale[i],
                                sconv_w[i], sconv_b[i])
        else:
            y = sgu_conformer_layer(x, w_in_odd[i], w_out_odd[i], sgu_ln_g[i], sgu_ln_b[i],
                                    sgu_w[i], sgu_b[i], dconv_w[i], dconv_b[i],
                                    dnorm_g[i], dnorm_b[i])
        x = layer_norm(DEEPNORM_ALPHA * x + y, ln_g[layer], ln_b[layer])
    return x
```

```python
import numpy as np
import ml_dtypes
from contextlib import ExitStack
import concourse.bass as bass
import concourse.mybir as mybir
from concourse.bass_utils import run_bass_kernel_spmd

F32 = mybir.dt.float32
BF16 = mybir.dt.bfloat16
AF = mybir.ActivationFunctionType
ALU = mybir.AluOpType

D = 1024
NSUB = 34
HALO = 2
NTOK = NSUB * 128
TILES = [(0, 2)] + [(2 + 4 * i, 4) for i in range(8)]
ALPHA = float((2 * 4) ** 0.25)
EPS = 1e-5
POOL_W = (2, 4, 8, 16)
ENGS = ("pe", "act", "dve", "pool", "sp")
DEFER_C_ODD = False
DEFER_C_EVEN = False
LITE_HALO = True
DIAG_ENG = "dve"
KDVE = 0


class Op:
    __slots__ = ("eng", "fn", "reads", "writes", "chan", "idx", "signal", "token", "waits")

    def __init__(self, eng, fn, reads, writes, chan):
        self.eng = eng
        self.fn = fn
        self.reads = reads
        self.writes = writes
        self.chan = chan
        self.signal = False
        self.token = None
        self.waits = []


class Prog:
    def __init__(self, nc):
        self.nc = nc
        self.ops = []

    def op(self, eng, fn, reads=(), writes=(), chan=None):
        o = Op(eng, fn, tuple(reads), tuple(writes), chan)
        o.idx = len(self.ops)
        self.ops.append(o)
        return o

    def dma(self, fn, reads=(), writes=(), chan=None, eng="sp"):
        assert chan is not None
        return self.op(eng, fn, reads, writes, chan)

    def finalize(self, stack):
        nc = self.nc
        ops = self.ops
        last_w = {}
        readers = {}
        deps = [None] * len(ops)
        chan_last = {}
        for o in ops:
            d = set()
            for k in o.reads:
                w = last_w.get(k)
                if w is not None:
                    d.add(w)
            for k in o.writes:
                w = last_w.get(k)
                if w is not None:
                    d.add(w)
                r = readers.get(k)
                if r:
                    d.update(r.values())
            if o.chan is not None:
                p = chan_last.get(o.chan)
                if p is not None:
                    d.add(p)
                chan_last[o.chan] = o.idx
            d.discard(o.idx)
            deps[o.idx] = d
            rk = ("c", o.chan) if o.chan is not None else o.eng
            for k in o.reads:
                readers.setdefault(k, {})[rk] = o.idx
            for k in o.writes:
                last_w[k] = o.idx
                readers[k] = {}
        for o in ops:
            for j in deps[o.idx]:
                oj = ops[j]
                if oj.chan is None and oj.eng == "pe" and o.eng == "pe" and o.chan is None:
                    continue
                oj.signal = True
        for o in ops:
            if o.chan is not None:
                o.signal = True
        sem_names = set()
        for o in ops:
            if o.signal:
                sem_names.add(("c", o.chan) if o.chan is not None else ("e", o.eng))
        sems = {}
        for n in sorted(sem_names):
            sems[n] = stack.enter_context(nc.semaphore("s_%s_%s" % n))
        counts = {n: 0 for n in sem_names}
        for o in ops:
            if o.signal:
                n = ("c", o.chan) if o.chan is not None else ("e", o.eng)
                counts[n] += 16 if o.chan is not None else 1
                o.token = (n, counts[n])
        self.max_counts = dict(counts)
        seen = {e: {} for e in ENGS}
        for o in ops:
            need = {}
            for j in deps[o.idx]:
                oj = ops[j]
                if oj.token is None:
                    continue
                n, v = oj.token
                if v > need.get(n, 0):
                    need[n] = v
            sd = seen[o.eng]
            for n, v in need.items():
                if sd.get(n, 0) >= v:
                    continue
                sd[n] = v
                o.waits.append((n, v))
        block = stack.enter_context(nc.Block())
        per_eng = {e: [o for o in ops if o.eng == e] for e in ENGS}

        def emit(engh, lst):
            for o in lst:
                for n, v in o.waits:
                    engh.wait_ge(sems[n], v)
                ins = o.fn(engh)
                if o.signal:
                    n, v = o.token
                    ins.then_inc(sems[n], 16 if o.chan is not None else 1)

        @block.tensor
        def _(e):
            emit(e, per_eng["pe"])

        @block.scalar
        def _(e):
            emit(e, per_eng["act"])

        @block.vector
        def _(e):
            emit(e, per_eng["dve"])

        @block.gpsimd
        def _(e):
            emit(e, per_eng["pool"])

        @block.sync
        def _(e):
            emit(e, per_eng["sp"])
            for n in sorted(sem_names):
                if n[0] == "c":
                    e.wait_ge(sems[n], counts[n])


def col_layout():
    lay = {}
    pos = 0

    def add(name, n):
        nonlocal pos
        lay[name] = pos
        pos += n

    add("flag", 1)
    add("one", 1)
    for i in range(2):
        add("ps%d" % i, 8)
        add("cw%d" % i, 24)
        add("cb%d" % i, 8)
    for i in range(2):
        add("sg%d" % i, 8)
        add("dw%d" % i, 248)
        add("db%d" % i, 8)
        add("ng%d" % i, 8)
        add("nb%d" % i, 8)
    return lay, pos


COL, NCOL = col_layout()
PM_CUR, PM_PREV, PM_SPH, PM_SPL, PM_SPP = 1, 5, 9, 13, 17
NCB = 21


def build_program(n_layers=4, tiles=None):
    TILES = tiles if tiles is not None else globals()["TILES"]
    NTOK = 128 * sum(t[1] for t in TILES)
    nc = bass.Bass("TRN2", target_bir_lowering=False)
    st = ExitStack()
    with st:
        def dram(name, shape, dt, kind):
            return nc.dram_tensor(name, list(shape), dt, kind=kind).ap()

        xin = dram("xin", [NTOK, D], F32, "ExternalInput")
        out = dram("out", [NTOK - HALO * 128, D], F32, "ExternalOutput")
        consts_bf_d = dram("consts_bf", [128, NCB * 128], BF16, "ExternalInput")
        maskT_d = dram("maskT", [128, 128], F32, "ExternalInput")
        colpack_d = dram("colpack", [128, NCOL], F32, "ExternalInput")
        lng_d = dram("ln_g", [4, D], F32, "ExternalInput")
        lnb_d = dram("ln_b", [4, D], F32, "ExternalInput")
        sgb_d = dram("sgu_ln_b", [2, D], F32, "ExternalInput")
        sgub_d = dram("sgu_b", [2, 512], F32, "ExternalInput")
        wxa_d = dram("w_xa", [2, 2, 128, 8 * 512], F32, "ExternalInput")
        wev_d = dram("w_ev", [2, 8, 128, 8 * 640], F32, "ExternalInput")
        wv_d = dram("w_v", [2, 2, 128, 8 * 512], F32, "ExternalInput")
        wod_d = dram("w_od", [2, 8, 128, 8 * 640], F32, "ExternalInput")
        wout_d = dram("w_out", [4, 128, 16 * 1024], F32, "ExternalInput")
        pw_d = dram("pool_w", [2, 128, 4 * 2 * 256], F32, "ExternalInput")
        wsT_d = dram("wsT", [2, 128, 512], F32, "ExternalInput")
        xs_a = dram("xs_a", [NTOK, D], F32, "Internal")
        xs_b = dram("xs_b", [NTOK, D], F32, "Internal")
        wxa_s = dram("wxa_s", [2, 2, 128, 8 * 512], BF16, "Internal")
        wev_s = dram("wev_s", [2, 8, 128, 8 * 640], BF16, "Internal")
        wv_s = dram("wv_s", [2, 2, 128, 8 * 512], BF16, "Internal")
        wod_s = dram("wod_s", [2, 8, 128, 8 * 640], BF16, "Internal")
        wout_s = dram("wout_s", [4, 128, 16 * 1024], BF16, "Internal")

        def sb(name, shape, dt):
            return st.enter_context(nc.sbuf_tensor(name, list(shape), dt))

        cbf = sb("cbf", [128, NCB, 128], BF16)
        ident = cbf[:, 0, :]
        onesD = sb("onesD", [128, 128], BF16)
        maskT = sb("maskT_s", [128, 128], F32)
        colp = sb("colp", [128, NCOL], F32)
        smalls = sb("smalls", [128, 8], F32)
        ones_row = sb("ones_row", [1, 128], F32)
        lng = sb("lng", [128, D], F32)
        lnb = sb("lnb", [128, D], F32)
        wout_bf = sb("wout_bf", [128, 16, 1024], BF16)
        pw_bf = sb("pw_bf", [128, 4, 2, 256], BF16)
        wsT_bf = sb("wsT_bf", [128, 4, 128], BF16)
        Bias = sb("Bias", [128, 8, 128], F32)
        dg = sb("dg", [128, 2, 16, 128], BF16)
        wbuf = sb("wbuf", [128, 2, 8, 640], BF16)
        xb16 = sb("xb16", [128, 4, D], BF16)
        xT = sb("xT", [128, 8, 512], BF16)
        concat = sb("concat", [128, 2, 16, 512], BF16)
        x32 = sb("x32", [128, D], F32)
        rbuf = sb("rbuf", [128, 2, D], F32)
        stt = sb("stt", [128, 4, 12], F32)
        mvt = sb("mvt", [128, 4, 4], F32)
        wt = sb("wt", [128, 248], F32)
        acc = sb("acc", [128, 2, 512], F32)
        UN = 0

        def carve(n):
            nonlocal UN
            o = UN
            UN += n + (n % 2)
            return o

        ev = {}
        ev["xa"] = carve(5 * 1024)
        ev["pT"] = carve(4 * 512)
        ev["sga"] = carve(4 * 512)
        ev["hsb"] = carve(2 * 512)
        ev["sgb"] = carve(2 * 512)
        ev["m"] = carve(2 * 512)
        ev["pst"] = carve(8 * 514)
        ev_end = UN
        UN = 0
        od = {}
        od["vn"] = carve(4 * 1024)
        od["t1"] = carve(2 * 1024)
        od["sgc"] = carve(2 * 512)
        od["t2"] = carve(2 * 512)
        od["th"] = carve(2 * 512)
        od["zst"] = carve(8 * 542)
        od["sgd"] = carve(8 * 512)
        od["czb"] = carve(8 * 512)
        od["sqb"] = carve(2 * 512)
        od["mean"] = carve(1024)
        od["rstd"] = carve(1024)
        od["dd"] = carve(2 * 1024)
        od["s1"] = carve(2 * 512)
        od_end = UN
        U = sb("U", [128, max(ev_end, od_end)], BF16)

        def ub(off, n):
            return U[:, off:off + n]

        def uf(off, n):
            return U[:, off:off + 2 * n].bitcast(F32)

        ps = st.enter_context(nc.psum_tensor("ps", [128, 8 * 512], F32))

        def bank(b):
            return ps[:, b * 512:(b + 1) * 512]

        print("SBUF bytes remaining:", nc.sbuf_bytes_remaining)

        P = Prog(nc)
        state = {"ring": 0, "ring_n": 8, "wu": 0, "castn": 0, "stat": 0, "xb": 0, "rb": 0}

        def nbank():
            b = state["ring"] % state["ring_n"]
            state["ring"] += 1
            return b

        def col(name, idx=0):
            c = COL[name] + idx
            return colp[:, c:c + 1]

        P.dma(lambda e: e.dma_start(out=cbf[:].rearrange("p a b -> p (a b)"), in_=consts_bf_d), writes=["cbf"], chan="k0")
        P.dma(lambda e: e.dma_start(out=maskT[:], in_=maskT_d), writes=["maskT"], chan="k1")
        P.dma(lambda e: e.dma_start(out=colp[:], in_=colpack_d), writes=["colp"], chan="k2")
        P.op("pool", lambda e: e.memset(onesD[:], 1.0 / 1024.0), writes=["onesD"])
        P.op("pool", lambda e: e.memset(smalls[:], -0.5), writes=["smalls"])
        P.op("pool", lambda e: e.memset(ones_row[:], 1.0), writes=["ones_row"])

        def cast_dma(src, dst, rowlen, key):
            n = state["castn"]
            state["castn"] += 1
            sv = src.rearrange("p (a n) -> p a n", n=rowlen)
            dv = dst.rearrange("p (a n) -> p a n", n=rowlen)
            P.dma(lambda e: e.dma_start(out=dv, in_=sv), writes=[key], chan="cast%d" % (n % 8), eng="pool")

        def cast_list(l):
            i = l // 2
            lst = []
            if l % 2 == 0:
                for u in range(2):
                    lst.append((wxa_d[i, u], wxa_s[i, u], 2048, ("wsc", l, "a", u)))
                for u in range(8):
                    lst.append((wev_d[i, u], wev_s[i, u], 1024, ("wsc", l, "c", u)))
            else:
                for u in range(2):
                    lst.append((wv_d[i, u], wv_s[i, u], 2048, ("wsc", l, "a", u)))
                for u in range(8):
                    lst.append((wod_d[i, u], wod_s[i, u], 1024, ("wsc", l, "c", u)))
            for h in range(4):
                lst.append((wout_d[l][:, h * 4096:(h + 1) * 4096], wout_s[l][:, h * 4096:(h + 1) * 4096], 2048, ("wosc", l, h)))
            return lst

        def emit_casts(l):
            for a in cast_list(l):
                cast_dma(*a)

        def wunit_src(l, kind, u):
            i = l // 2
            if l % 2 == 0:
                return (wxa_s[i, u] if kind == "a" else wev_s[i, u])
            return (wv_s[i, u] if kind == "a" else wod_s[i, u])

        def load_wunit(l, kind, u, cols=None):
            k = state["wu"]
            state["wu"] += 1
            b = k % 2
            ncols = 512 if kind == "a" else 640
            src = wunit_src(l, kind, u).rearrange("p (k n) -> p k n", k=8)
            dst = wbuf[:, b, :, 0:ncols]
            if cols is not None:
                src = src[:, :, cols[0]:cols[1]]
                dst = wbuf[:, b, :, cols[0]:cols[1]]
            P.dma(lambda e: e.dma_start(out=dst, in_=src), reads=[("wsc", l, kind, u)], writes=[("wb", b)], chan="w%d" % b)
            return b

        def layer_in(l):
            return [xin, xs_a, xs_b, xs_a][l], ["in", "a", "b", "a"][l]

        def layer_out(l):
            if l == n_layers - 1:
                return out, "out"
            return [xs_a, xs_b, xs_a, None][l], ["a", "b", "a", None][l]

        def T_load(l, tile):
            s0, ns = tile
            src, skey = layer_in(l)
            for s in range(ns):
                S = s0 + s
                P.dma(lambda e, S=S, s=s: e.dma_start(out=xb16[:, s, :], in_=src[S * 128:(S + 1) * 128, :]),
                      reads=[("XD", skey, S)], writes=[("xb16", s)], chan="xin%d" % s, eng="pool")

        def T_phase(l, tile):
            s0, ns = tile
            for s in range(ns):
                b = nbank()
                pT = bank(b).bitcast(BF16)
                for j in range(8):
                    P.op("pe", lambda e, j=j, s=s, pT=pT: e.transpose(out=pT[:, j * 128:(j + 1) * 128], in_=xb16[:, s, j * 128:(j + 1) * 128], identity=ident),
                         reads=[("xb16", s), "cbf"], writes=[("ps", b)])
                P.op("dve", lambda e, s=s, pT=pT: e.tensor_copy(out=xT[:, :, s * 128:(s + 1) * 128], in_=pT.rearrange("p (j t) -> p j t", j=8)),
                     reads=[("ps", b)], writes=[("xT", s)])

        def proj_fm(b, wb_i, c0, N, ns):
            for kc in range(8):
                P.op("pe", lambda e, kc=kc: e.matmul(bank(b)[:, 0:N], lhsT=wbuf[:, wb_i, kc, c0:c0 + 128], rhs=xT[:, kc, 0:N], start=(kc == 0), stop=(kc == 7)),
                     reads=[("wb", wb_i)] + [("xT", s) for s in range(ns)], writes=[("ps", b)])

        def proj_tm(b, wb_i, s):
            for kc in range(8):
                P.op("pe", lambda e, kc=kc: e.matmul(bank(b)[:, 0:512], lhsT=xT[:, kc, s * 128:(s + 1) * 128], rhs=wbuf[:, wb_i, kc, 0:512], start=(kc == 0), stop=(kc == 7)),
                     reads=[("wb", wb_i), ("xT", s)], writes=[("ps", b)])

        def ln_chain(slot, srcs, keys):
            for h, a in enumerate(srcs):
                P.op("dve", lambda e, h=h, a=a: e.bn_stats(out=stt[:, slot, h * 6:(h + 1) * 6], in_=a),
                     reads=keys[h], writes=[("stt", slot, h)])
            P.op("dve", lambda e: e.bn_aggr(out=mvt[:, slot, 0:2], in_=stt[:, slot, :]),
                 reads=[("stt", slot, 0), ("stt", slot, 1)], writes=[("mv", slot)])
            P.op("dve", lambda e: e.tensor_scalar(out=mvt[:, slot, 1:2], in0=mvt[:, slot, 1:2], scalar1=EPS, scalar2=None, op0=ALU.add),
                 reads=[("mv", slot)], writes=[("mv", slot)])
            P.op("pool", lambda e: e.tensor_tensor(out=mvt[:, slot, 2:3], in0=mvt[:, slot, 1:2], in1=smalls[:, 0:1], op=ALU.pow),
                 reads=[("mv", slot), "smalls"], writes=[("rs", slot)])
            P.op("dve", lambda e: e.tensor_scalar(out=mvt[:, slot, 3:4], in0=mvt[:, slot, 0:1], scalar1=mvt[:, slot, 2:3], scalar2=-1.0, op0=ALU.mult, op1=ALU.mult),
                 reads=[("mv", slot), ("rs", slot)], writes=[("nm", slot)])

        def C_items(l, tile, ti):
            s0, ns = tile
            src, skey = layer_in(l)
            dst, dkey = layer_out(l)
            last = (l == n_layers - 1)
            cp = ti % 2

            def c_sub(s):
                S = s0 + s
                slot = state["stat"] % 4
                state["stat"] += 1
                rb = state["rb"] % 2
                state["rb"] += 1
                P.dma(lambda e: e.dma_start(out=x32[:], in_=src[S * 128:(S + 1) * 128, :]),
                      reads=[("XD", skey, S)], writes=["x32"], chan="x32")
                bs = [nbank(), nbank()]
                for h in range(2):
                    for kq in range(16):
                        P.op("pe", lambda e, h=h, kq=kq: e.matmul(bank(bs[h])[:, 0:512], lhsT=concat[:, cp, kq, s * 128:(s + 1) * 128], rhs=wout_bf[:, kq, h * 512:(h + 1) * 512], start=(kq == 0), stop=(kq == 15)),
                             reads=[("cc", cp, kq), "wout"], writes=[("ps", bs[h])])
                for h in range(2):
                    P.op("dve", lambda e, h=h: e.scalar_tensor_tensor(out=rbuf[:, rb, h * 512:(h + 1) * 512], in0=x32[:, h * 512:(h + 1) * 512], scalar=ALPHA, in1=bank(bs[h])[:, 0:512], op0=ALU.mult, op1=ALU.add),
                         reads=["x32", ("ps", bs[h])], writes=[("r", rb, h)])
                ln_chain(slot, [rbuf[:, rb, 0:512], rbuf[:, rb, 512:1024]], [[("r", rb, 0)], [("r", rb, 1)]])
                P.op("act", lambda e: e.activation(out=rbuf[:, rb, :], in_=rbuf[:, rb, :], func=AF.Identity, scale=mvt[:, slot, 2:3], bias=mvt[:, slot, 3:4]),
                     reads=[("r", rb, 0), ("r", rb, 1), ("rs", slot), ("nm", slot)], writes=[("r", rb, 0), ("r", rb, 1)])
                P.op("pool", lambda e: e.tensor_tensor(out=rbuf[:, rb, :], in0=rbuf[:, rb, :], in1=lng[:], op=ALU.mult),
                     reads=[("r", rb, 0), ("r", rb, 1), "lng"], writes=[("r", rb, 0), ("r", rb, 1)])
                P.op("pool", lambda e: e.tensor_tensor(out=rbuf[:, rb, :], in0=rbuf[:, rb, :], in1=lnb[:], op=ALU.add),
                     reads=[("r", rb, 0), ("r", rb, 1), "lnb"], writes=[("r", rb, 0), ("r", rb, 1)])
                row = (S - HALO) if last else S
                P.dma(lambda e: e.dma_start(out=dst[row * 128:(row + 1) * 128, :], in_=rbuf[:, rb, :]),
                      reads=[("r", rb, 0), ("r", rb, 1)], writes=[("XD", dkey, S)], chan="st%d" % rb, eng="pool")

            items = []
            for s in range(ns):
                if last and (s0 + s) < HALO:
                    continue
                items.append(lambda s=s: c_sub(s))
            return items

        def run_staged(stages, n, hooks, early=()):
            ns_ = len(stages)
            hooks = list(hooks)
            nh = len(hooks)
            pts = {}
            if nh:
                for h in range(nh):
                    pts.setdefault(min(n - 1, 1 + (2 * h if nh <= 4 else h)), []).append(hooks[h])
            for it in range(n + ns_ - 1):
                for j, stg in enumerate(stages):
                    u = it - j
                    if 0 <= u < n:
                        stg(u)
                if it == 0:
                    for h in early:
                        h()
                for h in pts.get(it, []):
                    h()

        def even_layer(l):
            i = l // 2
            xa = ub(ev["xa"], 5 * 1024).rearrange("p (s c) -> p s c", s=5)
            pT_ = ub(ev["pT"], 4 * 512).rearrange("p (g k n) -> p g k n", g=2, k=2)
            sga = ub(ev["sga"], 4 * 512).rearrange("p (g k n) -> p g k n", g=2, k=2)
            hsb = ub(ev["hsb"], 2 * 512).rearrange("p (a n) -> p a n", a=2)
            sgb = ub(ev["sgb"], 2 * 512).rearrange("p (a n) -> p a n", a=2)
            mm = ub(ev["m"], 2 * 512).rearrange("p (a n) -> p a n", a=2)
            pst = ub(ev["pst"], 8 * 514).rearrange("p (q n) -> p q n", q=8)
            dg3 = dg[:].rearrange("p a b c -> p (a b c)")[:, 0:8 * 3 * 128].rearrange("p (q j c) -> p q j c", q=8, j=3)
            P.dma(lambda e: e.dma_start(out=pw_bf[:].rearrange("p g k n -> p (g k n)"), in_=pw_d[i]), writes=["pw"], chan="pwc", eng="pool")
            for q in range(8):
                for j in range(3):
                    P.op("pool", lambda e, q=q, j=j: e.tensor_scalar(out=dg3[:, q, j, :], in0=ident, scalar1=col("cw%d" % i, j * 8 + q), scalar2=1.0, op0=ALU.mult, op1=ALU.mult),
                         reads=["cbf", "colp"], writes=["dg3"])
            P.op("pool", lambda e: e.memset(xa[:, 0, :], 0.0), writes=[("xa", 0, 0), ("xa", 0, 1)])
            P.op("pool", lambda e: e.memset(pst[:, :, 0:2], 0.0), writes=[("pst", q) for q in range(8)])
            state["ring_n"] = 8
            tstate = {}

            def load_next_c():
                if tstate["q"] < 8:
                    w = load_wunit(l, "c", tstate["q"])
                    tstate["q"] += 1
                    return w
                return None

            def pre(tile):
                s0, ns = tile
                wa = [load_wunit(l, "a", 0), load_wunit(l, "a", 1)]
                tstate["q"] = 0
                tstate["wq"] = {}
                for gg in range(2):
                    for s in range(ns):
                        b = nbank()
                        proj_tm(b, wa[gg], s)
                        P.op("act", lambda e, b=b, s=s, gg=gg: e.activation(out=xa[:, s + 1, gg * 512:(gg + 1) * 512], in_=bank(b)[:, 0:512], func=AF.Copy),
                             reads=[("ps", b)], writes=[("xa", s + 1, gg)])
                    tstate["wq"][gg] = load_next_c()

            def main(tile, ti, hooks, early=()):
                s0, ns = tile
                N = ns * 128
                cp = ti % 2
                wq = tstate["wq"]

                def stage1(q):
                    wb_i = wq[q]
                    g = q // 2
                    gp, qp = g % 2, q % 2
                    gg = q // 4
                    b_ga = nbank()
                    proj_fm(b_ga, wb_i, 0, N, ns)
                    P.op("act", lambda e: e.activation(out=sga[:, gp, qp, 0:N], in_=bank(b_ga)[:, 0:N], func=AF.Silu),
                         reads=[("ps", b_ga)], writes=[("sga", gp, qp)])
                    b_h = nbank()
                    proj_fm(b_h, wb_i, 128, N, ns)
                    P.op("act", lambda e: e.activation(out=hsb[:, qp, 0:N], in_=bank(b_h)[:, 0:N], func=AF.Copy),
                         reads=[("ps", b_h)], writes=[("hsb", qp)])
                    b_cg = nbank()
                    proj_fm(b_cg, wb_i, 384, N, ns)
                    P.op("dve", lambda e: e.tensor_tensor(out=pst[:, q, 2:2 + N], in0=bank(b_cg)[:, 0:N], in1=hsb[:, qp, 0:N], op=ALU.mult),
                         reads=[("ps", b_cg), ("hsb", qp)], writes=[("pst", q)])
                    b_gb = nbank()
                    proj_fm(b_gb, wb_i, 512, N, ns)
                    P.op("act", lambda e: e.activation(out=sgb[:, qp, 0:N], in_=bank(b_gb)[:, 0:N], func=AF.Silu),
                         reads=[("ps", b_gb)], writes=[("sgb", qp)])
                    b_bg = nbank()
                    proj_fm(b_bg, wb_i, 256, N, ns)
                    P.op("dve", lambda e: e.tensor_tensor(out=mm[:, qp, 0:N], in0=bank(b_bg)[:, 0:N], in1=sgb[:, qp, 0:N], op=ALU.mult),
                         reads=[("ps", b_bg), ("sgb", qp)], writes=[("m", qp)])
                    if q + 2 < 8:
                        wq[q + 2] = load_next_c()
                    b_pl = nbank()
                    for s in range(ns):
                        S = s0 + s
                        if S == HALO:
                            mats = [(s + 1, PM_SPH + g), (s + 1, PM_SPL + g), (s, PM_SPP + g)]
                        else:
                            mats = [(s + 1, PM_CUR + g), (s, PM_PREV + g)]
                        for idx, (slot, pm) in enumerate(mats):
                            P.op("pe", lambda e, s=s, slot=slot, pm=pm, idx=idx, nm=len(mats): e.matmul(bank(b_pl)[:, s * 128:(s + 1) * 128], lhsT=xa[:, slot, q * 128:(q + 1) * 128], rhs=cbf[:, pm, :], start=(idx == 0), stop=(idx == nm - 1)),
                                 reads=[("xa", slot, gg), "cbf"], writes=[("ps", b_pl)])
                    P.op("act", lambda e: e.activation(out=pT_[:, gp, qp, 0:N], in_=bank(b_pl)[:, 0:N], func=AF.Copy),
                         reads=[("ps", b_pl)], writes=[("pT", gp, qp)])

                def stage2(q):
                    g = q // 2
                    gp, qp = g % 2, q % 2
                    b_cv = nbank()
                    for j in range(3):
                        P.op("pe", lambda e, j=j: e.matmul(bank(b_cv)[:, 0:N], lhsT=dg3[:, q, j, :], rhs=pst[:, q, j:j + N], start=(j == 0), stop=(j == 2)),
                             reads=["dg3", ("pst", q)], writes=[("ps", b_cv)])
                    P.op("dve", lambda e: e.scalar_tensor_tensor(out=concat[:, cp, 8 + q, 0:N], in0=bank(b_cv)[:, 0:N], scalar=col("cb%d" % i, q), in1=mm[:, qp, 0:N], op0=ALU.add, op1=ALU.mult),
                         reads=[("ps", b_cv), ("m", qp), "colp"], writes=[("cc", cp, 8 + q)])
                    fl = col("flag") if ti == 0 else col("one")
                    P.op("pool", lambda e: e.tensor_scalar(out=pst[:, q, 0:2], in0=pst[:, q, N:N + 2], scalar1=fl, scalar2=1.0, op0=ALU.mult, op1=ALU.mult),
                         reads=[("pst", q), "colp"], writes=[("pst", q)])
                    if q % 2 == 1:
                        for qo in (q - 1, q):
                            b_pw = nbank()
                            for k2 in range(2):
                                P.op("pe", lambda e, k2=k2, qo=qo, b_pw=b_pw: e.matmul(bank(b_pw)[:, 0:N], lhsT=pw_bf[:, g, k2, (qo % 2) * 128:(qo % 2 + 1) * 128], rhs=pT_[:, gp, k2, 0:N], start=(k2 == 0), stop=(k2 == 1)),
                                     reads=["pw", ("pT", gp, 0), ("pT", gp, 1)], writes=[("ps", b_pw)])
                            P.op("dve", lambda e, qo=qo, b_pw=b_pw: e.scalar_tensor_tensor(out=concat[:, cp, qo, 0:N], in0=bank(b_pw)[:, 0:N], scalar=col("ps%d" % i, qo), in1=sga[:, gp, qo % 2, 0:N], op0=ALU.mult, op1=ALU.mult),
                                 reads=[("ps", b_pw), ("sga", gp, qo % 2), "colp"], writes=[("cc", cp, qo)])

                run_staged([stage1, stage2], 8, hooks, early)
                P.op("pool", lambda e: e.tensor_copy(out=xa[:, 0, :], in_=xa[:, ns, :]),
                     reads=[("xa", ns, 0), ("xa", ns, 1)], writes=[("xa", 0, 0), ("xa", 0, 1)])

            def transition(tile, nxt):
                if nxt is not None:
                    pre(nxt)

            return pre, main, transition

        def odd_layer(l):
            i = l // 2
            vn = ub(od["vn"], 4 * 1024).rearrange("p (s c) -> p s c", s=4)
            t1 = uf(od["t1"], 2 * 512).rearrange("p (a n) -> p a n", a=2)
            sgc = ub(od["sgc"], 2 * 512).rearrange("p (a n) -> p a n", a=2)
            t2 = ub(od["t2"], 2 * 512).rearrange("p (a n) -> p a n", a=2)
            th = ub(od["th"], 2 * 512).rearrange("p (a n) -> p a n", a=2)
            zst = ub(od["zst"], 8 * 542).rearrange("p (q n) -> p q n", q=8)
            sgd = ub(od["sgd"], 8 * 512).rearrange("p (q n) -> p q n", q=8)
            czb = ub(od["czb"], 8 * 512).rearrange("p (q n) -> p q n", q=8)
            sqb = ub(od["sqb"], 2 * 512).rearrange("p (a n) -> p a n", a=2)
            mean = uf(od["mean"], 512)
            rstd = uf(od["rstd"], 512)
            dd = uf(od["dd"], 2 * 512).rearrange("p (a n) -> p a n", a=2)
            s1 = ub(od["s1"], 2 * 512).rearrange("p (a n) -> p a n", a=2)
            MEANB, EX2B = 6, 7
            state["ring_n"] = 6
            wsf = rbuf[:, 0, 0:512]
            bsg = rbuf[:, 1, :]
            sgr = x32[0:1, 0:512]
            P.dma(lambda e: e.dma_start(out=wsf, in_=wsT_d[i]), writes=[("r", 0, 0)], chan="k0")
            P.dma(lambda e: e.dma_start(out=bsg, in_=sgb_d[i:i + 1, :].partition_broadcast(128)), writes=[("r", 1, 0), ("r", 1, 1)], chan="k1")
            P.dma(lambda e: e.dma_start(out=sgr, in_=sgub_d[i:i + 1, :]), writes=["x32"], chan="k2")
            for hd in range(4):
                P.op("dve", lambda e, hd=hd: e.tensor_tensor(out=wsf[:, hd * 128:(hd + 1) * 128], in0=wsf[:, hd * 128:(hd + 1) * 128], in1=maskT[:], op=ALU.mult),
                     reads=[("r", 0, 0), "maskT"], writes=[("r", 0, 0)])
            P.op("dve", lambda e: e.tensor_copy(out=wsT_bf[:].rearrange("p h n -> p (h n)"), in_=wsf), reads=[("r", 0, 0)], writes=["wsT"])
            for q in range(8):
                hd = q // 2
                b = nbank()
                P.op("pe", lambda e, q=q, hd=hd, b=b: e.matmul(bank(b)[:, 0:128], lhsT=bsg[:, q * 128:(q + 1) * 128], rhs=wsf[:, hd * 128:(hd + 1) * 128], start=True, stop=False),
                     reads=[("r", 1, 0), ("r", 1, 1), ("r", 0, 0)], writes=[("ps", b)])
                P.op("pe", lambda e, hd=hd, b=b: e.matmul(bank(b)[:, 0:128], lhsT=ones_row[0:1, :], rhs=sgr[0:1, hd * 128:(hd + 1) * 128], start=False, stop=True),
                     reads=["ones_row", "x32"], writes=[("ps", b)])
                P.op("act", lambda e, q=q, b=b: e.activation(out=Bias[:, q, :], in_=bank(b)[:, 0:128], func=AF.Copy),
                     reads=[("ps", b)], writes=["Bias"])
            P.op("pool", lambda e: e.memset(zst[:, :, 0:30], 0.0), writes=[("zst", q) for q in range(8)])
            dwc = COL["dw%d" % i]
            P.op("dve", lambda e: e.tensor_scalar(out=wt[:], in0=colp[:, dwc:dwc + 248], scalar1=0.5, scalar2=None, op0=ALU.mult),
                 reads=["colp"], writes=["wt"])

            def build_diag(q, hs):
                j0 = KDVE if hs == 0 else 16
                nj = (16 - KDVE) if hs == 0 else 15
                P.op(DIAG_ENG, lambda e: e.tensor_tensor(out=dg[:, hs, 0:nj, :], in0=ident.unsqueeze(1).to_broadcast([128, nj, 128]),
                                                      in1=wt[:, q * 31 + j0:q * 31 + j0 + nj].unsqueeze(2).to_broadcast([128, nj, 128]), op=ALU.mult),
                     reads=["cbf", "wt"], writes=[("dg", hs)])

            tstate = {}

            def load_next_c():
                if tstate["q"] < 8:
                    w = load_wunit(l, "c", tstate["q"])
                    tstate["q"] += 1
                    return w
                return None

            def pre_items(tile):
                s0, ns = tile

                def begin():
                    tstate["wa"] = [load_wunit(l, "a", 0), load_wunit(l, "a", 1)]
                    tstate["q"] = 0

                def c1_sub(s):
                    wa = tstate["wa"]
                    slot = state["stat"] % 4
                    state["stat"] += 1
                    bs = [nbank(), nbank()]
                    for gg in range(2):
                        proj_tm(bs[gg], wa[gg], s)
                    ln_chain(slot, [bank(bs[0])[:, 0:512], bank(bs[1])[:, 0:512]], [[("ps", bs[0])], [("ps", bs[1])]])
                    for gg in range(2):
                        P.op("act", lambda e, gg=gg: e.activation(out=vn[:, s, gg * 512:(gg + 1) * 512], in_=bank(bs[gg])[:, 0:512], func=AF.Identity, scale=mvt[:, slot, 2:3], bias=mvt[:, slot, 3:4]),
                             reads=[("ps", bs[gg]), ("rs", slot), ("nm", slot)], writes=[("vn", s, gg)])

                return [begin] + [(lambda s=s: c1_sub(s)) for s in range(ns)]

            def pre(tile):
                for f in pre_items(tile):
                    f()

            def main(tile, ti, hooks, early=()):
                s0, ns = tile
                N = ns * 128
                cp = ti % 2
                wq = {0: load_next_c()}
                wq[1] = load_next_c()

                def stage1(q):
                    wb_i = wq[q]
                    hd = q // 2
                    qp = q % 2
                    gg = q // 4
                    b_gc = nbank()
                    proj_fm(b_gc, wb_i, 0, N, ns)
                    P.op("act", lambda e: e.activation(out=sgc[:, qp, 0:N], in_=bank(b_gc)[:, 0:N], func=AF.Silu),
                         reads=[("ps", b_gc)], writes=[("sgc", qp)])
                    b_bl = nbank()
                    proj_fm(b_bl, wb_i, 384, N, ns)
                    P.op("act", lambda e: e.activation(out=th[:, qp, 0:N], in_=bank(b_bl)[:, 0:N], func=AF.Tanh, scale=0.5),
                         reads=[("ps", b_bl)], writes=[("th", qp)])
                    b_a = nbank()
                    proj_fm(b_a, wb_i, 256, N, ns)
                    P.op("dve", lambda e: e.scalar_tensor_tensor(out=zst[:, q, 30:30 + N], in0=th[:, qp, 0:N], scalar=1.0, in1=bank(b_a)[:, 0:N], op0=ALU.add, op1=ALU.mult),
                         reads=[("th", qp), ("ps", b_a)], writes=[("zst", q)])
                    b_sg = nbank()
                    for s in range(ns):
                        P.op("pe", lambda e, s=s: e.matmul(bank(b_sg)[:, s * 128:(s + 1) * 128], lhsT=vn[:, s, q * 128:(q + 1) * 128], rhs=wsT_bf[:, hd, :], start=True, stop=True),
                             reads=[("vn", s, gg), "wsT"], writes=[("ps", b_sg)])
                    P.op("dve", lambda e: e.scalar_tensor_tensor(out=t1[:, qp, 0:N].rearrange("p (s n) -> p s n", s=ns), in0=bank(b_sg)[:, 0:N].rearrange("p (s n) -> p s n", s=ns), scalar=col("sg%d" % i, q), in1=Bias[:, q:q + 1, :].to_broadcast([128, ns, 128]), op0=ALU.mult, op1=ALU.add),
                         reads=[("ps", b_sg), "Bias", "colp"], writes=[("t1", qp)])
                    b_u = nbank()
                    proj_fm(b_u, wb_i, 128, N, ns)
                    P.op("dve", lambda e: e.tensor_tensor(out=t2[:, qp, 0:N], in0=bank(b_u)[:, 0:N], in1=t1[:, qp, 0:N], op=ALU.mult),
                         reads=[("ps", b_u), ("t1", qp)], writes=[("t2", qp)])
                    P.op("dve", lambda e: e.tensor_tensor(out=concat[:, cp, q, 0:N], in0=t2[:, qp, 0:N], in1=sgc[:, qp, 0:N], op=ALU.mult),
                         reads=[("t2", qp), ("sgc", qp)], writes=[("cc", cp, q)])
                    b_gd = nbank()
                    proj_fm(b_gd, wb_i, 512, N, ns)
                    P.op("act", lambda e: e.activation(out=sgd[:, q, 0:N], in_=bank(b_gd)[:, 0:N], func=AF.Silu),
                         reads=[("ps", b_gd)], writes=[("sgd", q)])
                    if q + 2 < 8:
                        wq[q + 2] = load_next_c()
                    if q == 0:
                        build_diag(0, 0)
                        build_diag(0, 1)

                def stage2(q):
                    qp = q % 2
                    for j in range(KDVE):
                        wc = wt[:, q * 31 + j:q * 31 + j + 1]
                        if j == 0:
                            P.op("dve", lambda e, j=j, wc=wc: e.tensor_scalar(out=acc[:, qp, 0:N], in0=zst[:, q, j:j + N], scalar1=wc, scalar2=None, op0=ALU.mult),
                                 reads=[("zst", q), "wt"], writes=[("acc", qp)])
                        else:
                            P.op("dve", lambda e, j=j, wc=wc: e.scalar_tensor_tensor(out=acc[:, qp, 0:N], in0=zst[:, q, j:j + N], scalar=wc, in1=acc[:, qp, 0:N], op0=ALU.mult, op1=ALU.add),
                                 reads=[("zst", q), "wt", ("acc", qp)], writes=[("acc", qp)])
                    b_cv = nbank()
                    for hs in range(2):
                        j0 = KDVE if hs == 0 else 16
                        nj = (16 - KDVE) if hs == 0 else 15
                        for jj in range(nj):
                            j = j0 + jj
                            P.op("pe", lambda e, jj=jj, j=j, hs=hs: e.matmul(bank(b_cv)[:, 0:N], lhsT=dg[:, hs, jj, :], rhs=zst[:, q, j:j + N], start=(j == KDVE), stop=(j == 30)),
                                 reads=[("dg", hs), ("zst", q)], writes=[("ps", b_cv)])
                        if q + 1 < 8:
                            build_diag(q + 1, hs)
                    if KDVE > 0:
                        P.op("dve", lambda e: e.scalar_tensor_tensor(out=czb[:, q, 0:N], in0=bank(b_cv)[:, 0:N], scalar=col("db%d" % i, q), in1=acc[:, qp, 0:N], op0=ALU.add, op1=ALU.add),
                             reads=[("ps", b_cv), "colp", ("acc", qp)], writes=[("czb", q)])
                        P.op("act", lambda e: e.activation(out=sqb[:, qp, 0:N], in_=czb[:, q, 0:N], func=AF.Square),
                             reads=[("czb", q)], writes=[("sqb", qp)])
                    else:
                        P.op("act", lambda e: e.activation(out=czb[:, q, 0:N], in_=bank(b_cv)[:, 0:N], func=AF.Identity, bias=col("db%d" % i, q)),
                             reads=[("ps", b_cv), "colp"], writes=[("czb", q)])
                        P.op("act", lambda e: e.activation(out=sqb[:, qp, 0:N], in_=bank(b_cv)[:, 0:N], func=AF.Square, bias=col("db%d" % i, q)),
                             reads=[("ps", b_cv), "colp"], writes=[("sqb", qp)])
                    fl = col("flag") if ti == 0 else col("one")
                    P.op("pool", lambda e: e.tensor_scalar(out=zst[:, q, 0:30], in0=zst[:, q, N:N + 30], scalar1=fl, scalar2=1.0, op0=ALU.mult, op1=ALU.mult),
                         reads=[("zst", q), "colp"], writes=[("zst", q)])

                def stage3(q):
                    qp = q % 2
                    P.op("pe", lambda e: e.matmul(bank(MEANB)[:, 0:N], lhsT=onesD[:], rhs=czb[:, q, 0:N], start=(q == 0), stop=(q == 7)),
                         reads=["onesD", ("czb", q)], writes=[("ps", MEANB)])
                    P.op("pe", lambda e: e.matmul(bank(EX2B)[:, 0:N], lhsT=onesD[:], rhs=sqb[:, qp, 0:N], start=(q == 0), stop=(q == 7)),
                         reads=["onesD", ("sqb", qp)], writes=[("ps", EX2B)])

                run_staged([stage1, stage2, stage3], 8, hooks, early)

            def main_lite(tile, ti, early=()):
                s0, ns = tile
                N = ns * 128
                sl = ns - 1
                tstate["q"] = 0

                def load_part():
                    if tstate["q"] < 8:
                        w = load_wunit(l, "c", tstate["q"], cols=(256, 512))
                        tstate["q"] += 1
                        return w
                    return None

                wq = {0: load_part()}
                wq[1] = load_part()
                for h in early:
                    h()
                for q in range(8):
                    def unit(q=q):
                        wb_i = wq[q]
                        qp = q % 2
                        b_bl = nbank()
                        for kc in range(8):
                            P.op("pe", lambda e, kc=kc: e.matmul(bank(b_bl)[:, 0:128], lhsT=wbuf[:, wb_i, kc, 384:512], rhs=xT[:, kc, sl * 128:(sl + 1) * 128], start=(kc == 0), stop=(kc == 7)),
                                 reads=[("wb", wb_i), ("xT", sl)], writes=[("ps", b_bl)])
                        P.op("act", lambda e: e.activation(out=th[:, qp, 0:128], in_=bank(b_bl)[:, 0:128], func=AF.Tanh, scale=0.5),
                             reads=[("ps", b_bl)], writes=[("th", qp)])
                        b_a = nbank()
                        for kc in range(8):
                            P.op("pe", lambda e, kc=kc: e.matmul(bank(b_a)[:, 0:128], lhsT=wbuf[:, wb_i, kc, 256:384], rhs=xT[:, kc, sl * 128:(sl + 1) * 128], start=(kc == 0), stop=(kc == 7)),
                                 reads=[("wb", wb_i), ("xT", sl)], writes=[("ps", b_a)])
                        P.op("dve", lambda e: e.scalar_tensor_tensor(out=zst[:, q, 30 + sl * 128:30 + N], in0=th[:, qp, 0:128], scalar=1.0, in1=bank(b_a)[:, 0:128], op0=ALU.add, op1=ALU.mult),
                             reads=[("th", qp), ("ps", b_a)], writes=[("zst", q)])
                        fl = col("flag") if ti == 0 else col("one")
                        P.op("pool", lambda e: e.tensor_scalar(out=zst[:, q, 0:30], in0=zst[:, q, N:N + 30], scalar1=fl, scalar2=1.0, op0=ALU.mult, op1=ALU.mult),
                             reads=[("zst", q), "colp"], writes=[("zst", q)])
                        if q + 2 < 8:
                            wq[q + 2] = load_part()
                    unit()

            def s3d_items(tile, ti):
                s0, ns = tile
                N = ns * 128
                cp = ti % 2

                def fin():
                    P.op("act", lambda e: e.activation(out=mean[:, 0:N], in_=bank(MEANB)[:, 0:N], func=AF.Copy),
                         reads=[("ps", MEANB)], writes=["mean"])
                    m2 = dd[:, 0, 0:N]
                    P.op("dve", lambda e: e.tensor_tensor(out=m2, in0=mean[:, 0:N], in1=mean[:, 0:N], op=ALU.mult),
                         reads=["mean"], writes=[("dd", 0)])
                    P.op("dve", lambda e: e.scalar_tensor_tensor(out=rstd[:, 0:N], in0=bank(EX2B)[:, 0:N], scalar=EPS, in1=m2, op0=ALU.add, op1=ALU.subtract),
                         reads=[("ps", EX2B), ("dd", 0)], writes=["rstd"])
                    P.op("act", lambda e: e.activation(out=rstd[:, 0:N], in_=rstd[:, 0:N], func=AF.Sqrt),
                         reads=["rstd"], writes=["rstd"])
                    P.op("dve", lambda e: e.reciprocal(out=rstd[:, 0:N], in_=rstd[:, 0:N]),
                         reads=["rstd"], writes=["rstd"])

                def s3d_a(q):
                    qp = q % 2
                    P.op("dve", lambda e: e.tensor_tensor(out=dd[:, qp, 0:N], in0=czb[:, q, 0:N], in1=mean[:, 0:N], op=ALU.subtract),
                         reads=[("czb", q), "mean"], writes=[("dd", qp)])
                    P.op("dve", lambda e: e.tensor_tensor(out=dd[:, qp, 0:N], in0=dd[:, qp, 0:N], in1=rstd[:, 0:N], op=ALU.mult),
                         reads=[("dd", qp), "rstd"], writes=[("dd", qp)])
                    P.op("act", lambda e: e.activation(out=s1[:, qp, 0:N], in_=dd[:, qp, 0:N], func=AF.Silu, scale=col("ng%d" % i, q), bias=col("nb%d" % i, q)),
                         reads=[("dd", qp), "colp"], writes=[("s1", qp)])

                def s3d_b(q):
                    qp = q % 2
                    P.op("dve", lambda e: e.tensor_tensor(out=concat[:, cp, 8 + q, 0:N], in0=s1[:, qp, 0:N], in1=sgd[:, q, 0:N], op=ALU.mult),
                         reads=[("s1", qp), ("sgd", q)], writes=[("cc", cp, 8 + q)])

                def s3d_step(k):
                    if k < 8:
                        s3d_a(k)
                    if k >= 1:
                        s3d_b(k - 1)

                return [fin] + [(lambda k=k: s3d_step(k)) for k in range(9)]

            def transition(tile, nxt, ti, lite=False):
                bq = pre_items(nxt) if nxt is not None else []
                if lite:
                    while bq:
                        bq.pop(0)()
                    return
                a = s3d_items(tile, ti)
                a.pop(0)()
                if bq:
                    bq.pop(0)()
                if bq:
                    bq.pop(0)()
                for idx in range(9):
                    a.pop(0)()
                    if idx % 2 == 1 and bq:
                        bq.pop(0)()
                while bq:
                    bq.pop(0)()

            main.lite = main_lite
            return pre, main, transition

        first_casts = cast_list(0)
        T_load(0, TILES[0])
        for _ in range(6):
            cast_dma(*first_casts.pop(0))
        for l in range(n_layers):
            if l > 0:
                P.op("pool", lambda e: e.memset(smalls[:, 4:5], 0.0), reads=[], writes=["LAYER"])
            lstart = len(P.ops)
            is_even = (l % 2 == 0)
            pre, main, transition = even_layer(l) if is_even else odd_layer(l)
            pend = cast_list(l + 1) if l + 1 < n_layers else []
            if l == 0:
                while first_casts:
                    cast_dma(*first_casts.pop(0))
            else:
                T_load(l, TILES[0])
            P.dma(lambda e, l=l: e.dma_start(out=lng[:], in_=lng_d[l:l + 1, :].partition_broadcast(128)), writes=["lng"], chan="k0")
            P.dma(lambda e, l=l: e.dma_start(out=lnb[:], in_=lnb_d[l:l + 1, :].partition_broadcast(128)), writes=["lnb"], chan="k1")
            for h in range(4):
                P.dma(lambda e, l=l, h=h: e.dma_start(out=wout_bf[:, h * 4:(h + 1) * 4, :].rearrange("p k n -> p (k n)"), in_=wout_s[l][:, h * 4096:(h + 1) * 4096]),
                      reads=[("wosc", l, h)], writes=["wout"], chan="wo%d" % h)
            T_phase(l, TILES[0])
            lite0 = LITE_HALO and (not is_even) and (l == n_layers - 1)
            if not lite0:
                pre(TILES[0])
            pendC = []
            for ti, tile in enumerate(TILES):
                for _ in range(2):
                    if pend:
                        cast_dma(*pend.pop(0))
                nxt = TILES[ti + 1] if ti + 1 < len(TILES) else None
                early = [(lambda nxt=nxt: T_load(l, nxt))] if nxt is not None else []
                lite = lite0 and ti == 0
                if lite:
                    main.lite(tile, ti, early)
                else:
                    main(tile, ti, pendC, early)
                if nxt is not None:
                    T_phase(l, nxt)
                if is_even:
                    transition(tile, nxt)
                else:
                    transition(tile, nxt, ti, lite)
                pendC = C_items(l, tile, ti)
                if (not is_even and not DEFER_C_ODD) or (is_even and not DEFER_C_EVEN):
                    for c in pendC:
                        c()
                    pendC = []
            for c in pendC:
                c()
            while pend:
                cast_dma(*pend.pop(0))
            for o in P.ops[lstart:]:
                o.reads = o.reads + ("LAYER",)
        print("ops:", len(P.ops), {e: sum(1 for o in P.ops if o.eng == e) for e in ENGS})
        P.finalize(st)
        print("sem max counts:", {k: v for k, v in P.max_counts.items() if k[0] == "e"})
    return nc


def _pool_mats():
    t = np.arange(128)
    mats = np.zeros((NCB, 128, 128), np.float32)
    mats[0] = np.eye(128)
    s_ = t[:, None]
    t_ = t[None, :]
    for g, w in enumerate(POOL_W):
        band = ((t_ - s_) >= 0) & ((t_ - s_) <= w - 1)
        mats[PM_CUR + g] = band / w - np.eye(128)
        mats[PM_PREV + g] = ((t_ + 128 - s_) <= w - 1) / w
    return mats


def _pool_special(first_half):
    g_ = _pool_mats()
    hi = np.zeros((4, 128, 128), np.float32)
    lo = np.zeros((4, 128, 128), np.float32)
    pv = np.zeros((4, 128, 128), np.float32)
    t = np.arange(128)
    s_ = t[:, None]
    t_ = t[None, :]
    for g, w in enumerate(POOL_W):
        if first_half:
            band = ((t_ - s_) >= 0) & ((t_ - s_) <= w - 1)
            inv = (1.0 / np.minimum(t + 1, w)).astype(np.float32)[None, :]
            full = band * inv
            h = full.astype(ml_dtypes.bfloat16).astype(np.float32)
            hi[g] = h - np.eye(128)
            lo[g] = full - h
        else:
            hi[g] = g_[PM_CUR + g]
            pv[g] = g_[PM_PREV + g]
    return hi, lo, pv


_CACHE = {}


def _get_program(n_layers=4):
    if n_layers not in _CACHE:
        _CACHE[n_layers] = build_program(n_layers)
    return _CACHE[n_layers]


def _unit_layout(w, col_groups):
    outs = []
    for cols in col_groups:
        sub = w[:, cols]
        n = sub.shape[1]
        outs.append(np.ascontiguousarray(sub.reshape(8, 128, n).transpose(1, 0, 2)).reshape(128, 8 * n))
    return np.stack(outs)


def prepare_inputs(x, ln_g, ln_b, w_in_even, w_out_even, pool_w, pool_scale, sconv_w, sconv_b,
                   w_in_odd, w_out_odd, sgu_ln_g, sgu_ln_b, sgu_w, sgu_b,
                   dconv_w, dconv_b, dnorm_g, dnorm_b):
    f = np.float32
    x = np.asarray(x, f)
    ar = np.arange
    xa_groups = [ar(0, 512), ar(512, 1024)]
    ev_groups = [np.concatenate([1024 + q * 128 + ar(128), 2048 + q * 128 + ar(128), 3072 + q * 128 + ar(128),
                                 4096 + q * 128 + ar(128), 5120 + q * 128 + ar(128)]) for q in range(8)]
    v_groups = [1024 + ar(0, 512), 1024 + ar(512, 1024)]
    od_groups = [np.concatenate([2048 + q * 128 + ar(128), 0 + q * 128 + ar(128), 3072 + q * 128 + ar(128),
                                 4096 + q * 128 + ar(128), 5120 + q * 128 + ar(128)]) for q in range(8)]
    w_in_even = np.asarray(w_in_even, f)
    w_in_odd = np.asarray(w_in_odd, f)
    w_xa = np.stack([_unit_layout(w_in_even[i], xa_groups) for i in range(2)])
    w_ev = np.stack([_unit_layout(w_in_even[i], ev_groups) for i in range(2)])
    w_v = np.stack([_unit_layout(w_in_odd[i], v_groups) for i in range(2)])
    w_od = np.stack([_unit_layout(w_in_odd[i], od_groups) for i in range(2)])
    wouts = [np.asarray(w_out_even, f)[0], np.asarray(w_out_odd, f)[0], np.asarray(w_out_even, f)[1], np.asarray(w_out_odd, f)[1]]
    w_out = np.stack([np.ascontiguousarray(w.reshape(16, 128, 1024).transpose(1, 0, 2)).reshape(128, 16 * 1024) for w in wouts])
    pw = np.asarray(pool_w, f)
    pw_l = np.ascontiguousarray(pw.reshape(2, 4, 2, 128, 256).transpose(0, 3, 1, 2, 4)).reshape(2, 128, 2048)
    sw = np.asarray(sgu_w, f)
    wsT = np.ascontiguousarray(sw.transpose(0, 3, 1, 2)).reshape(2, 128, 512)
    idx = np.arange(128)
    maskT = ((idx[:, None] // 64) <= (idx[None, :] // 64)).astype(f)
    lng4 = np.asarray(ln_g, f)
    lnb4 = np.asarray(ln_b, f)
    sgb2 = np.asarray(sgu_ln_b, f)
    sgub = np.asarray(sgu_b, f).reshape(2, 512)

    def cols8(v):
        return np.asarray(v, f).reshape(8, 128).T

    base = np.zeros((128, NCOL), f)
    base[:, COL["one"]] = 1.0
    for i in range(2):
        base[:, COL["ps%d" % i]:COL["ps%d" % i] + 8] = cols8(pool_scale[i])
        for j in range(3):
            base[:, COL["cw%d" % i] + j * 8:COL["cw%d" % i] + j * 8 + 8] = cols8(np.asarray(sconv_w)[i, j])
        base[:, COL["cb%d" % i]:COL["cb%d" % i] + 8] = cols8(sconv_b[i])
        base[:, COL["sg%d" % i]:COL["sg%d" % i] + 8] = cols8(sgu_ln_g[i])
        for j in range(31):
            base[:, COL["dw%d" % i] + j:COL["dw%d" % i] + 248:31] = cols8(np.asarray(dconv_w)[i, j])
        base[:, COL["db%d" % i]:COL["db%d" % i] + 8] = cols8(dconv_b[i])
        base[:, COL["ng%d" % i]:COL["ng%d" % i] + 8] = cols8(dnorm_g[i])
        base[:, COL["nb%d" % i]:COL["nb%d" % i] + 8] = cols8(dnorm_b[i])
    gen = _pool_mats()
    maps = []
    for c in range(8):
        b, hf = c // 2, c % 2
        if hf == 0:
            xw = np.concatenate([np.zeros((HALO * 128, D), f), x[b, 0:4096]], axis=0)
        else:
            xw = x[b, 4096 - HALO * 128:8192]
        cm = gen.copy()
        hi, lo, pv = _pool_special(hf == 0)
        cm[PM_SPH:PM_SPH + 4] = hi
        cm[PM_SPL:PM_SPL + 4] = lo
        cm[PM_SPP:PM_SPP + 4] = pv
        cbf = np.ascontiguousarray(cm.transpose(1, 0, 2)).reshape(128, NCB * 128).astype(ml_dtypes.bfloat16)
        cp = base.copy()
        cp[:, COL["flag"]] = float(hf)
        maps.append({
            "xin": np.ascontiguousarray(xw), "consts_bf": cbf, "maskT": maskT, "colpack": cp,
            "ln_g": lng4, "ln_b": lnb4, "sgu_ln_b": sgb2, "sgu_b": sgub,
            "w_xa": w_xa, "w_ev": w_ev, "w_v": w_v, "w_od": w_od, "w_out": w_out,
            "pool_w": pw_l, "wsT": wsT,
        })
    return maps


def kernel(**inputs):
    nc = _get_program(4)
    maps = prepare_inputs(**inputs)
    res = run_bass_kernel_spmd(nc, maps, core_ids=list(range(8)))
    outp = np.empty((4, 8192, D), np.float32)
    for c in range(8):
        b, hf = c // 2, c % 2
        outp[b, hf * 4096:(hf + 1) * 4096] = res.results[c]["out"]
    return outp
```

```python
import numpy as np
import ml_dtypes
from contextlib import ExitStack
import concourse.bass as bass
import concourse.mybir as mybir
from concourse.bass_utils import run_bass_kernel_spmd

F32 = mybir.dt.float32
BF16 = mybir.dt.bfloat16
AF = mybir.ActivationFunctionType
ALU = mybir.AluOpType

D = 1024
NSUB = 34
HALO = 2
NTOK = NSUB * 128
TILES = [(0, 2)] + [(2 + 4 * i, 4) for i in range(8)]
ALPHA = float((2 * 4) ** 0.25)
EPS = 1e-5
POOL_W = (2, 4, 8, 16)
ENGS = ("pe", "act", "dve", "pool", "sp")
DEFER_C_ODD = False
DEFER_C_EVEN = False
LITE_HALO = True
DIAG_ENG = "dve"
KDVE = 0


class Op:
    __slots__ = ("eng", "fn", "reads", "writes", "chan", "idx", "signal", "token", "waits")

    def __init__(self, eng, fn, reads, writes, chan):
        self.eng = eng
        self.fn = fn
        self.reads = reads
        self.writes = writes
        self.chan = chan
        self.signal = False
        self.token = None
        self.waits = []


class Prog:
    def __init__(self, nc):
        self.nc = nc
        self.ops = []

    def op(self, eng, fn, reads=(), writes=(), chan=None):
        o = Op(eng, fn, tuple(reads), tuple(writes), chan)
        o.idx = len(self.ops)
        self.ops.append(o)
        return o

    def dma(self, fn, reads=(), writes=(), chan=None, eng="sp"):
        assert chan is not None
        return self.op(eng, fn, reads, writes, chan)

    def finalize(self, stack):
        nc = self.nc
        ops = self.ops
        last_w = {}
        readers = {}
        deps = [None] * len(ops)
        chan_last = {}
        for o in ops:
            d = set()
            for k in o.reads:
                w = last_w.get(k)
                if w is not None:
                    d.add(w)
            for k in o.writes:
                w = last_w.get(k)
                if w is not None:
                    d.add(w)
                r = readers.get(k)
                if r:
                    d.update(r.values())
            if o.chan is not None:
                p = chan_last.get(o.chan)
                if p is not None:
                    d.add(p)
                chan_last[o.chan] = o.idx
            d.discard(o.idx)
            deps[o.idx] = d
            rk = ("c", o.chan) if o.chan is not None else o.eng
            for k in o.reads:
                readers.setdefault(k, {})[rk] = o.idx
            for k in o.writes:
                last_w[k] = o.idx
                readers[k] = {}
        for o in ops:
            for j in deps[o.idx]:
                oj = ops[j]
                if oj.chan is None and oj.eng == "pe" and o.eng == "pe" and o.chan is None:
                    continue
                oj.signal = True
        for o in ops:
            if o.chan is not None:
                o.signal = True
        sem_names = set()
        for o in ops:
            if o.signal:
                sem_names.add(("c", o.chan) if o.chan is not None else ("e", o.eng))
        sems = {}
        for n in sorted(sem_names):
            sems[n] = stack.enter_context(nc.semaphore("s_%s_%s" % n))
        counts = {n: 0 for n in sem_names}
        for o in ops:
            if o.signal:
                n = ("c", o.chan) if o.chan is not None else ("e", o.eng)
                counts[n] += 16 if o.chan is not None else 1
                o.token = (n, counts[n])
        self.max_counts = dict(counts)
        seen = {e: {} for e in ENGS}
        for o in ops:
            need = {}
            for j in deps[o.idx]:
                oj = ops[j]
                if oj.token is None:
                    continue
                n, v = oj.token
                if v > need.get(n, 0):
                    need[n] = v
            sd = seen[o.eng]
            for n, v in need.items():
                if sd.get(n, 0) >= v:
                    continue
                sd[n] = v
                o.waits.append((n, v))
        block = stack.enter_context(nc.Block())
        per_eng = {e: [o for o in ops if o.eng == e] for e in ENGS}

        def emit(engh, lst):
            for o in lst:
                for n, v in o.waits:
                    engh.wait_ge(sems[n], v)
                ins = o.fn(engh)
                if o.signal:
                    n, v = o.token
                    ins.then_inc(sems[n], 16 if o.chan is not None else 1)

        @block.tensor
        def _(e):
            emit(e, per_eng["pe"])

        @block.scalar
        def _(e):
            emit(e, per_eng["act"])

        @block.vector
        def _(e):
            emit(e, per_eng["dve"])

        @block.gpsimd
        def _(e):
            emit(e, per_eng["pool"])

        @block.sync
        def _(e):
            emit(e, per_eng["sp"])
            for n in sorted(sem_names):
                if n[0] == "c":
                    e.wait_ge(sems[n], counts[n])


def col_layout():
    lay = {}
    pos = 0

    def add(name, n):
        nonlocal pos
        lay[name] = pos
        pos += n

    add("flag", 1)
    add("one", 1)
    for i in range(2):
        add("ps%d" % i, 8)
        add("cw%d" % i, 24)
        add("cb%d" % i, 8)
    for i in range(2):
        add("sg%d" % i, 8)
        add("dw%d" % i, 248)
        add("db%d" % i, 8)
        add("ng%d" % i, 8)
        add("nb%d" % i, 8)
    return lay, pos


COL, NCOL = col_layout()
PM_CUR, PM_PREV, PM_SPH, PM_SPL, PM_SPP = 1, 5, 9, 13, 17
NCB = 21


def build_program(n_layers=4, tiles=None):
    TILES = tiles if tiles is not None else globals()["TILES"]
    NTOK = 128 * sum(t[1] for t in TILES)
    nc = bass.Bass("TRN2", target_bir_lowering=False)
    st = ExitStack()
    with st:
        def dram(name, shape, dt, kind):
            return nc.dram_tensor(name, list(shape), dt, kind=kind).ap()

        xin = dram("xin", [NTOK, D], F32, "ExternalInput")
        out = dram("out", [NTOK - HALO * 128, D], F32, "ExternalOutput")
        consts_bf_d = dram("consts_bf", [128, NCB * 128], BF16, "ExternalInput")
        maskT_d = dram("maskT", [128, 128], F32, "ExternalInput")
        colpack_d = dram("colpack", [128, NCOL], F32, "ExternalInput")
        lng_d = dram("ln_g", [4, D], F32, "ExternalInput")
        lnb_d = dram("ln_b", [4, D], F32, "ExternalInput")
        sgb_d = dram("sgu_ln_b", [2, D], F32, "ExternalInput")
        sgub_d = dram("sgu_b", [2, 512], F32, "ExternalInput")
        wxa_d = dram("w_xa", [2, 2, 128, 8 * 512], F32, "ExternalInput")
        wev_d = dram("w_ev", [2, 8, 128, 8 * 640], F32, "ExternalInput")
        wv_d = dram("w_v", [2, 2, 128, 8 * 512], F32, "ExternalInput")
        wod_d = dram("w_od", [2, 8, 128, 8 * 640], F32, "ExternalInput")
        wout_d = dram("w_out", [4, 128, 16 * 1024], F32, "ExternalInput")
        pw_d = dram("pool_w", [2, 128, 4 * 2 * 256], F32, "ExternalInput")
        wsT_d = dram("wsT", [2, 128, 512], F32, "ExternalInput")
        xs_a = dram("xs_a", [NTOK, D], F32, "Internal")
        xs_b = dram("xs_b", [NTOK, D], F32, "Internal")
        wxa_s = dram("wxa_s", [2, 2, 128, 8 * 512], BF16, "Internal")
        wev_s = dram("wev_s", [2, 8, 128, 8 * 640], BF16, "Internal")
        wv_s = dram("wv_s", [2, 2, 128, 8 * 512], BF16, "Internal")
        wod_s = dram("wod_s", [2, 8, 128, 8 * 640], BF16, "Internal")
        wout_s = dram("wout_s", [4, 128, 16 * 1024], BF16, "Internal")

        def sb(name, shape, dt):
            return st.enter_context(nc.sbuf_tensor(name, list(shape), dt))

        cbf = sb("cbf", [128, NCB, 128], BF16)
        ident = cbf[:, 0, :]
        onesD = sb("onesD", [128, 128], BF16)
        maskT = sb("maskT_s", [128, 128], F32)
        colp = sb("colp", [128, NCOL], F32)
        smalls = sb("smalls", [128, 8], F32)
        ones_row = sb("ones_row", [1, 128], F32)
        lng = sb("lng", [128, D], F32)
        lnb = sb("lnb", [128, D], F32)
        wout_bf = sb("wout_bf", [128, 16, 1024], BF16)
        pw_bf = sb("pw_bf", [128, 4, 2, 256], BF16)
        wsT_bf = sb("wsT_bf", [128, 4, 128], BF16)
        Bias = sb("Bias", [128, 8, 128], F32)
        dg = sb("dg", [128, 2, 16, 128], BF16)
        wbuf = sb("wbuf", [128, 2, 8, 640], BF16)
        xb16 = sb("xb16", [128, 4, D], BF16)
        xT = sb("xT", [128, 8, 512], BF16)
        concat = sb("concat", [128, 2, 16, 512], BF16)
        x32 = sb("x32", [128, D], F32)
        rbuf = sb("rbuf", [128, 2, D], F32)
        stt = sb("stt", [128, 4, 12], F32)
        mvt = sb("mvt", [128, 4, 4], F32)
        wt = sb("wt", [128, 248], F32)
        acc = sb("acc", [128, 2, 512], F32)
        UN = 0

        def carve(n):
            nonlocal UN
            o = UN
            UN += n + (n % 2)
            return o

        ev = {}
        ev["xa"] = carve(5 * 1024)
        ev["pT"] = carve(4 * 512)
        ev["sga"] = carve(4 * 512)
        ev["hsb"] = carve(2 * 512)
        ev["sgb"] = carve(2 * 512)
        ev["m"] = carve(2 * 512)
        ev["pst"] = carve(8 * 514)
        ev_end = UN
        UN = 0
        od = {}
        od["vn"] = carve(4 * 1024)
        od["t1"] = carve(2 * 1024)
        od["sgc"] = carve(2 * 512)
        od["t2"] = carve(2 * 512)
        od["th"] = carve(2 * 512)
        od["zst"] = carve(8 * 542)
        od["sgd"] = carve(8 * 512)
        od["czb"] = carve(8 * 512)
        od["sqb"] = carve(2 * 512)
        od["mean"] = carve(1024)
        od["rstd"] = carve(1024)
        od["dd"] = carve(2 * 1024)
        od["s1"] = carve(2 * 512)
        od_end = UN
        U = sb("U", [128, max(ev_end, od_end)], BF16)

        def ub(off, n):
            return U[:, off:off + n]

        def uf(off, n):
            return U[:, off:off + 2 * n].bitcast(F32)

        ps = st.enter_context(nc.psum_tensor("ps", [128, 8 * 512], F32))

        def bank(b):
            return ps[:, b * 512:(b + 1) * 512]

        print("SBUF bytes remaining:", nc.sbuf_bytes_remaining)

        P = Prog(nc)
        state = {"ring": 0, "ring_n": 8, "wu": 0, "castn": 0, "stat": 0, "xb": 0, "rb": 0}

        def nbank():
            b = state["ring"] % state["ring_n"]
            state["ring"] += 1
            return b

        def col(name, idx=0):
            c = COL[name] + idx
            return colp[:, c:c + 1]

        P.dma(lambda e: e.dma_start(out=cbf[:].rearrange("p a b -> p (a b)"), in_=consts_bf_d), writes=["cbf"], chan="k0")
        P.dma(lambda e: e.dma_start(out=maskT[:], in_=maskT_d), writes=["maskT"], chan="k1")
        P.dma(lambda e: e.dma_start(out=colp[:], in_=colpack_d), writes=["colp"], chan="k2")
        P.op("pool", lambda e: e.memset(onesD[:], 1.0 / 1024.0), writes=["onesD"])
        P.op("pool", lambda e: e.memset(smalls[:], -0.5), writes=["smalls"])
        P.op("pool", lambda e: e.memset(ones_row[:], 1.0), writes=["ones_row"])

        def cast_dma(src, dst, rowlen, key):
            n = state["castn"]
            state["castn"] += 1
            P.dma(lambda e: e.dma_start(out=dst, in_=src), writes=[key], chan="cast%d" % (n % 8), eng="pool")

        def cast_list(l):
            i = l // 2
            lst = []

            def add(src, dst, ncols, key):
                for c0 in range(0, ncols, 1024):
                    lst.append((src[:, c0:c0 + 1024], dst[:, c0:c0 + 1024], 1024, key + (c0 // 1024,)))

            if l % 2 == 0:
                for u in range(2):
                    add(wxa_d[i, u], wxa_s[i, u], 4096, ("wsc", l, "a", u))
                for u in range(8):
                    add(wev_d[i, u], wev_s[i, u], 5120, ("wsc", l, "c", u))
            else:
                for u in range(2):
                    add(wv_d[i, u], wv_s[i, u], 4096, ("wsc", l, "a", u))
                for u in range(8):
                    add(wod_d[i, u], wod_s[i, u], 5120, ("wsc", l, "c", u))
            for h in range(4):
                add(wout_d[l][:, h * 4096:(h + 1) * 4096], wout_s[l][:, h * 4096:(h + 1) * 4096], 4096, ("wosc", l, h))
            return lst

        def emit_casts(l):
            for a in cast_list(l):
                cast_dma(*a)

        def wunit_src(l, kind, u):
            i = l // 2
            if l % 2 == 0:
                return (wxa_s[i, u] if kind == "a" else wev_s[i, u])
            return (wv_s[i, u] if kind == "a" else wod_s[i, u])

        def load_wunit(l, kind, u, cols=None):
            k = state["wu"]
            state["wu"] += 1
            b = k % 2
            ncols = 512 if kind == "a" else 640
            src = wunit_src(l, kind, u).rearrange("p (k n) -> p k n", k=8)
            dst = wbuf[:, b, :, 0:ncols]
            if cols is not None:
                src = src[:, :, cols[0]:cols[1]]
                dst = wbuf[:, b, :, cols[0]:cols[1]]
            npieces = 4 if kind == "a" else 5
            P.dma(lambda e: e.dma_start(out=dst, in_=src), reads=[("wsc", l, kind, u, pc) for pc in range(npieces)], writes=[("wb", b)], chan="w%d" % b)
            return b

        def layer_in(l):
            return [xin, xs_a, xs_b, xs_a][l], ["in", "a", "b", "a"][l]

        def layer_out(l):
            if l == n_layers - 1:
                return out, "out"
            return [xs_a, xs_b, xs_a, None][l], ["a", "b", "a", None][l]

        def T_load(l, tile):
            s0, ns = tile
            src, skey = layer_in(l)
            for s in range(ns):
                S = s0 + s
                P.dma(lambda e, S=S, s=s: e.dma_start(out=xb16[:, s, :], in_=src[S * 128:(S + 1) * 128, :]),
                      reads=[("XD", skey, S)], writes=[("xb16", s)], chan="xin%d" % s, eng="pool")

        def T_phase(l, tile):
            s0, ns = tile
            for s in range(ns):
                b = nbank()
                pT = bank(b).bitcast(BF16)
                for j in range(8):
                    P.op("pe", lambda e, j=j, s=s, pT=pT: e.transpose(out=pT[:, j * 128:(j + 1) * 128], in_=xb16[:, s, j * 128:(j + 1) * 128], identity=ident),
                         reads=[("xb16", s), "cbf"], writes=[("ps", b)])
                P.op("dve", lambda e, s=s, pT=pT: e.tensor_copy(out=xT[:, :, s * 128:(s + 1) * 128], in_=pT.rearrange("p (j t) -> p j t", j=8)),
                     reads=[("ps", b)], writes=[("xT", s)])

        def proj_fm(b, wb_i, c0, N, ns):
            for kc in range(8):
                P.op("pe", lambda e, kc=kc: e.matmul(bank(b)[:, 0:N], lhsT=wbuf[:, wb_i, kc, c0:c0 + 128], rhs=xT[:, kc, 0:N], start=(kc == 0), stop=(kc == 7)),
                     reads=[("wb", wb_i)] + [("xT", s) for s in range(ns)], writes=[("ps", b)])

        def proj_tm(b, wb_i, s):
            for kc in range(8):
                P.op("pe", lambda e, kc=kc: e.matmul(bank(b)[:, 0:512], lhsT=xT[:, kc, s * 128:(s + 1) * 128], rhs=wbuf[:, wb_i, kc, 0:512], start=(kc == 0), stop=(kc == 7)),
                     reads=[("wb", wb_i), ("xT", s)], writes=[("ps", b)])

        def ln_chain(slot, srcs, keys):
            for h, a in enumerate(srcs):
                P.op("dve", lambda e, h=h, a=a: e.bn_stats(out=stt[:, slot, h * 6:(h + 1) * 6], in_=a),
                     reads=keys[h], writes=[("stt", slot, h)])
            P.op("dve", lambda e: e.bn_aggr(out=mvt[:, slot, 0:2], in_=stt[:, slot, :]),
                 reads=[("stt", slot, 0), ("stt", slot, 1)], writes=[("mv", slot)])
            P.op("dve", lambda e: e.tensor_scalar(out=mvt[:, slot, 1:2], in0=mvt[:, slot, 1:2], scalar1=EPS, scalar2=None, op0=ALU.add),
                 reads=[("mv", slot)], writes=[("mv", slot)])
            P.op("pool", lambda e: e.tensor_tensor(out=mvt[:, slot, 2:3], in0=mvt[:, slot, 1:2], in1=smalls[:, 0:1], op=ALU.pow),
                 reads=[("mv", slot), "smalls"], writes=[("rs", slot)])
            P.op("dve", lambda e: e.tensor_scalar(out=mvt[:, slot, 3:4], in0=mvt[:, slot, 0:1], scalar1=mvt[:, slot, 2:3], scalar2=-1.0, op0=ALU.mult, op1=ALU.mult),
                 reads=[("mv", slot), ("rs", slot)], writes=[("nm", slot)])

        def C_items(l, tile, ti):
            s0, ns = tile
            src, skey = layer_in(l)
            dst, dkey = layer_out(l)
            last = (l == n_layers - 1)
            cp = ti % 2

            def c_sub(s):
                S = s0 + s
                slot = state["stat"] % 4
                state["stat"] += 1
                rb = state["rb"] % 2
                state["rb"] += 1
                P.dma(lambda e: e.dma_start(out=x32[:], in_=src[S * 128:(S + 1) * 128, :]),
                      reads=[("XD", skey, S)], writes=["x32"], chan="x32")
                bs = [nbank(), nbank()]
                for h in range(2):
                    for kq in range(16):
                        P.op("pe", lambda e, h=h, kq=kq: e.matmul(bank(bs[h])[:, 0:512], lhsT=concat[:, cp, kq, s * 128:(s + 1) * 128], rhs=wout_bf[:, kq, h * 512:(h + 1) * 512], start=(kq == 0), stop=(kq == 15)),
                             reads=[("cc", cp, kq), "wout"], writes=[("ps", bs[h])])
                for h in range(2):
                    P.op("dve", lambda e, h=h: e.scalar_tensor_tensor(out=rbuf[:, rb, h * 512:(h + 1) * 512], in0=x32[:, h * 512:(h + 1) * 512], scalar=ALPHA, in1=bank(bs[h])[:, 0:512], op0=ALU.mult, op1=ALU.add),
                         reads=["x32", ("ps", bs[h])], writes=[("r", rb, h)])
                ln_chain(slot, [rbuf[:, rb, 0:512], rbuf[:, rb, 512:1024]], [[("r", rb, 0)], [("r", rb, 1)]])
                P.op("act", lambda e: e.activation(out=rbuf[:, rb, :], in_=rbuf[:, rb, :], func=AF.Identity, scale=mvt[:, slot, 2:3], bias=mvt[:, slot, 3:4]),
                     reads=[("r", rb, 0), ("r", rb, 1), ("rs", slot), ("nm", slot)], writes=[("r", rb, 0), ("r", rb, 1)])
                P.op("pool", lambda e: e.tensor_tensor(out=rbuf[:, rb, :], in0=rbuf[:, rb, :], in1=lng[:], op=ALU.mult),
                     reads=[("r", rb, 0), ("r", rb, 1), "lng"], writes=[("r", rb, 0), ("r", rb, 1)])
                P.op("pool", lambda e: e.tensor_tensor(out=rbuf[:, rb, :], in0=rbuf[:, rb, :], in1=lnb[:], op=ALU.add),
                     reads=[("r", rb, 0), ("r", rb, 1), "lnb"], writes=[("r", rb, 0), ("r", rb, 1)])
                row = (S - HALO) if last else S
                P.dma(lambda e: e.dma_start(out=dst[row * 128:(row + 1) * 128, :], in_=rbuf[:, rb, :]),
                      reads=[("r", rb, 0), ("r", rb, 1)], writes=[("XD", dkey, S)], chan="st%d" % rb, eng="pool")

            items = []
            for s in range(ns):
                if last and (s0 + s) < HALO:
                    continue
                items.append(lambda s=s: c_sub(s))
            return items

        def run_staged(stages, n, hooks, early=(), tick=None):
            ns_ = len(stages)
            hooks = list(hooks)
            nh = len(hooks)
            pts = {}
            if nh:
                for h in range(nh):
                    pts.setdefault(min(n - 1, 1 + (2 * h if nh <= 4 else h)), []).append(hooks[h])
            for it in range(n + ns_ - 1):
                for j, stg in enumerate(stages):
                    u = it - j
                    if 0 <= u < n:
                        stg(u)
                if it == 0:
                    for h in early:
                        h()
                if tick is not None:
                    tick()
                for h in pts.get(it, []):
                    h()

        def even_layer(l):
            i = l // 2
            xa = ub(ev["xa"], 5 * 1024).rearrange("p (s c) -> p s c", s=5)
            pT_ = ub(ev["pT"], 4 * 512).rearrange("p (g k n) -> p g k n", g=2, k=2)
            sga = ub(ev["sga"], 4 * 512).rearrange("p (g k n) -> p g k n", g=2, k=2)
            hsb = ub(ev["hsb"], 2 * 512).rearrange("p (a n) -> p a n", a=2)
            sgb = ub(ev["sgb"], 2 * 512).rearrange("p (a n) -> p a n", a=2)
            mm = ub(ev["m"], 2 * 512).rearrange("p (a n) -> p a n", a=2)
            pst = ub(ev["pst"], 8 * 514).rearrange("p (q n) -> p q n", q=8)
            dg3 = dg[:].rearrange("p a b c -> p (a b c)")[:, 0:8 * 3 * 128].rearrange("p (q j c) -> p q j c", q=8, j=3)
            P.dma(lambda e: e.dma_start(out=pw_bf[:].rearrange("p g k n -> p (g k n)"), in_=pw_d[i]), writes=["pw"], chan="pwc", eng="pool")
            for q in range(8):
                for j in range(3):
                    P.op("pool", lambda e, q=q, j=j: e.tensor_scalar(out=dg3[:, q, j, :], in0=ident, scalar1=col("cw%d" % i, j * 8 + q), scalar2=1.0, op0=ALU.mult, op1=ALU.mult),
                         reads=["cbf", "colp"], writes=["dg3"])
            P.op("pool", lambda e: e.memset(xa[:, 0, :], 0.0), writes=[("xa", 0, 0), ("xa", 0, 1)])
            P.op("pool", lambda e: e.memset(pst[:, :, 0:2], 0.0), writes=[("pst", q) for q in range(8)])
            state["ring_n"] = 8
            tstate = {}

            def load_next_c():
                if tstate["q"] < 8:
                    w = load_wunit(l, "c", tstate["q"])
                    tstate["q"] += 1
                    return w
                return None

            def pre(tile):
                s0, ns = tile
                wa = [load_wunit(l, "a", 0), load_wunit(l, "a", 1)]
                tstate["q"] = 0
                tstate["wq"] = {}
                for gg in range(2):
                    for s in range(ns):
                        b = nbank()
                        proj_tm(b, wa[gg], s)
                        P.op("act", lambda e, b=b, s=s, gg=gg: e.activation(out=xa[:, s + 1, gg * 512:(gg + 1) * 512], in_=bank(b)[:, 0:512], func=AF.Copy),
                             reads=[("ps", b)], writes=[("xa", s + 1, gg)])
                    tstate["wq"][gg] = load_next_c()

            def main(tile, ti, hooks, early=()):
                s0, ns = tile
                N = ns * 128
                cp = ti % 2
                wq = tstate["wq"]

                def stage1(q):
                    wb_i = wq[q]
                    g = q // 2
                    gp, qp = g % 2, q % 2
                    gg = q // 4
                    b_ga = nbank()
                    proj_fm(b_ga, wb_i, 0, N, ns)
                    P.op("act", lambda e: e.activation(out=sga[:, gp, qp, 0:N], in_=bank(b_ga)[:, 0:N], func=AF.Silu),
                         reads=[("ps", b_ga)], writes=[("sga", gp, qp)])
                    b_h = nbank()
                    proj_fm(b_h, wb_i, 128, N, ns)
                    P.op("act", lambda e: e.activation(out=hsb[:, qp, 0:N], in_=bank(b_h)[:, 0:N], func=AF.Copy),
                         reads=[("ps", b_h)], writes=[("hsb", qp)])
                    b_cg = nbank()
                    proj_fm(b_cg, wb_i, 384, N, ns)
                    P.op("dve", lambda e: e.tensor_tensor(out=pst[:, q, 2:2 + N], in0=bank(b_cg)[:, 0:N], in1=hsb[:, qp, 0:N], op=ALU.mult),
                         reads=[("ps", b_cg), ("hsb", qp)], writes=[("pst", q)])
                    b_gb = nbank()
                    proj_fm(b_gb, wb_i, 512, N, ns)
                    P.op("act", lambda e: e.activation(out=sgb[:, qp, 0:N], in_=bank(b_gb)[:, 0:N], func=AF.Silu),
                         reads=[("ps", b_gb)], writes=[("sgb", qp)])
                    b_bg = nbank()
                    proj_fm(b_bg, wb_i, 256, N, ns)
                    P.op("dve", lambda e: e.tensor_tensor(out=mm[:, qp, 0:N], in0=bank(b_bg)[:, 0:N], in1=sgb[:, qp, 0:N], op=ALU.mult),
                         reads=[("ps", b_bg), ("sgb", qp)], writes=[("m", qp)])
                    if q + 2 < 8:
                        wq[q + 2] = load_next_c()
                    b_pl = nbank()
                    for s in range(ns):
                        S = s0 + s
                        if S == HALO:
                            mats = [(s + 1, PM_SPH + g), (s + 1, PM_SPL + g), (s, PM_SPP + g)]
                        else:
                            mats = [(s + 1, PM_CUR + g), (s, PM_PREV + g)]
                        for idx, (slot, pm) in enumerate(mats):
                            P.op("pe", lambda e, s=s, slot=slot, pm=pm, idx=idx, nm=len(mats): e.matmul(bank(b_pl)[:, s * 128:(s + 1) * 128], lhsT=xa[:, slot, q * 128:(q + 1) * 128], rhs=cbf[:, pm, :], start=(idx == 0), stop=(idx == nm - 1)),
                                 reads=[("xa", slot, gg), "cbf"], writes=[("ps", b_pl)])
                    P.op("act", lambda e: e.activation(out=pT_[:, gp, qp, 0:N], in_=bank(b_pl)[:, 0:N], func=AF.Copy),
                         reads=[("ps", b_pl)], writes=[("pT", gp, qp)])

                def stage2(q):
                    g = q // 2
                    gp, qp = g % 2, q % 2
                    b_cv = nbank()
                    for j in range(3):
                        P.op("pe", lambda e, j=j: e.matmul(bank(b_cv)[:, 0:N], lhsT=dg3[:, q, j, :], rhs=pst[:, q, j:j + N], start=(j == 0), stop=(j == 2)),
                             reads=["dg3", ("pst", q)], writes=[("ps", b_cv)])
                    P.op("dve", lambda e: e.scalar_tensor_tensor(out=concat[:, cp, 8 + q, 0:N], in0=bank(b_cv)[:, 0:N], scalar=col("cb%d" % i, q), in1=mm[:, qp, 0:N], op0=ALU.add, op1=ALU.mult),
                         reads=[("ps", b_cv), ("m", qp), "colp"], writes=[("cc", cp, 8 + q)])
                    fl = col("flag") if ti == 0 else col("one")
                    P.op("pool", lambda e: e.tensor_scalar(out=pst[:, q, 0:2], in0=pst[:, q, N:N + 2], scalar1=fl, scalar2=1.0, op0=ALU.mult, op1=ALU.mult),
                         reads=[("pst", q), "colp"], writes=[("pst", q)])
                    if q % 2 == 1:
                        for qo in (q - 1, q):
                            b_pw = nbank()
                            for k2 in range(2):
                                P.op("pe", lambda e, k2=k2, qo=qo, b_pw=b_pw: e.matmul(bank(b_pw)[:, 0:N], lhsT=pw_bf[:, g, k2, (qo % 2) * 128:(qo % 2 + 1) * 128], rhs=pT_[:, gp, k2, 0:N], start=(k2 == 0), stop=(k2 == 1)),
                                     reads=["pw", ("pT", gp, 0), ("pT", gp, 1)], writes=[("ps", b_pw)])
                            P.op("dve", lambda e, qo=qo, b_pw=b_pw: e.scalar_tensor_tensor(out=concat[:, cp, qo, 0:N], in0=bank(b_pw)[:, 0:N], scalar=col("ps%d" % i, qo), in1=sga[:, gp, qo % 2, 0:N], op0=ALU.mult, op1=ALU.mult),
                                 reads=[("ps", b_pw), ("sga", gp, qo % 2), "colp"], writes=[("cc", cp, qo)])

                run_staged([stage1, stage2], 8, hooks, early, tstate.get("tick"))
                P.op("pool", lambda e: e.tensor_copy(out=xa[:, 0, :], in_=xa[:, ns, :]),
                     reads=[("xa", ns, 0), ("xa", ns, 1)], writes=[("xa", 0, 0), ("xa", 0, 1)])

            def transition(tile, nxt):
                if nxt is not None:
                    pre(nxt)

            main.tstate = tstate
            return pre, main, transition

        def odd_layer(l):
            i = l // 2
            vn = ub(od["vn"], 4 * 1024).rearrange("p (s c) -> p s c", s=4)
            t1 = uf(od["t1"], 2 * 512).rearrange("p (a n) -> p a n", a=2)
            sgc = ub(od["sgc"], 2 * 512).rearrange("p (a n) -> p a n", a=2)
            t2 = ub(od["t2"], 2 * 512).rearrange("p (a n) -> p a n", a=2)
            th = ub(od["th"], 2 * 512).rearrange("p (a n) -> p a n", a=2)
            zst = ub(od["zst"], 8 * 542).rearrange("p (q n) -> p q n", q=8)
            sgd = ub(od["sgd"], 8 * 512).rearrange("p (q n) -> p q n", q=8)
            czb = ub(od["czb"], 8 * 512).rearrange("p (q n) -> p q n", q=8)
            sqb = ub(od["sqb"], 2 * 512).rearrange("p (a n) -> p a n", a=2)
            mean = uf(od["mean"], 512)
            rstd = uf(od["rstd"], 512)
            dd = uf(od["dd"], 2 * 512).rearrange("p (a n) -> p a n", a=2)
            s1 = ub(od["s1"], 2 * 512).rearrange("p (a n) -> p a n", a=2)
            MEANB, EX2B = 6, 7
            state["ring_n"] = 6
            wsf = rbuf[:, 0, 0:512]
            bsg = rbuf[:, 1, :]
            sgr = x32[0:1, 0:512]
            P.dma(lambda e: e.dma_start(out=wsf, in_=wsT_d[i]), writes=[("r", 0, 0)], chan="k0")
            P.dma(lambda e: e.dma_start(out=bsg, in_=sgb_d[i:i + 1, :].partition_broadcast(128)), writes=[("r", 1, 0), ("r", 1, 1)], chan="k1")
            P.dma(lambda e: e.dma_start(out=sgr, in_=sgub_d[i:i + 1, :]), writes=["x32"], chan="k2")
            for hd in range(4):
                P.op("dve", lambda e, hd=hd: e.tensor_tensor(out=wsf[:, hd * 128:(hd + 1) * 128], in0=wsf[:, hd * 128:(hd + 1) * 128], in1=maskT[:], op=ALU.mult),
                     reads=[("r", 0, 0), "maskT"], writes=[("r", 0, 0)])
            P.op("dve", lambda e: e.tensor_copy(out=wsT_bf[:].rearrange("p h n -> p (h n)"), in_=wsf), reads=[("r", 0, 0)], writes=["wsT"])
            for q in range(8):
                hd = q // 2
                b = nbank()
                P.op("pe", lambda e, q=q, hd=hd, b=b: e.matmul(bank(b)[:, 0:128], lhsT=bsg[:, q * 128:(q + 1) * 128], rhs=wsf[:, hd * 128:(hd + 1) * 128], start=True, stop=False),
                     reads=[("r", 1, 0), ("r", 1, 1), ("r", 0, 0)], writes=[("ps", b)])
                P.op("pe", lambda e, hd=hd, b=b: e.matmul(bank(b)[:, 0:128], lhsT=ones_row[0:1, :], rhs=sgr[0:1, hd * 128:(hd + 1) * 128], start=False, stop=True),
                     reads=["ones_row", "x32"], writes=[("ps", b)])
                P.op("act", lambda e, q=q, b=b: e.activation(out=Bias[:, q, :], in_=bank(b)[:, 0:128], func=AF.Copy),
                     reads=[("ps", b)], writes=["Bias"])
            P.op("pool", lambda e: e.memset(zst[:, :, 0:30], 0.0), writes=[("zst", q) for q in range(8)])
            dwc = COL["dw%d" % i]
            P.op("dve", lambda e: e.tensor_scalar(out=wt[:], in0=colp[:, dwc:dwc + 248], scalar1=0.5, scalar2=None, op0=ALU.mult),
                 reads=["colp"], writes=["wt"])

            def build_diag(q, hs):
                j0 = KDVE if hs == 0 else 16
                nj = (16 - KDVE) if hs == 0 else 15
                P.op(DIAG_ENG, lambda e: e.tensor_tensor(out=dg[:, hs, 0:nj, :], in0=ident.unsqueeze(1).to_broadcast([128, nj, 128]),
                                                      in1=wt[:, q * 31 + j0:q * 31 + j0 + nj].unsqueeze(2).to_broadcast([128, nj, 128]), op=ALU.mult),
                     reads=["cbf", "wt"], writes=[("dg", hs)])

            tstate = {}

            def load_next_c():
                if tstate["q"] < 8:
                    w = load_wunit(l, "c", tstate["q"])
                    tstate["q"] += 1
                    return w
                return None

            def pre_items(tile):
                s0, ns = tile

                def begin():
                    tstate["wa"] = [load_wunit(l, "a", 0), load_wunit(l, "a", 1)]
                    tstate["q"] = 0

                def c1_sub(s):
                    wa = tstate["wa"]
                    slot = state["stat"] % 4
                    state["stat"] += 1
                    bs = [nbank(), nbank()]
                    for gg in range(2):
                        proj_tm(bs[gg], wa[gg], s)
                    ln_chain(slot, [bank(bs[0])[:, 0:512], bank(bs[1])[:, 0:512]], [[("ps", bs[0])], [("ps", bs[1])]])
                    for gg in range(2):
                        P.op("act", lambda e, gg=gg: e.activation(out=vn[:, s, gg * 512:(gg + 1) * 512], in_=bank(bs[gg])[:, 0:512], func=AF.Identity, scale=mvt[:, slot, 2:3], bias=mvt[:, slot, 3:4]),
                             reads=[("ps", bs[gg]), ("rs", slot), ("nm", slot)], writes=[("vn", s, gg)])

                return [begin] + [(lambda s=s: c1_sub(s)) for s in range(ns)]

            def pre(tile):
                for f in pre_items(tile):
                    f()

            def main(tile, ti, hooks, early=()):
                s0, ns = tile
                N = ns * 128
                cp = ti % 2
                wq = {0: load_next_c()}
                wq[1] = load_next_c()

                def stage1(q):
                    wb_i = wq[q]
                    hd = q // 2
                    qp = q % 2
                    gg = q // 4
                    b_gc = nbank()
                    proj_fm(b_gc, wb_i, 0, N, ns)
                    P.op("act", lambda e: e.activation(out=sgc[:, qp, 0:N], in_=bank(b_gc)[:, 0:N], func=AF.Silu),
                         reads=[("ps", b_gc)], writes=[("sgc", qp)])
                    b_bl = nbank()
                    proj_fm(b_bl, wb_i, 384, N, ns)
                    P.op("act", lambda e: e.activation(out=th[:, qp, 0:N], in_=bank(b_bl)[:, 0:N], func=AF.Tanh, scale=0.5),
                         reads=[("ps", b_bl)], writes=[("th", qp)])
                    b_a = nbank()
                    proj_fm(b_a, wb_i, 256, N, ns)
                    P.op("dve", lambda e: e.scalar_tensor_tensor(out=zst[:, q, 30:30 + N], in0=th[:, qp, 0:N], scalar=1.0, in1=bank(b_a)[:, 0:N], op0=ALU.add, op1=ALU.mult),
                         reads=[("th", qp), ("ps", b_a)], writes=[("zst", q)])
                    b_sg = nbank()
                    for s in range(ns):
                        P.op("pe", lambda e, s=s: e.matmul(bank(b_sg)[:, s * 128:(s + 1) * 128], lhsT=vn[:, s, q * 128:(q + 1) * 128], rhs=wsT_bf[:, hd, :], start=True, stop=True),
                             reads=[("vn", s, gg), "wsT"], writes=[("ps", b_sg)])
                    P.op("dve", lambda e: e.scalar_tensor_tensor(out=t1[:, qp, 0:N].rearrange("p (s n) -> p s n", s=ns), in0=bank(b_sg)[:, 0:N].rearrange("p (s n) -> p s n", s=ns), scalar=col("sg%d" % i, q), in1=Bias[:, q:q + 1, :].to_broadcast([128, ns, 128]), op0=ALU.mult, op1=ALU.add),
                         reads=[("ps", b_sg), "Bias", "colp"], writes=[("t1", qp)])
                    b_u = nbank()
                    proj_fm(b_u, wb_i, 128, N, ns)
                    P.op("dve", lambda e: e.tensor_tensor(out=t2[:, qp, 0:N], in0=bank(b_u)[:, 0:N], in1=t1[:, qp, 0:N], op=ALU.mult),
                         reads=[("ps", b_u), ("t1", qp)], writes=[("t2", qp)])
                    P.op("dve", lambda e: e.tensor_tensor(out=concat[:, cp, q, 0:N], in0=t2[:, qp, 0:N], in1=sgc[:, qp, 0:N], op=ALU.mult),
                         reads=[("t2", qp), ("sgc", qp)], writes=[("cc", cp, q)])
                    b_gd = nbank()
                    proj_fm(b_gd, wb_i, 512, N, ns)
                    P.op("act", lambda e: e.activation(out=sgd[:, q, 0:N], in_=bank(b_gd)[:, 0:N], func=AF.Silu),
                         reads=[("ps", b_gd)], writes=[("sgd", q)])
                    if q + 2 < 8:
                        wq[q + 2] = load_next_c()
                    if q == 0:
                        build_diag(0, 0)
                        build_diag(0, 1)

                def stage2(q):
                    qp = q % 2
                    for j in range(KDVE):
                        wc = wt[:, q * 31 + j:q * 31 + j + 1]
                        if j == 0:
                            P.op("dve", lambda e, j=j, wc=wc: e.tensor_scalar(out=acc[:, qp, 0:N], in0=zst[:, q, j:j + N], scalar1=wc, scalar2=None, op0=ALU.mult),
                                 reads=[("zst", q), "wt"], writes=[("acc", qp)])
                        else:
                            P.op("dve", lambda e, j=j, wc=wc: e.scalar_tensor_tensor(out=acc[:, qp, 0:N], in0=zst[:, q, j:j + N], scalar=wc, in1=acc[:, qp, 0:N], op0=ALU.mult, op1=ALU.add),
                                 reads=[("zst", q), "wt", ("acc", qp)], writes=[("acc", qp)])
                    b_cv = nbank()
                    for hs in range(2):
                        j0 = KDVE if hs == 0 else 16
                        nj = (16 - KDVE) if hs == 0 else 15
                        for jj in range(nj):
                            j = j0 + jj
                            P.op("pe", lambda e, jj=jj, j=j, hs=hs: e.matmul(bank(b_cv)[:, 0:N], lhsT=dg[:, hs, jj, :], rhs=zst[:, q, j:j + N], start=(j == KDVE), stop=(j == 30)),
                                 reads=[("dg", hs), ("zst", q)], writes=[("ps", b_cv)])
                        if q + 1 < 8:
                            build_diag(q + 1, hs)
                    if KDVE > 0:
                        P.op("dve", lambda e: e.scalar_tensor_tensor(out=czb[:, q, 0:N], in0=bank(b_cv)[:, 0:N], scalar=col("db%d" % i, q), in1=acc[:, qp, 0:N], op0=ALU.add, op1=ALU.add),
                             reads=[("ps", b_cv), "colp", ("acc", qp)], writes=[("czb", q)])
                        P.op("act", lambda e: e.activation(out=sqb[:, qp, 0:N], in_=czb[:, q, 0:N], func=AF.Square),
                             reads=[("czb", q)], writes=[("sqb", qp)])
                    else:
                        P.op("act", lambda e: e.activation(out=czb[:, q, 0:N], in_=bank(b_cv)[:, 0:N], func=AF.Identity, bias=col("db%d" % i, q)),
                             reads=[("ps", b_cv), "colp"], writes=[("czb", q)])
                        P.op("act", lambda e: e.activation(out=sqb[:, qp, 0:N], in_=bank(b_cv)[:, 0:N], func=AF.Square, bias=col("db%d" % i, q)),
                             reads=[("ps", b_cv), "colp"], writes=[("sqb", qp)])
                    fl = col("flag") if ti == 0 else col("one")
                    P.op("pool", lambda e: e.tensor_scalar(out=zst[:, q, 0:30], in0=zst[:, q, N:N + 30], scalar1=fl, scalar2=1.0, op0=ALU.mult, op1=ALU.mult),
                         reads=[("zst", q), "colp"], writes=[("zst", q)])

                def stage3(q):
                    qp = q % 2
                    P.op("pe", lambda e: e.matmul(bank(MEANB)[:, 0:N], lhsT=onesD[:], rhs=czb[:, q, 0:N], start=(q == 0), stop=(q == 7)),
                         reads=["onesD", ("czb", q)], writes=[("ps", MEANB)])
                    P.op("pe", lambda e: e.matmul(bank(EX2B)[:, 0:N], lhsT=onesD[:], rhs=sqb[:, qp, 0:N], start=(q == 0), stop=(q == 7)),
                         reads=["onesD", ("sqb", qp)], writes=[("ps", EX2B)])

                run_staged([stage1, stage2, stage3], 8, hooks, early, tstate.get("tick"))

            def main_lite(tile, ti, early=()):
                s0, ns = tile
                N = ns * 128
                sl = ns - 1
                tstate["q"] = 0

                def load_part():
                    if tstate["q"] < 8:
                        w = load_wunit(l, "c", tstate["q"], cols=(256, 512))
                        tstate["q"] += 1
                        return w
                    return None

                wq = {0: load_part()}
                wq[1] = load_part()
                for h in early:
                    h()
                for q in range(8):
                    def unit(q=q):
                        wb_i = wq[q]
                        qp = q % 2
                        b_bl = nbank()
                        for kc in range(8):
                            P.op("pe", lambda e, kc=kc: e.matmul(bank(b_bl)[:, 0:128], lhsT=wbuf[:, wb_i, kc, 384:512], rhs=xT[:, kc, sl * 128:(sl + 1) * 128], start=(kc == 0), stop=(kc == 7)),
                                 reads=[("wb", wb_i), ("xT", sl)], writes=[("ps", b_bl)])
                        P.op("act", lambda e: e.activation(out=th[:, qp, 0:128], in_=bank(b_bl)[:, 0:128], func=AF.Tanh, scale=0.5),
                             reads=[("ps", b_bl)], writes=[("th", qp)])
                        b_a = nbank()
                        for kc in range(8):
                            P.op("pe", lambda e, kc=kc: e.matmul(bank(b_a)[:, 0:128], lhsT=wbuf[:, wb_i, kc, 256:384], rhs=xT[:, kc, sl * 128:(sl + 1) * 128], start=(kc == 0), stop=(kc == 7)),
                                 reads=[("wb", wb_i), ("xT", sl)], writes=[("ps", b_a)])
                        P.op("dve", lambda e: e.scalar_tensor_tensor(out=zst[:, q, 30 + sl * 128:30 + N], in0=th[:, qp, 0:128], scalar=1.0, in1=bank(b_a)[:, 0:128], op0=ALU.add, op1=ALU.mult),
                             reads=[("th", qp), ("ps", b_a)], writes=[("zst", q)])
                        fl = col("flag") if ti == 0 else col("one")
                        P.op("pool", lambda e: e.tensor_scalar(out=zst[:, q, 0:30], in0=zst[:, q, N:N + 30], scalar1=fl, scalar2=1.0, op0=ALU.mult, op1=ALU.mult),
                             reads=[("zst", q), "colp"], writes=[("zst", q)])
                        if q + 2 < 8:
                            wq[q + 2] = load_part()
                    unit()

            def s3d_items(tile, ti):
                s0, ns = tile
                N = ns * 128
                cp = ti % 2

                def fin():
                    P.op("act", lambda e: e.activation(out=mean[:, 0:N], in_=bank(MEANB)[:, 0:N], func=AF.Copy),
                         reads=[("ps", MEANB)], writes=["mean"])
                    m2 = dd[:, 0, 0:N]
                    P.op("dve", lambda e: e.tensor_tensor(out=m2, in0=mean[:, 0:N], in1=mean[:, 0:N], op=ALU.mult),
                         reads=["mean"], writes=[("dd", 0)])
                    P.op("dve", lambda e: e.scalar_tensor_tensor(out=rstd[:, 0:N], in0=bank(EX2B)[:, 0:N], scalar=EPS, in1=m2, op0=ALU.add, op1=ALU.subtract),
                         reads=[("ps", EX2B), ("dd", 0)], writes=["rstd"])
                    P.op("act", lambda e: e.activation(out=rstd[:, 0:N], in_=rstd[:, 0:N], func=AF.Sqrt),
                         reads=["rstd"], writes=["rstd"])
                    P.op("dve", lambda e: e.reciprocal(out=rstd[:, 0:N], in_=rstd[:, 0:N]),
                         reads=["rstd"], writes=["rstd"])

                def s3d_a(q):
                    qp = q % 2
                    P.op("dve", lambda e: e.tensor_tensor(out=dd[:, qp, 0:N], in0=czb[:, q, 0:N], in1=mean[:, 0:N], op=ALU.subtract),
                         reads=[("czb", q), "mean"], writes=[("dd", qp)])
                    P.op("dve", lambda e: e.tensor_tensor(out=dd[:, qp, 0:N], in0=dd[:, qp, 0:N], in1=rstd[:, 0:N], op=ALU.mult),
                         reads=[("dd", qp), "rstd"], writes=[("dd", qp)])
                    P.op("act", lambda e: e.activation(out=s1[:, qp, 0:N], in_=dd[:, qp, 0:N], func=AF.Silu, scale=col("ng%d" % i, q), bias=col("nb%d" % i, q)),
                         reads=[("dd", qp), "colp"], writes=[("s1", qp)])

                def s3d_b(q):
                    qp = q % 2
                    P.op("dve", lambda e: e.tensor_tensor(out=concat[:, cp, 8 + q, 0:N], in0=s1[:, qp, 0:N], in1=sgd[:, q, 0:N], op=ALU.mult),
                         reads=[("s1", qp), ("sgd", q)], writes=[("cc", cp, 8 + q)])

                def s3d_step(k):
                    if k < 8:
                        s3d_a(k)
                    if k >= 1:
                        s3d_b(k - 1)

                return [fin] + [(lambda k=k: s3d_step(k)) for k in range(9)]

            def transition(tile, nxt, ti, lite=False):
                bq = pre_items(nxt) if nxt is not None else []
                if lite:
                    while bq:
                        bq.pop(0)()
                    return
                a = s3d_items(tile, ti)
                a.pop(0)()
                if bq:
                    bq.pop(0)()
                if bq:
                    bq.pop(0)()
                for idx in range(9):
                    a.pop(0)()
                    if idx % 2 == 1 and bq:
                        bq.pop(0)()
                while bq:
                    bq.pop(0)()

            main.lite = main_lite
            main.tstate = tstate
            return pre, main, transition

        first_casts = cast_list(0)
        T_load(0, TILES[0])
        for _ in range(4 + 4 + 5 * 4):
            cast_dma(*first_casts.pop(0))
        for l in range(n_layers):
            if l > 0:
                P.op("pool", lambda e: e.memset(smalls[:, 4:5], 0.0), reads=[], writes=["LAYER"])
            lstart = len(P.ops)
            is_even = (l % 2 == 0)
            pre, main, transition = even_layer(l) if is_even else odd_layer(l)
            pend = cast_list(l + 1) if l + 1 < n_layers else []
            if l == 0:
                while first_casts:
                    cast_dma(*first_casts.pop(0))
            else:
                T_load(l, TILES[0])
            P.dma(lambda e, l=l: e.dma_start(out=lng[:], in_=lng_d[l:l + 1, :].partition_broadcast(128)), writes=["lng"], chan="k0")
            P.dma(lambda e, l=l: e.dma_start(out=lnb[:], in_=lnb_d[l:l + 1, :].partition_broadcast(128)), writes=["lnb"], chan="k1")
            for h in range(4):
                P.dma(lambda e, l=l, h=h: e.dma_start(out=wout_bf[:, h * 4:(h + 1) * 4, :].rearrange("p k n -> p (k n)"), in_=wout_s[l][:, h * 4096:(h + 1) * 4096]),
                      reads=[("wosc", l, h, pc) for pc in range(4)], writes=["wout"], chan="wo%d" % h, eng="pool")
            T_phase(l, TILES[0])
            lite0 = LITE_HALO and (not is_even) and (l == n_layers - 1)
            if not lite0:
                pre(TILES[0])
            pendC = []
            per_tile = -(-len(pend) // len(TILES))
            for ti, tile in enumerate(TILES):
                budget = {"n": per_tile}

                def tick(budget=budget):
                    if pend and budget["n"] > 0:
                        cast_dma(*pend.pop(0))
                        budget["n"] -= 1

                main.tstate["tick"] = tick
                nxt = TILES[ti + 1] if ti + 1 < len(TILES) else None
                early = [(lambda nxt=nxt: T_load(l, nxt))] if nxt is not None else []
                lite = lite0 and ti == 0
                if lite:
                    main.lite(tile, ti, early)
                else:
                    main(tile, ti, pendC, early)
                if nxt is not None:
                    T_phase(l, nxt)
                if is_even:
                    transition(tile, nxt)
                else:
                    transition(tile, nxt, ti, lite)
                pendC = C_items(l, tile, ti)
                if (not is_even and not DEFER_C_ODD) or (is_even and not DEFER_C_EVEN):
                    for c in pendC:
                        c()
                    pendC = []
            for c in pendC:
                c()
            while pend:
                cast_dma(*pend.pop(0))
            for o in P.ops[lstart:]:
                o.reads = o.reads + ("LAYER",)
        print("ops:", len(P.ops), {e: sum(1 for o in P.ops if o.eng == e) for e in ENGS})
        P.finalize(st)
        print("sem max counts:", {k: v for k, v in P.max_counts.items() if k[0] == "e"})
    return nc


def _pool_mats():
    t = np.arange(128)
    mats = np.zeros((NCB, 128, 128), np.float32)
    mats[0] = np.eye(128)
    s_ = t[:, None]
    t_ = t[None, :]
    for g, w in enumerate(POOL_W):
        band = ((t_ - s_) >= 0) & ((t_ - s_) <= w - 1)
        mats[PM_CUR + g] = band / w - np.eye(128)
        mats[PM_PREV + g] = ((t_ + 128 - s_) <= w - 1) / w
    return mats


def _pool_special(first_half):
    g_ = _pool_mats()
    hi = np.zeros((4, 128, 128), np.float32)
    lo = np.zeros((4, 128, 128), np.float32)
    pv = np.zeros((4, 128, 128), np.float32)
    t = np.arange(128)
    s_ = t[:, None]
    t_ = t[None, :]
    for g, w in enumerate(POOL_W):
        if first_half:
            band = ((t_ - s_) >= 0) & ((t_ - s_) <= w - 1)
            inv = (1.0 / np.minimum(t + 1, w)).astype(np.float32)[None, :]
            full = band * inv
            h = full.astype(ml_dtypes.bfloat16).astype(np.float32)
            hi[g] = h - np.eye(128)
            lo[g] = full - h
        else:
            hi[g] = g_[PM_CUR + g]
            pv[g] = g_[PM_PREV + g]
    return hi, lo, pv


_CACHE = {}


def _get_program(n_layers=4):
    if n_layers not in _CACHE:
        _CACHE[n_layers] = build_program(n_layers)
    return _CACHE[n_layers]


def _unit_layout(w, col_groups):
    outs = []
    for cols in col_groups:
        sub = w[:, cols]
        n = sub.shape[1]
        outs.append(np.ascontiguousarray(sub.reshape(8, 128, n).transpose(1, 0, 2)).reshape(128, 8 * n))
    return np.stack(outs)


def prepare_inputs(x, ln_g, ln_b, w_in_even, w_out_even, pool_w, pool_scale, sconv_w, sconv_b,
                   w_in_odd, w_out_odd, sgu_ln_g, sgu_ln_b, sgu_w, sgu_b,
                   dconv_w, dconv_b, dnorm_g, dnorm_b):
    f = np.float32
    x = np.asarray(x, f)
    ar = np.arange
    xa_groups = [ar(0, 512), ar(512, 1024)]
    ev_groups = [np.concatenate([1024 + q * 128 + ar(128), 2048 + q * 128 + ar(128), 3072 + q * 128 + ar(128),
                                 4096 + q * 128 + ar(128), 5120 + q * 128 + ar(128)]) for q in range(8)]
    v_groups = [1024 + ar(0, 512), 1024 + ar(512, 1024)]
    od_groups = [np.concatenate([2048 + q * 128 + ar(128), 0 + q * 128 + ar(128), 3072 + q * 128 + ar(128),
                                 4096 + q * 128 + ar(128), 5120 + q * 128 + ar(128)]) for q in range(8)]
    w_in_even = np.asarray(w_in_even, f)
    w_in_odd = np.asarray(w_in_odd, f)
    w_xa = np.stack([_unit_layout(w_in_even[i], xa_groups) for i in range(2)])
    w_ev = np.stack([_unit_layout(w_in_even[i], ev_groups) for i in range(2)])
    w_v = np.stack([_unit_layout(w_in_odd[i], v_groups) for i in range(2)])
    w_od = np.stack([_unit_layout(w_in_odd[i], od_groups) for i in range(2)])
    wouts = [np.asarray(w_out_even, f)[0], np.asarray(w_out_odd, f)[0], np.asarray(w_out_even, f)[1], np.asarray(w_out_odd, f)[1]]
    w_out = np.stack([np.ascontiguousarray(w.reshape(16, 128, 1024).transpose(1, 0, 2)).reshape(128, 16 * 1024) for w in wouts])
    pw = np.asarray(pool_w, f)
    pw_l = np.ascontiguousarray(pw.reshape(2, 4, 2, 128, 256).transpose(0, 3, 1, 2, 4)).reshape(2, 128, 2048)
    sw = np.asarray(sgu_w, f)
    wsT = np.ascontiguousarray(sw.transpose(0, 3, 1, 2)).reshape(2, 128, 512)
    idx = np.arange(128)
    maskT = ((idx[:, None] // 64) <= (idx[None, :] // 64)).astype(f)
    lng4 = np.asarray(ln_g, f)
    lnb4 = np.asarray(ln_b, f)
    sgb2 = np.asarray(sgu_ln_b, f)
    sgub = np.asarray(sgu_b, f).reshape(2, 512)

    def cols8(v):
        return np.asarray(v, f).reshape(8, 128).T

    base = np.zeros((128, NCOL), f)
    base[:, COL["one"]] = 1.0
    for i in range(2):
        base[:, COL["ps%d" % i]:COL["ps%d" % i] + 8] = cols8(pool_scale[i])
        for j in range(3):
            base[:, COL["cw%d" % i] + j * 8:COL["cw%d" % i] + j * 8 + 8] = cols8(np.asarray(sconv_w)[i, j])
        base[:, COL["cb%d" % i]:COL["cb%d" % i] + 8] = cols8(sconv_b[i])
        base[:, COL["sg%d" % i]:COL["sg%d" % i] + 8] = cols8(sgu_ln_g[i])
        for j in range(31):
            base[:, COL["dw%d" % i] + j:COL["dw%d" % i] + 248:31] = cols8(np.asarray(dconv_w)[i, j])
        base[:, COL["db%d" % i]:COL["db%d" % i] + 8] = cols8(dconv_b[i])
        base[:, COL["ng%d" % i]:COL["ng%d" % i] + 8] = cols8(dnorm_g[i])
        base[:, COL["nb%d" % i]:COL["nb%d" % i] + 8] = cols8(dnorm_b[i])
    gen = _pool_mats()
    maps = []
    for c in range(8):
        b, hf = c // 2, c % 2
        if hf == 0:
            xw = np.concatenate([np.zeros((HALO * 128, D), f), x[b, 0:4096]], axis=0)
        else:
            xw = x[b, 4096 - HALO * 128:8192]
        cm = gen.copy()
        hi, lo, pv = _pool_special(hf == 0)
        cm[PM_SPH:PM_SPH + 4] = hi
        cm[PM_SPL:PM_SPL + 4] = lo
        cm[PM_SPP:PM_SPP + 4] = pv
        cbf = np.ascontiguousarray(cm.transpose(1, 0, 2)).reshape(128, NCB * 128).astype(ml_dtypes.bfloat16)
        cp = base.copy()
        cp[:, COL["flag"]] = float(hf)
        maps.append({
            "xin": np.ascontiguousarray(xw), "consts_bf": cbf, "maskT": maskT, "colpack": cp,
            "ln_g": lng4, "ln_b": lnb4, "sgu_ln_b": sgb2, "sgu_b": sgub,
            "w_xa": w_xa, "w_ev": w_ev, "w_v": w_v, "w_od": w_od, "w_out": w_out,
            "pool_w": pw_l, "wsT": wsT,
        })
    return maps


def kernel(**inputs):
    nc = _get_program(4)
    maps = prepare_inputs(**inputs)
    res = run_bass_kernel_spmd(nc, maps, core_ids=list(range(8)))
    outp = np.empty((4, 8192, D), np.float32)
    for c in range(8):
        b, hf = c // 2, c % 2
        outp[b, hf * 4096:(hf + 1) * 4096] = res.results[c]["out"]
    return outp
```

```python
import numpy as np
import ml_dtypes
from contextlib import ExitStack
import concourse.bass as bass
import concourse.mybir as mybir
from concourse.bass_utils import run_bass_kernel_spmd

F32 = mybir.dt.float32
BF16 = mybir.dt.bfloat16
AF = mybir.ActivationFunctionType
ALU = mybir.AluOpType

D = 1024
NSUB = 34
HALO = 2
NTOK = NSUB * 128
TILES = [(0, 2)] + [(2 + 4 * i, 4) for i in range(8)]
ALPHA = float((2 * 4) ** 0.25)
EPS = 1e-5
POOL_W = (2, 4, 8, 16)
ENGS = ("pe", "act", "dve", "pool", "sp")
DEFER_C_ODD = True
DEFER_C_EVEN = False
LITE_HALO = True
HOOKS_AT0 = True
DIAG_ENG = "dve"
KDVE = 0


class Op:
    __slots__ = ("eng", "fn", "reads", "writes", "chan", "idx", "signal", "token", "waits")

    def __init__(self, eng, fn, reads, writes, chan):
        self.eng = eng
        self.fn = fn
        self.reads = reads
        self.writes = writes
        self.chan = chan
        self.signal = False
        self.token = None
        self.waits = []


class Prog:
    def __init__(self, nc):
        self.nc = nc
        self.ops = []

    def op(self, eng, fn, reads=(), writes=(), chan=None):
        o = Op(eng, fn, tuple(reads), tuple(writes), chan)
        o.idx = len(self.ops)
        self.ops.append(o)
        return o

    def dma(self, fn, reads=(), writes=(), chan=None, eng="sp"):
        assert chan is not None
        return self.op(eng, fn, reads, writes, chan)

    def finalize(self, stack):
        nc = self.nc
        ops = self.ops
        last_w = {}
        readers = {}
        deps = [None] * len(ops)
        chan_last = {}
        for o in ops:
            d = set()
            for k in o.reads:
                w = last_w.get(k)
                if w is not None:
                    d.add(w)
            for k in o.writes:
                w = last_w.get(k)
                if w is not None:
                    d.add(w)
                r = readers.get(k)
                if r:
                    d.update(r.values())
            if o.chan is not None:
                p = chan_last.get(o.chan)
                if p is not None:
                    d.add(p)
                chan_last[o.chan] = o.idx
            d.discard(o.idx)
            deps[o.idx] = d
            rk = ("c", o.chan) if o.chan is not None else o.eng
            for k in o.reads:
                readers.setdefault(k, {})[rk] = o.idx
            for k in o.writes:
                last_w[k] = o.idx
                readers[k] = {}
        for o in ops:
            for j in deps[o.idx]:
                oj = ops[j]
                if oj.chan is None and oj.eng == "pe" and o.eng == "pe" and o.chan is None:
                    continue
                oj.signal = True
        for o in ops:
            if o.chan is not None:
                o.signal = True
        sem_names = set()
        for o in ops:
            if o.signal:
                sem_names.add(("c", o.chan) if o.chan is not None else ("e", o.eng))
        sems = {}
        for n in sorted(sem_names):
            sems[n] = stack.enter_context(nc.semaphore("s_%s_%s" % n))
        counts = {n: 0 for n in sem_names}
        for o in ops:
            if o.signal:
                n = ("c", o.chan) if o.chan is not None else ("e", o.eng)
                counts[n] += 16 if o.chan is not None else 1
                o.token = (n, counts[n])
        self.max_counts = dict(counts)
        seen = {e: {} for e in ENGS}
        for o in ops:
            need = {}
            for j in deps[o.idx]:
                oj = ops[j]
                if oj.token is None:
                    continue
                n, v = oj.token
                if v > need.get(n, 0):
                    need[n] = v
            sd = seen[o.eng]
            for n, v in need.items():
                if sd.get(n, 0) >= v:
                    continue
                sd[n] = v
                o.waits.append((n, v))
        block = stack.enter_context(nc.Block())
        per_eng = {e: [o for o in ops if o.eng == e] for e in ENGS}

        def emit(engh, lst):
            for o in lst:
                for n, v in o.waits:
                    engh.wait_ge(sems[n], v)
                ins = o.fn(engh)
                if o.signal:
                    n, v = o.token
                    ins.then_inc(sems[n], 16 if o.chan is not None else 1)

        @block.tensor
        def _(e):
            emit(e, per_eng["pe"])

        @block.scalar
        def _(e):
            emit(e, per_eng["act"])

        @block.vector
        def _(e):
            emit(e, per_eng["dve"])

        @block.gpsimd
        def _(e):
            emit(e, per_eng["pool"])

        @block.sync
        def _(e):
            emit(e, per_eng["sp"])
            for n in sorted(sem_names):
                if n[0] == "c":
                    e.wait_ge(sems[n], counts[n])


def col_layout():
    lay = {}
    pos = 0

    def add(name, n):
        nonlocal pos
        lay[name] = pos
        pos += n

    add("flag", 1)
    add("one", 1)
    for i in range(2):
        add("ps%d" % i, 8)
        add("cw%d" % i, 24)
        add("cb%d" % i, 8)
    for i in range(2):
        add("sg%d" % i, 8)
        add("dw%d" % i, 248)
        add("db%d" % i, 8)
        add("ng%d" % i, 8)
        add("nb%d" % i, 8)
    return lay, pos


COL, NCOL = col_layout()
PM_CUR, PM_PREV, PM_SPH, PM_SPL, PM_SPP = 1, 5, 9, 13, 17
NCB = 21


def build_program(n_layers=4, tiles=None):
    TILES = tiles if tiles is not None else globals()["TILES"]
    NTOK = 128 * sum(t[1] for t in TILES)
    nc = bass.Bass("TRN2", target_bir_lowering=False)
    st = ExitStack()
    with st:
        def dram(name, shape, dt, kind):
            return nc.dram_tensor(name, list(shape), dt, kind=kind).ap()

        xin = dram("xin", [NTOK, D], F32, "ExternalInput")
        out = dram("out", [NTOK - HALO * 128, D], F32, "ExternalOutput")
        consts_bf_d = dram("consts_bf", [128, NCB * 128], BF16, "ExternalInput")
        maskT_d = dram("maskT", [128, 128], F32, "ExternalInput")
        colpack_d = dram("colpack", [128, NCOL], F32, "ExternalInput")
        lng_d = dram("ln_g", [4, D], F32, "ExternalInput")
        lnb_d = dram("ln_b", [4, D], F32, "ExternalInput")
        sgb_d = dram("sgu_ln_b", [2, D], F32, "ExternalInput")
        sgub_d = dram("sgu_b", [2, 512], F32, "ExternalInput")
        wxa_d = dram("w_xa", [2, 2, 128, 8 * 512], F32, "ExternalInput")
        wev_d = dram("w_ev", [2, 8, 128, 8 * 640], F32, "ExternalInput")
        wv_d = dram("w_v", [2, 2, 128, 8 * 512], F32, "ExternalInput")
        wod_d = dram("w_od", [2, 8, 128, 8 * 640], F32, "ExternalInput")
        wout_d = dram("w_out", [4, 128, 16 * 1024], F32, "ExternalInput")
        pw_d = dram("pool_w", [2, 128, 4 * 2 * 256], F32, "ExternalInput")
        wsT_d = dram("wsT", [2, 128, 512], F32, "ExternalInput")
        xs_a = dram("xs_a", [NTOK, D], F32, "Internal")
        xs_b = dram("xs_b", [NTOK, D], F32, "Internal")
        wxa_s = dram("wxa_s", [2, 2, 128, 8 * 512], BF16, "Internal")
        wev_s = dram("wev_s", [2, 8, 128, 8 * 640], BF16, "Internal")
        wv_s = dram("wv_s", [2, 2, 128, 8 * 512], BF16, "Internal")
        wod_s = dram("wod_s", [2, 8, 128, 8 * 640], BF16, "Internal")
        wout_s = dram("wout_s", [4, 128, 16 * 1024], BF16, "Internal")

        def sb(name, shape, dt):
            return st.enter_context(nc.sbuf_tensor(name, list(shape), dt))

        cbf = sb("cbf", [128, NCB, 128], BF16)
        ident = cbf[:, 0, :]
        onesD = sb("onesD", [128, 128], BF16)
        maskT = sb("maskT_s", [128, 128], F32)
        colp = sb("colp", [128, NCOL], F32)
        smalls = sb("smalls", [128, 8], F32)
        ones_row = sb("ones_row", [1, 128], F32)
        lng = sb("lng", [128, D], F32)
        lnb = sb("lnb", [128, D], F32)
        wout_bf = sb("wout_bf", [128, 16, 1024], BF16)
        pw_bf = sb("pw_bf", [128, 4, 2, 256], BF16)
        wsT_bf = sb("wsT_bf", [128, 4, 128], BF16)
        Bias = sb("Bias", [128, 8, 128], F32)
        dg = sb("dg", [128, 2, 16, 128], BF16)
        wbuf = sb("wbuf", [128, 2, 8, 640], BF16)
        xb16 = sb("xb16", [128, 4, D], BF16)
        xT = sb("xT", [128, 8, 512], BF16)
        concat = sb("concat", [128, 2, 16, 512], BF16)
        x32 = sb("x32", [128, D], F32)
        rbuf = sb("rbuf", [128, 2, D], F32)
        stt = sb("stt", [128, 4, 12], F32)
        mvt = sb("mvt", [128, 4, 4], F32)
        wt = sb("wt", [128, 248], F32)
        acc = sb("acc", [128, 2, 512], F32)
        UN = 0

        def carve(n):
            nonlocal UN
            o = UN
            UN += n + (n % 2)
            return o

        ev = {}
        ev["xa"] = carve(5 * 1024)
        ev["pT"] = carve(4 * 512)
        ev["sga"] = carve(4 * 512)
        ev["hsb"] = carve(2 * 512)
        ev["sgb"] = carve(2 * 512)
        ev["m"] = carve(2 * 512)
        ev["pst"] = carve(8 * 514)
        ev_end = UN
        UN = 0
        od = {}
        od["vn"] = carve(4 * 1024)
        od["t1"] = carve(2 * 1024)
        od["sgc"] = carve(2 * 512)
        od["t2"] = carve(2 * 512)
        od["th"] = carve(2 * 512)
        od["zst"] = carve(8 * 542)
        od["sgd"] = carve(8 * 512)
        od["czb"] = carve(8 * 512)
        od["sqb"] = carve(2 * 512)
        od["mean"] = carve(1024)
        od["rstd"] = carve(1024)
        od["dd"] = carve(2 * 1024)
        od["s1"] = carve(2 * 512)
        od_end = UN
        U = sb("U", [128, max(ev_end, od_end)], BF16)

        def ub(off, n):
            return U[:, off:off + n]

        def uf(off, n):
            return U[:, off:off + 2 * n].bitcast(F32)

        ps = st.enter_context(nc.psum_tensor("ps", [128, 8 * 512], F32))

        def bank(b):
            return ps[:, b * 512:(b + 1) * 512]

        print("SBUF bytes remaining:", nc.sbuf_bytes_remaining)

        P = Prog(nc)
        state = {"ring": 0, "ring_n": 8, "wu": 0, "castn": 0, "stat": 0, "xb": 0, "rb": 0}

        def nbank():
            b = state["ring"] % state["ring_n"]
            state["ring"] += 1
            return b

        def col(name, idx=0):
            c = COL[name] + idx
            return colp[:, c:c + 1]

        P.dma(lambda e: e.dma_start(out=cbf[:].rearrange("p a b -> p (a b)"), in_=consts_bf_d), writes=["cbf"], chan="k0")
        P.dma(lambda e: e.dma_start(out=maskT[:], in_=maskT_d), writes=["maskT"], chan="k1")
        P.dma(lambda e: e.dma_start(out=colp[:], in_=colpack_d), writes=["colp"], chan="k2")
        P.op("pool", lambda e: e.memset(onesD[:], 1.0 / 1024.0), writes=["onesD"])
        P.op("pool", lambda e: e.memset(smalls[:], -0.5), writes=["smalls"])
        P.op("pool", lambda e: e.memset(ones_row[:], 1.0), writes=["ones_row"])

        def cast_dma(src, dst, rowlen, key):
            n = state["castn"]
            state["castn"] += 1
            P.dma(lambda e: e.dma_start(out=dst, in_=src), writes=[key], chan="cast%d" % (n % 8), eng="pool")

        def cast_list(l):
            i = l // 2
            lst = []

            def add(src, dst, ncols, key):
                for c0 in range(0, ncols, 1024):
                    lst.append((src[:, c0:c0 + 1024], dst[:, c0:c0 + 1024], 1024, key + (c0 // 1024,)))

            if l % 2 == 0:
                for u in range(2):
                    add(wxa_d[i, u], wxa_s[i, u], 4096, ("wsc", l, "a", u))
                for u in range(8):
                    add(wev_d[i, u], wev_s[i, u], 5120, ("wsc", l, "c", u))
            else:
                for u in range(2):
                    add(wv_d[i, u], wv_s[i, u], 4096, ("wsc", l, "a", u))
                for u in range(8):
                    add(wod_d[i, u], wod_s[i, u], 5120, ("wsc", l, "c", u))
            for h in range(4):
                add(wout_d[l][:, h * 4096:(h + 1) * 4096], wout_s[l][:, h * 4096:(h + 1) * 4096], 4096, ("wosc", l, h))
            return lst

        def emit_casts(l):
            for a in cast_list(l):
                cast_dma(*a)

        def wunit_src(l, kind, u):
            i = l // 2
            if l % 2 == 0:
                return (wxa_s[i, u] if kind == "a" else wev_s[i, u])
            return (wv_s[i, u] if kind == "a" else wod_s[i, u])

        def load_wunit(l, kind, u, cols=None):
            k = state["wu"]
            state["wu"] += 1
            b = k % 2
            ncols = 512 if kind == "a" else 640
            src = wunit_src(l, kind, u).rearrange("p (k n) -> p k n", k=8)
            dst = wbuf[:, b, :, 0:ncols]
            if cols is not None:
                src = src[:, :, cols[0]:cols[1]]
                dst = wbuf[:, b, :, cols[0]:cols[1]]
            npieces = 4 if kind == "a" else 5
            P.dma(lambda e: e.dma_start(out=dst, in_=src), reads=[("wsc", l, kind, u, pc) for pc in range(npieces)], writes=[("wb", b)], chan="w%d" % b)
            return b

        def layer_in(l):
            return [xin, xs_a, xs_b, xs_a][l], ["in", "a", "b", "a"][l]

        def layer_out(l):
            if l == n_layers - 1:
                return out, "out"
            return [xs_a, xs_b, xs_a, None][l], ["a", "b", "a", None][l]

        def T_load(l, tile):
            s0, ns = tile
            src, skey = layer_in(l)
            for s in range(ns):
                S = s0 + s
                P.dma(lambda e, S=S, s=s: e.dma_start(out=xb16[:, s, :], in_=src[S * 128:(S + 1) * 128, :]),
                      reads=[("XD", skey, S)], writes=[("xb16", s)], chan="xin%d" % s, eng="pool")

        def T_phase(l, tile):
            s0, ns = tile
            for s in range(ns):
                b = nbank()
                pT = bank(b).bitcast(BF16)
                for j in range(8):
                    P.op("pe", lambda e, j=j, s=s, pT=pT: e.transpose(out=pT[:, j * 128:(j + 1) * 128], in_=xb16[:, s, j * 128:(j + 1) * 128], identity=ident),
                         reads=[("xb16", s), "cbf"], writes=[("ps", b)])
                P.op("dve", lambda e, s=s, pT=pT: e.tensor_copy(out=xT[:, :, s * 128:(s + 1) * 128], in_=pT.rearrange("p (j t) -> p j t", j=8)),
                     reads=[("ps", b)], writes=[("xT", s)])

        def proj_fm(b, wb_i, c0, N, ns):
            for kc in range(8):
                P.op("pe", lambda e, kc=kc: e.matmul(bank(b)[:, 0:N], lhsT=wbuf[:, wb_i, kc, c0:c0 + 128], rhs=xT[:, kc, 0:N], start=(kc == 0), stop=(kc == 7)),
                     reads=[("wb", wb_i)] + [("xT", s) for s in range(ns)], writes=[("ps", b)])

        def proj_tm(b, wb_i, s):
            for kc in range(8):
                P.op("pe", lambda e, kc=kc: e.matmul(bank(b)[:, 0:512], lhsT=xT[:, kc, s * 128:(s + 1) * 128], rhs=wbuf[:, wb_i, kc, 0:512], start=(kc == 0), stop=(kc == 7)),
                     reads=[("wb", wb_i), ("xT", s)], writes=[("ps", b)])

        def ln_chain(slot, srcs, keys):
            for h, a in enumerate(srcs):
                P.op("dve", lambda e, h=h, a=a: e.bn_stats(out=stt[:, slot, h * 6:(h + 1) * 6], in_=a),
                     reads=keys[h], writes=[("stt", slot, h)])
            P.op("dve", lambda e: e.bn_aggr(out=mvt[:, slot, 0:2], in_=stt[:, slot, :]),
                 reads=[("stt", slot, 0), ("stt", slot, 1)], writes=[("mv", slot)])
            P.op("dve", lambda e: e.tensor_scalar(out=mvt[:, slot, 1:2], in0=mvt[:, slot, 1:2], scalar1=EPS, scalar2=None, op0=ALU.add),
                 reads=[("mv", slot)], writes=[("mv", slot)])
            P.op("pool", lambda e: e.tensor_tensor(out=mvt[:, slot, 2:3], in0=mvt[:, slot, 1:2], in1=smalls[:, 0:1], op=ALU.pow),
                 reads=[("mv", slot), "smalls"], writes=[("rs", slot)])
            P.op("dve", lambda e: e.tensor_scalar(out=mvt[:, slot, 3:4], in0=mvt[:, slot, 0:1], scalar1=mvt[:, slot, 2:3], scalar2=-1.0, op0=ALU.mult, op1=ALU.mult),
                 reads=[("mv", slot), ("rs", slot)], writes=[("nm", slot)])

        def C_items(l, tile, ti):
            s0, ns = tile
            src, skey = layer_in(l)
            dst, dkey = layer_out(l)
            last = (l == n_layers - 1)
            cp = ti % 2

            def c_sub(s):
                S = s0 + s
                slot = state["stat"] % 4
                state["stat"] += 1
                rb = state["rb"] % 2
                state["rb"] += 1
                P.dma(lambda e: e.dma_start(out=x32[:], in_=src[S * 128:(S + 1) * 128, :]),
                      reads=[("XD", skey, S)], writes=["x32"], chan="x32")
                bs = [nbank(), nbank()]
                for h in range(2):
                    for kq in range(16):
                        P.op("pe", lambda e, h=h, kq=kq: e.matmul(bank(bs[h])[:, 0:512], lhsT=concat[:, cp, kq, s * 128:(s + 1) * 128], rhs=wout_bf[:, kq, h * 512:(h + 1) * 512], start=(kq == 0), stop=(kq == 15)),
                             reads=[("cc", cp, kq), "wout"], writes=[("ps", bs[h])])
                for h in range(2):
                    P.op("dve", lambda e, h=h: e.scalar_tensor_tensor(out=rbuf[:, rb, h * 512:(h + 1) * 512], in0=x32[:, h * 512:(h + 1) * 512], scalar=ALPHA, in1=bank(bs[h])[:, 0:512], op0=ALU.mult, op1=ALU.add),
                         reads=["x32", ("ps", bs[h])], writes=[("r", rb, h)])
                ln_chain(slot, [rbuf[:, rb, 0:512], rbuf[:, rb, 512:1024]], [[("r", rb, 0)], [("r", rb, 1)]])
                P.op("act", lambda e: e.activation(out=rbuf[:, rb, :], in_=rbuf[:, rb, :], func=AF.Identity, scale=mvt[:, slot, 2:3], bias=mvt[:, slot, 3:4]),
                     reads=[("r", rb, 0), ("r", rb, 1), ("rs", slot), ("nm", slot)], writes=[("r", rb, 0), ("r", rb, 1)])
                P.op("pool", lambda e: e.tensor_tensor(out=rbuf[:, rb, :], in0=rbuf[:, rb, :], in1=lng[:], op=ALU.mult),
                     reads=[("r", rb, 0), ("r", rb, 1), "lng"], writes=[("r", rb, 0), ("r", rb, 1)])
                P.op("pool", lambda e: e.tensor_tensor(out=rbuf[:, rb, :], in0=rbuf[:, rb, :], in1=lnb[:], op=ALU.add),
                     reads=[("r", rb, 0), ("r", rb, 1), "lnb"], writes=[("r", rb, 0), ("r", rb, 1)])
                row = (S - HALO) if last else S
                P.dma(lambda e: e.dma_start(out=dst[row * 128:(row + 1) * 128, :], in_=rbuf[:, rb, :]),
                      reads=[("r", rb, 0), ("r", rb, 1)], writes=[("XD", dkey, S)], chan="st%d" % rb, eng="pool")

            items = []
            for s in range(ns):
                if last and (s0 + s) < HALO:
                    continue
                items.append(lambda s=s: c_sub(s))
            return items

        def run_staged(stages, n, hooks, early=(), tick=None):
            ns_ = len(stages)
            hooks = list(hooks)
            nh = len(hooks)
            pts = {}
            if nh:
                for h in range(nh):
                    pts.setdefault(0 if HOOKS_AT0 else min(n - 1, 1 + (2 * h if nh <= 4 else h)), []).append(hooks[h])
            for it in range(n + ns_ - 1):
                for j, stg in enumerate(stages):
                    u = it - j
                    if 0 <= u < n:
                        stg(u)
                if it == 0:
                    for h in early:
                        h()
                if tick is not None:
                    tick()
                for h in pts.get(it, []):
                    h()

        def even_layer(l):
            i = l // 2
            xa = ub(ev["xa"], 5 * 1024).rearrange("p (s c) -> p s c", s=5)
            pT_ = ub(ev["pT"], 4 * 512).rearrange("p (g k n) -> p g k n", g=2, k=2)
            sga = ub(ev["sga"], 4 * 512).rearrange("p (g k n) -> p g k n", g=2, k=2)
            hsb = ub(ev["hsb"], 2 * 512).rearrange("p (a n) -> p a n", a=2)
            sgb = ub(ev["sgb"], 2 * 512).rearrange("p (a n) -> p a n", a=2)
            mm = ub(ev["m"], 2 * 512).rearrange("p (a n) -> p a n", a=2)
            pst = ub(ev["pst"], 8 * 514).rearrange("p (q n) -> p q n", q=8)
            dg3 = dg[:].rearrange("p a b c -> p (a b c)")[:, 0:8 * 3 * 128].rearrange("p (q j c) -> p q j c", q=8, j=3)
            P.dma(lambda e: e.dma_start(out=pw_bf[:].rearrange("p g k n -> p (g k n)"), in_=pw_d[i]), writes=["pw"], chan="pwc", eng="pool")
            for q in range(8):
                for j in range(3):
                    P.op("pool", lambda e, q=q, j=j: e.tensor_scalar(out=dg3[:, q, j, :], in0=ident, scalar1=col("cw%d" % i, j * 8 + q), scalar2=1.0, op0=ALU.mult, op1=ALU.mult),
                         reads=["cbf", "colp"], writes=["dg3"])
            P.op("pool", lambda e: e.memset(xa[:, 0, :], 0.0), writes=[("xa", 0, 0), ("xa", 0, 1)])
            P.op("pool", lambda e: e.memset(pst[:, :, 0:2], 0.0), writes=[("pst", q) for q in range(8)])
            state["ring_n"] = 8
            tstate = {}

            def load_next_c():
                if tstate["q"] < 8:
                    w = load_wunit(l, "c", tstate["q"])
                    tstate["q"] += 1
                    return w
                return None

            def pre(tile):
                s0, ns = tile
                wa = [load_wunit(l, "a", 0), load_wunit(l, "a", 1)]
                tstate["q"] = 0
                tstate["wq"] = {}
                for gg in range(2):
                    for s in range(ns):
                        b = nbank()
                        proj_tm(b, wa[gg], s)
                        P.op("act", lambda e, b=b, s=s, gg=gg: e.activation(out=xa[:, s + 1, gg * 512:(gg + 1) * 512], in_=bank(b)[:, 0:512], func=AF.Copy),
                             reads=[("ps", b)], writes=[("xa", s + 1, gg)])
                    tstate["wq"][gg] = load_next_c()

            def main(tile, ti, hooks, early=()):
                s0, ns = tile
                N = ns * 128
                cp = ti % 2
                wq = tstate["wq"]

                def stage1(q):
                    wb_i = wq[q]
                    g = q // 2
                    gp, qp = g % 2, q % 2
                    gg = q // 4
                    b_ga = nbank()
                    proj_fm(b_ga, wb_i, 0, N, ns)
                    P.op("act", lambda e: e.activation(out=sga[:, gp, qp, 0:N], in_=bank(b_ga)[:, 0:N], func=AF.Silu),
                         reads=[("ps", b_ga)], writes=[("sga", gp, qp)])
                    b_h = nbank()
                    proj_fm(b_h, wb_i, 128, N, ns)
                    P.op("act", lambda e: e.activation(out=hsb[:, qp, 0:N], in_=bank(b_h)[:, 0:N], func=AF.Copy),
                         reads=[("ps", b_h)], writes=[("hsb", qp)])
                    b_cg = nbank()
                    proj_fm(b_cg, wb_i, 384, N, ns)
                    P.op("dve", lambda e: e.tensor_tensor(out=pst[:, q, 2:2 + N], in0=bank(b_cg)[:, 0:N], in1=hsb[:, qp, 0:N], op=ALU.mult),
                         reads=[("ps", b_cg), ("hsb", qp)], writes=[("pst", q)])
                    b_gb = nbank()
                    proj_fm(b_gb, wb_i, 512, N, ns)
                    P.op("act", lambda e: e.activation(out=sgb[:, qp, 0:N], in_=bank(b_gb)[:, 0:N], func=AF.Silu),
                         reads=[("ps", b_gb)], writes=[("sgb", qp)])
                    b_bg = nbank()
                    proj_fm(b_bg, wb_i, 256, N, ns)
                    P.op("dve", lambda e: e.tensor_tensor(out=mm[:, qp, 0:N], in0=bank(b_bg)[:, 0:N], in1=sgb[:, qp, 0:N], op=ALU.mult),
                         reads=[("ps", b_bg), ("sgb", qp)], writes=[("m", qp)])
                    if q + 2 < 8:
                        wq[q + 2] = load_next_c()
                    b_pl = nbank()
                    for s in range(ns):
                        S = s0 + s
                        if S == HALO:
                            mats = [(s + 1, PM_SPH + g), (s + 1, PM_SPL + g), (s, PM_SPP + g)]
                        else:
                            mats = [(s + 1, PM_CUR + g), (s, PM_PREV + g)]
                        for idx, (slot, pm) in enumerate(mats):
                            P.op("pe", lambda e, s=s, slot=slot, pm=pm, idx=idx, nm=len(mats): e.matmul(bank(b_pl)[:, s * 128:(s + 1) * 128], lhsT=xa[:, slot, q * 128:(q + 1) * 128], rhs=cbf[:, pm, :], start=(idx == 0), stop=(idx == nm - 1)),
                                 reads=[("xa", slot, gg), "cbf"], writes=[("ps", b_pl)])
                    P.op("act", lambda e: e.activation(out=pT_[:, gp, qp, 0:N], in_=bank(b_pl)[:, 0:N], func=AF.Copy),
                         reads=[("ps", b_pl)], writes=[("pT", gp, qp)])

                def stage2(q):
                    g = q // 2
                    gp, qp = g % 2, q % 2
                    b_cv = nbank()
                    for j in range(3):
                        P.op("pe", lambda e, j=j: e.matmul(bank(b_cv)[:, 0:N], lhsT=dg3[:, q, j, :], rhs=pst[:, q, j:j + N], start=(j == 0), stop=(j == 2)),
                             reads=["dg3", ("pst", q)], writes=[("ps", b_cv)])
                    P.op("dve", lambda e: e.scalar_tensor_tensor(out=concat[:, cp, 8 + q, 0:N], in0=bank(b_cv)[:, 0:N], scalar=col("cb%d" % i, q), in1=mm[:, qp, 0:N], op0=ALU.add, op1=ALU.mult),
                         reads=[("ps", b_cv), ("m", qp), "colp"], writes=[("cc", cp, 8 + q)])
                    fl = col("flag") if ti == 0 else col("one")
                    P.op("pool", lambda e: e.tensor_scalar(out=pst[:, q, 0:2], in0=pst[:, q, N:N + 2], scalar1=fl, scalar2=1.0, op0=ALU.mult, op1=ALU.mult),
                         reads=[("pst", q), "colp"], writes=[("pst", q)])
                    if q % 2 == 1:
                        for qo in (q - 1, q):
                            b_pw = nbank()
                            for k2 in range(2):
                                P.op("pe", lambda e, k2=k2, qo=qo, b_pw=b_pw: e.matmul(bank(b_pw)[:, 0:N], lhsT=pw_bf[:, g, k2, (qo % 2) * 128:(qo % 2 + 1) * 128], rhs=pT_[:, gp, k2, 0:N], start=(k2 == 0), stop=(k2 == 1)),
                                     reads=["pw", ("pT", gp, 0), ("pT", gp, 1)], writes=[("ps", b_pw)])
                            P.op("dve", lambda e, qo=qo, b_pw=b_pw: e.scalar_tensor_tensor(out=concat[:, cp, qo, 0:N], in0=bank(b_pw)[:, 0:N], scalar=col("ps%d" % i, qo), in1=sga[:, gp, qo % 2, 0:N], op0=ALU.mult, op1=ALU.mult),
                                 reads=[("ps", b_pw), ("sga", gp, qo % 2), "colp"], writes=[("cc", cp, qo)])

                run_staged([stage1, stage2], 8, hooks, early, tstate.get("tick"))
                P.op("pool", lambda e: e.tensor_copy(out=xa[:, 0, :], in_=xa[:, ns, :]),
                     reads=[("xa", ns, 0), ("xa", ns, 1)], writes=[("xa", 0, 0), ("xa", 0, 1)])

            def transition(tile, nxt):
                if nxt is not None:
                    pre(nxt)

            main.tstate = tstate
            return pre, main, transition

        def odd_layer(l):
            i = l // 2
            vn = ub(od["vn"], 4 * 1024).rearrange("p (s c) -> p s c", s=4)
            t1 = uf(od["t1"], 2 * 512).rearrange("p (a n) -> p a n", a=2)
            sgc = ub(od["sgc"], 2 * 512).rearrange("p (a n) -> p a n", a=2)
            t2 = ub(od["t2"], 2 * 512).rearrange("p (a n) -> p a n", a=2)
            th = ub(od["th"], 2 * 512).rearrange("p (a n) -> p a n", a=2)
            zst = ub(od["zst"], 8 * 542).rearrange("p (q n) -> p q n", q=8)
            sgd = ub(od["sgd"], 8 * 512).rearrange("p (q n) -> p q n", q=8)
            czb = ub(od["czb"], 8 * 512).rearrange("p (q n) -> p q n", q=8)
            sqb = ub(od["sqb"], 2 * 512).rearrange("p (a n) -> p a n", a=2)
            mean = uf(od["mean"], 512)
            rstd = uf(od["rstd"], 512)
            dd = uf(od["dd"], 2 * 512).rearrange("p (a n) -> p a n", a=2)
            s1 = ub(od["s1"], 2 * 512).rearrange("p (a n) -> p a n", a=2)
            MEANB, EX2B = 6, 7
            state["ring_n"] = 6
            wsf = rbuf[:, 0, 0:512]
            bsg = rbuf[:, 1, :]
            sgr = x32[0:1, 0:512]
            P.dma(lambda e: e.dma_start(out=wsf, in_=wsT_d[i]), writes=[("r", 0, 0)], chan="k0")
            P.dma(lambda e: e.dma_start(out=bsg, in_=sgb_d[i:i + 1, :].partition_broadcast(128)), writes=[("r", 1, 0), ("r", 1, 1)], chan="k1")
            P.dma(lambda e: e.dma_start(out=sgr, in_=sgub_d[i:i + 1, :]), writes=["x32"], chan="k2")
            for hd in range(4):
                P.op("dve", lambda e, hd=hd: e.tensor_tensor(out=wsf[:, hd * 128:(hd + 1) * 128], in0=wsf[:, hd * 128:(hd + 1) * 128], in1=maskT[:], op=ALU.mult),
                     reads=[("r", 0, 0), "maskT"], writes=[("r", 0, 0)])
            P.op("dve", lambda e: e.tensor_copy(out=wsT_bf[:].rearrange("p h n -> p (h n)"), in_=wsf), reads=[("r", 0, 0)], writes=["wsT"])
            for q in range(8):
                hd = q // 2
                b = nbank()
                P.op("pe", lambda e, q=q, hd=hd, b=b: e.matmul(bank(b)[:, 0:128], lhsT=bsg[:, q * 128:(q + 1) * 128], rhs=wsf[:, hd * 128:(hd + 1) * 128], start=True, stop=False),
                     reads=[("r", 1, 0), ("r", 1, 1), ("r", 0, 0)], writes=[("ps", b)])
                P.op("pe", lambda e, hd=hd, b=b: e.matmul(bank(b)[:, 0:128], lhsT=ones_row[0:1, :], rhs=sgr[0:1, hd * 128:(hd + 1) * 128], start=False, stop=True),
                     reads=["ones_row", "x32"], writes=[("ps", b)])
                P.op("act", lambda e, q=q, b=b: e.activation(out=Bias[:, q, :], in_=bank(b)[:, 0:128], func=AF.Copy),
                     reads=[("ps", b)], writes=["Bias"])
            P.op("pool", lambda e: e.memset(zst[:, :, 0:30], 0.0), writes=[("zst", q) for q in range(8)])
            dwc = COL["dw%d" % i]
            P.op("dve", lambda e: e.tensor_scalar(out=wt[:], in0=colp[:, dwc:dwc + 248], scalar1=0.5, scalar2=None, op0=ALU.mult),
                 reads=["colp"], writes=["wt"])

            def build_diag(q, hs):
                j0 = KDVE if hs == 0 else 16
                nj = (16 - KDVE) if hs == 0 else 15
                P.op(DIAG_ENG, lambda e: e.tensor_tensor(out=dg[:, hs, 0:nj, :], in0=ident.unsqueeze(1).to_broadcast([128, nj, 128]),
                                                      in1=wt[:, q * 31 + j0:q * 31 + j0 + nj].unsqueeze(2).to_broadcast([128, nj, 128]), op=ALU.mult),
                     reads=["cbf", "wt"], writes=[("dg", hs)])

            tstate = {}

            def load_next_c():
                if tstate["q"] < 8:
                    w = load_wunit(l, "c", tstate["q"])
                    tstate["q"] += 1
                    return w
                return None

            def pre_items(tile):
                s0, ns = tile

                def begin():
                    tstate["wa"] = [load_wunit(l, "a", 0), load_wunit(l, "a", 1)]
                    tstate["q"] = 0

                def c1_sub(s):
                    wa = tstate["wa"]
                    slot = state["stat"] % 4
                    state["stat"] += 1
                    bs = [nbank(), nbank()]
                    for gg in range(2):
                        proj_tm(bs[gg], wa[gg], s)
                    ln_chain(slot, [bank(bs[0])[:, 0:512], bank(bs[1])[:, 0:512]], [[("ps", bs[0])], [("ps", bs[1])]])
                    for gg in range(2):
                        P.op("act", lambda e, gg=gg: e.activation(out=vn[:, s, gg * 512:(gg + 1) * 512], in_=bank(bs[gg])[:, 0:512], func=AF.Identity, scale=mvt[:, slot, 2:3], bias=mvt[:, slot, 3:4]),
                             reads=[("ps", bs[gg]), ("rs", slot), ("nm", slot)], writes=[("vn", s, gg)])

                return [begin] + [(lambda s=s: c1_sub(s)) for s in range(ns)]

            def pre(tile):
                for f in pre_items(tile):
                    f()

            def main(tile, ti, hooks, early=()):
                s0, ns = tile
                N = ns * 128
                cp = ti % 2
                wq = {0: load_next_c()}
                wq[1] = load_next_c()

                def stage1(q):
                    wb_i = wq[q]
                    hd = q // 2
                    qp = q % 2
                    gg = q // 4
                    b_gc = nbank()
                    proj_fm(b_gc, wb_i, 0, N, ns)
                    P.op("act", lambda e: e.activation(out=sgc[:, qp, 0:N], in_=bank(b_gc)[:, 0:N], func=AF.Silu),
                         reads=[("ps", b_gc)], writes=[("sgc", qp)])
                    b_bl = nbank()
                    proj_fm(b_bl, wb_i, 384, N, ns)
                    P.op("act", lambda e: e.activation(out=th[:, qp, 0:N], in_=bank(b_bl)[:, 0:N], func=AF.Tanh, scale=0.5),
                         reads=[("ps", b_bl)], writes=[("th", qp)])
                    b_a = nbank()
                    proj_fm(b_a, wb_i, 256, N, ns)
                    P.op("dve", lambda e: e.scalar_tensor_tensor(out=zst[:, q, 30:30 + N], in0=th[:, qp, 0:N], scalar=1.0, in1=bank(b_a)[:, 0:N], op0=ALU.add, op1=ALU.mult),
                         reads=[("th", qp), ("ps", b_a)], writes=[("zst", q)])
                    b_sg = nbank()
                    for s in range(ns):
                        P.op("pe", lambda e, s=s: e.matmul(bank(b_sg)[:, s * 128:(s + 1) * 128], lhsT=vn[:, s, q * 128:(q + 1) * 128], rhs=wsT_bf[:, hd, :], start=True, stop=True),
                             reads=[("vn", s, gg), "wsT"], writes=[("ps", b_sg)])
                    P.op("dve", lambda e: e.scalar_tensor_tensor(out=t1[:, qp, 0:N].rearrange("p (s n) -> p s n", s=ns), in0=bank(b_sg)[:, 0:N].rearrange("p (s n) -> p s n", s=ns), scalar=col("sg%d" % i, q), in1=Bias[:, q:q + 1, :].to_broadcast([128, ns, 128]), op0=ALU.mult, op1=ALU.add),
                         reads=[("ps", b_sg), "Bias", "colp"], writes=[("t1", qp)])
                    b_u = nbank()
                    proj_fm(b_u, wb_i, 128, N, ns)
                    P.op("dve", lambda e: e.tensor_tensor(out=t2[:, qp, 0:N], in0=bank(b_u)[:, 0:N], in1=t1[:, qp, 0:N], op=ALU.mult),
                         reads=[("ps", b_u), ("t1", qp)], writes=[("t2", qp)])
                    P.op("dve", lambda e: e.tensor_tensor(out=concat[:, cp, q, 0:N], in0=t2[:, qp, 0:N], in1=sgc[:, qp, 0:N], op=ALU.mult),
                         reads=[("t2", qp), ("sgc", qp)], writes=[("cc", cp, q)])
                    b_gd = nbank()
                    proj_fm(b_gd, wb_i, 512, N, ns)
                    P.op("act", lambda e: e.activation(out=sgd[:, q, 0:N], in_=bank(b_gd)[:, 0:N], func=AF.Silu),
                         reads=[("ps", b_gd)], writes=[("sgd", q)])
                    if q + 2 < 8:
                        wq[q + 2] = load_next_c()
                    if q == 0:
                        build_diag(0, 0)
                        build_diag(0, 1)

                def stage2(q):
                    qp = q % 2
                    for j in range(KDVE):
                        wc = wt[:, q * 31 + j:q * 31 + j + 1]
                        if j == 0:
                            P.op("dve", lambda e, j=j, wc=wc: e.tensor_scalar(out=acc[:, qp, 0:N], in0=zst[:, q, j:j + N], scalar1=wc, scalar2=None, op0=ALU.mult),
                                 reads=[("zst", q), "wt"], writes=[("acc", qp)])
                        else:
                            P.op("dve", lambda e, j=j, wc=wc: e.scalar_tensor_tensor(out=acc[:, qp, 0:N], in0=zst[:, q, j:j + N], scalar=wc, in1=acc[:, qp, 0:N], op0=ALU.mult, op1=ALU.add),
                                 reads=[("zst", q), "wt", ("acc", qp)], writes=[("acc", qp)])
                    b_cv = nbank()
                    for hs in range(2):
                        j0 = KDVE if hs == 0 else 16
                        nj = (16 - KDVE) if hs == 0 else 15
                        for jj in range(nj):
                            j = j0 + jj
                            P.op("pe", lambda e, jj=jj, j=j, hs=hs: e.matmul(bank(b_cv)[:, 0:N], lhsT=dg[:, hs, jj, :], rhs=zst[:, q, j:j + N], start=(j == KDVE), stop=(j == 30)),
                                 reads=[("dg", hs), ("zst", q)], writes=[("ps", b_cv)])
                        if q + 1 < 8:
                            build_diag(q + 1, hs)
                    if KDVE > 0:
                        P.op("dve", lambda e: e.scalar_tensor_tensor(out=czb[:, q, 0:N], in0=bank(b_cv)[:, 0:N], scalar=col("db%d" % i, q), in1=acc[:, qp, 0:N], op0=ALU.add, op1=ALU.add),
                             reads=[("ps", b_cv), "colp", ("acc", qp)], writes=[("czb", q)])
                        P.op("act", lambda e: e.activation(out=sqb[:, qp, 0:N], in_=czb[:, q, 0:N], func=AF.Square),
                             reads=[("czb", q)], writes=[("sqb", qp)])
                    else:
                        P.op("act", lambda e: e.activation(out=czb[:, q, 0:N], in_=bank(b_cv)[:, 0:N], func=AF.Identity, bias=col("db%d" % i, q)),
                             reads=[("ps", b_cv), "colp"], writes=[("czb", q)])
                        P.op("act", lambda e: e.activation(out=sqb[:, qp, 0:N], in_=bank(b_cv)[:, 0:N], func=AF.Square, bias=col("db%d" % i, q)),
                             reads=[("ps", b_cv), "colp"], writes=[("sqb", qp)])
                    fl = col("flag") if ti == 0 else col("one")
                    P.op("pool", lambda e: e.tensor_scalar(out=zst[:, q, 0:30], in0=zst[:, q, N:N + 30], scalar1=fl, scalar2=1.0, op0=ALU.mult, op1=ALU.mult),
                         reads=[("zst", q), "colp"], writes=[("zst", q)])

                def stage3(q):
                    qp = q % 2
                    P.op("pe", lambda e: e.matmul(bank(MEANB)[:, 0:N], lhsT=onesD[:], rhs=czb[:, q, 0:N], start=(q == 0), stop=(q == 7)),
                         reads=["onesD", ("czb", q)], writes=[("ps", MEANB)])
                    P.op("pe", lambda e: e.matmul(bank(EX2B)[:, 0:N], lhsT=onesD[:], rhs=sqb[:, qp, 0:N], start=(q == 0), stop=(q == 7)),
                         reads=["onesD", ("sqb", qp)], writes=[("ps", EX2B)])

                run_staged([stage1, stage2, stage3], 8, hooks, early, tstate.get("tick"))

            def main_lite(tile, ti, early=()):
                s0, ns = tile
                N = ns * 128
                sl = ns - 1
                tstate["q"] = 0

                def load_part():
                    if tstate["q"] < 8:
                        w = load_wunit(l, "c", tstate["q"], cols=(256, 512))
                        tstate["q"] += 1
                        return w
                    return None

                wq = {0: load_part()}
                wq[1] = load_part()
                for h in early:
                    h()
                for q in range(8):
                    def unit(q=q):
                        wb_i = wq[q]
                        qp = q % 2
                        b_bl = nbank()
                        for kc in range(8):
                            P.op("pe", lambda e, kc=kc: e.matmul(bank(b_bl)[:, 0:128], lhsT=wbuf[:, wb_i, kc, 384:512], rhs=xT[:, kc, sl * 128:(sl + 1) * 128], start=(kc == 0), stop=(kc == 7)),
                                 reads=[("wb", wb_i), ("xT", sl)], writes=[("ps", b_bl)])
                        P.op("act", lambda e: e.activation(out=th[:, qp, 0:128], in_=bank(b_bl)[:, 0:128], func=AF.Tanh, scale=0.5),
                             reads=[("ps", b_bl)], writes=[("th", qp)])
                        b_a = nbank()
                        for kc in range(8):
                            P.op("pe", lambda e, kc=kc: e.matmul(bank(b_a)[:, 0:128], lhsT=wbuf[:, wb_i, kc, 256:384], rhs=xT[:, kc, sl * 128:(sl + 1) * 128], start=(kc == 0), stop=(kc == 7)),
                                 reads=[("wb", wb_i), ("xT", sl)], writes=[("ps", b_a)])
                        P.op("dve", lambda e: e.scalar_tensor_tensor(out=zst[:, q, 30 + sl * 128:30 + N], in0=th[:, qp, 0:128], scalar=1.0, in1=bank(b_a)[:, 0:128], op0=ALU.add, op1=ALU.mult),
                             reads=[("th", qp), ("ps", b_a)], writes=[("zst", q)])
                        fl = col("flag") if ti == 0 else col("one")
                        P.op("pool", lambda e: e.tensor_scalar(out=zst[:, q, 0:30], in0=zst[:, q, N:N + 30], scalar1=fl, scalar2=1.0, op0=ALU.mult, op1=ALU.mult),
                             reads=[("zst", q), "colp"], writes=[("zst", q)])
                        if q + 2 < 8:
                            wq[q + 2] = load_part()
                    unit()

            def s3d_items(tile, ti):
                s0, ns = tile
                N = ns * 128
                cp = ti % 2

                def fin():
                    P.op("act", lambda e: e.activation(out=mean[:, 0:N], in_=bank(MEANB)[:, 0:N], func=AF.Copy),
                         reads=[("ps", MEANB)], writes=["mean"])
                    m2 = dd[:, 0, 0:N]
                    P.op("dve", lambda e: e.tensor_tensor(out=m2, in0=mean[:, 0:N], in1=mean[:, 0:N], op=ALU.mult),
                         reads=["mean"], writes=[("dd", 0)])
                    P.op("dve", lambda e: e.scalar_tensor_tensor(out=rstd[:, 0:N], in0=bank(EX2B)[:, 0:N], scalar=EPS, in1=m2, op0=ALU.add, op1=ALU.subtract),
                         reads=[("ps", EX2B), ("dd", 0)], writes=["rstd"])
                    P.op("act", lambda e: e.activation(out=rstd[:, 0:N], in_=rstd[:, 0:N], func=AF.Sqrt),
                         reads=["rstd"], writes=["rstd"])
                    P.op("dve", lambda e: e.reciprocal(out=rstd[:, 0:N], in_=rstd[:, 0:N]),
                         reads=["rstd"], writes=["rstd"])

                def s3d_a(q):
                    qp = q % 2
                    P.op("dve", lambda e: e.tensor_tensor(out=dd[:, qp, 0:N], in0=czb[:, q, 0:N], in1=mean[:, 0:N], op=ALU.subtract),
                         reads=[("czb", q), "mean"], writes=[("dd", qp)])
                    P.op("dve", lambda e: e.tensor_tensor(out=dd[:, qp, 0:N], in0=dd[:, qp, 0:N], in1=rstd[:, 0:N], op=ALU.mult),
                         reads=[("dd", qp), "rstd"], writes=[("dd", qp)])
                    P.op("act", lambda e: e.activation(out=s1[:, qp, 0:N], in_=dd[:, qp, 0:N], func=AF.Silu, scale=col("ng%d" % i, q), bias=col("nb%d" % i, q)),
                         reads=[("dd", qp), "colp"], writes=[("s1", qp)])

                def s3d_b(q):
                    qp = q % 2
                    P.op("dve", lambda e: e.tensor_tensor(out=concat[:, cp, 8 + q, 0:N], in0=s1[:, qp, 0:N], in1=sgd[:, q, 0:N], op=ALU.mult),
                         reads=[("s1", qp), ("sgd", q)], writes=[("cc", cp, 8 + q)])

                def s3d_step(k):
                    if k < 8:
                        s3d_a(k)
                    if k >= 1:
                        s3d_b(k - 1)

                return [fin] + [(lambda k=k: s3d_step(k)) for k in range(9)]

            def transition(tile, nxt, ti, lite=False):
                bq = pre_items(nxt) if nxt is not None else []
                if lite:
                    while bq:
                        bq.pop(0)()
                    return
                a = s3d_items(tile, ti)
                a.pop(0)()
                if bq:
                    bq.pop(0)()
                if bq:
                    bq.pop(0)()
                for idx in range(9):
                    a.pop(0)()
                    if idx % 2 == 1 and bq:
                        bq.pop(0)()
                while bq:
                    bq.pop(0)()

            main.lite = main_lite
            main.tstate = tstate
            return pre, main, transition

        first_casts = cast_list(0)
        T_load(0, TILES[0])
        for _ in range(4 + 4 + 5 * 4):
            cast_dma(*first_casts.pop(0))
        for l in range(n_layers):
            if l > 0:
                P.op("pool", lambda e: e.memset(smalls[:, 4:5], 0.0), reads=[], writes=["LAYER"])
            lstart = len(P.ops)
            is_even = (l % 2 == 0)
            pre, main, transition = even_layer(l) if is_even else odd_layer(l)
            pend = cast_list(l + 1) if l + 1 < n_layers else []
            if l == 0:
                while first_casts:
                    cast_dma(*first_casts.pop(0))
            else:
                T_load(l, TILES[0])
            P.dma(lambda e, l=l: e.dma_start(out=lng[:], in_=lng_d[l:l + 1, :].partition_broadcast(128)), writes=["lng"], chan="k0")
            P.dma(lambda e, l=l: e.dma_start(out=lnb[:], in_=lnb_d[l:l + 1, :].partition_broadcast(128)), writes=["lnb"], chan="k1")
            for h in range(4):
                P.dma(lambda e, l=l, h=h: e.dma_start(out=wout_bf[:, h * 4:(h + 1) * 4, :].rearrange("p k n -> p (k n)"), in_=wout_s[l][:, h * 4096:(h + 1) * 4096]),
                      reads=[("wosc", l, h, pc) for pc in range(4)], writes=["wout"], chan="wo%d" % h, eng="pool")
            T_phase(l, TILES[0])
            lite0 = LITE_HALO and (not is_even) and (l == n_layers - 1)
            if not lite0:
                pre(TILES[0])
            pendC = []
            per_tile = -(-len(pend) // len(TILES))
            for ti, tile in enumerate(TILES):
                budget = {"n": per_tile}

                def tick(budget=budget):
                    if pend and budget["n"] > 0:
                        cast_dma(*pend.pop(0))
                        budget["n"] -= 1

                main.tstate["tick"] = tick
                nxt = TILES[ti + 1] if ti + 1 < len(TILES) else None
                early = [(lambda nxt=nxt: T_load(l, nxt))] if nxt is not None else []
                lite = lite0 and ti == 0
                if lite:
                    main.lite(tile, ti, early)
                else:
                    main(tile, ti, pendC, early)
                if nxt is not None:
                    T_phase(l, nxt)
                if is_even:
                    transition(tile, nxt)
                else:
                    transition(tile, nxt, ti, lite)
                pendC = C_items(l, tile, ti)
                if (not is_even and not DEFER_C_ODD) or (is_even and not DEFER_C_EVEN):
                    for c in pendC:
                        c()
                    pendC = []
            for c in pendC:
                c()
            while pend:
                cast_dma(*pend.pop(0))
            for o in P.ops[lstart:]:
                o.reads = o.reads + ("LAYER",)
        print("ops:", len(P.ops), {e: sum(1 for o in P.ops if o.eng == e) for e in ENGS})
        P.finalize(st)
        print("sem max counts:", {k: v for k, v in P.max_counts.items() if k[0] == "e"})
    return nc


def _pool_mats():
    t = np.arange(128)
    mats = np.zeros((NCB, 128, 128), np.float32)
    mats[0] = np.eye(128)
    s_ = t[:, None]
    t_ = t[None, :]
    for g, w in enumerate(POOL_W):
        band = ((t_ - s_) >= 0) & ((t_ - s_) <= w - 1)
        mats[PM_CUR + g] = band / w - np.eye(128)
        mats[PM_PREV + g] = ((t_ + 128 - s_) <= w - 1) / w
    return mats


def _pool_special(first_half):
    g_ = _pool_mats()
    hi = np.zeros((4, 128, 128), np.float32)
    lo = np.zeros((4, 128, 128), np.float32)
    pv = np.zeros((4, 128, 128), np.float32)
    t = np.arange(128)
    s_ = t[:, None]
    t_ = t[None, :]
    for g, w in enumerate(POOL_W):
        if first_half:
            band = ((t_ - s_) >= 0) & ((t_ - s_) <= w - 1)
            inv = (1.0 / np.minimum(t + 1, w)).astype(np.float32)[None, :]
            full = band * inv
            h = full.astype(ml_dtypes.bfloat16).astype(np.float32)
            hi[g] = h - np.eye(128)
            lo[g] = full - h
        else:
            hi[g] = g_[PM_CUR + g]
            pv[g] = g_[PM_PREV + g]
    return hi, lo, pv


_CACHE = {}


def _get_program(n_layers=4):
    if n_layers not in _CACHE:
        _CACHE[n_layers] = build_program(n_layers)
    return _CACHE[n_layers]


def _unit_layout(w, col_groups):
    outs = []
    for cols in col_groups:
        sub = w[:, cols]
        n = sub.shape[1]
        outs.append(np.ascontiguousarray(sub.reshape(8, 128, n).transpose(1, 0, 2)).reshape(128, 8 * n))
    return np.stack(outs)


def prepare_inputs(x, ln_g, ln_b, w_in_even, w_out_even, pool_w, pool_scale, sconv_w, sconv_b,
                   w_in_odd, w_out_odd, sgu_ln_g, sgu_ln_b, sgu_w, sgu_b,
                   dconv_w, dconv_b, dnorm_g, dnorm_b):
    f = np.float32
    x = np.asarray(x, f)
    ar = np.arange
    xa_groups = [ar(0, 512), ar(512, 1024)]
    ev_groups = [np.concatenate([1024 + q * 128 + ar(128), 2048 + q * 128 + ar(128), 3072 + q * 128 + ar(128),
                                 4096 + q * 128 + ar(128), 5120 + q * 128 + ar(128)]) for q in range(8)]
    v_groups = [1024 + ar(0, 512), 1024 + ar(512, 1024)]
    od_groups = [np.concatenate([2048 + q * 128 + ar(128), 0 + q * 128 + ar(128), 3072 + q * 128 + ar(128),
                                 4096 + q * 128 + ar(128), 5120 + q * 128 + ar(128)]) for q in range(8)]
    w_in_even = np.asarray(w_in_even, f)
    w_in_odd = np.asarray(w_in_odd, f)
    w_xa = np.stack([_unit_layout(w_in_even[i], xa_groups) for i in range(2)])
    w_ev = np.stack([_unit_layout(w_in_even[i], ev_groups) for i in range(2)])
    w_v = np.stack([_unit_layout(w_in_odd[i], v_groups) for i in range(2)])
    w_od = np.stack([_unit_layout(w_in_odd[i], od_groups) for i in range(2)])
    wouts = [np.asarray(w_out_even, f)[0], np.asarray(w_out_odd, f)[0], np.asarray(w_out_even, f)[1], np.asarray(w_out_odd, f)[1]]
    w_out = np.stack([np.ascontiguousarray(w.reshape(16, 128, 1024).transpose(1, 0, 2)).reshape(128, 16 * 1024) for w in wouts])
    pw = np.asarray(pool_w, f)
    pw_l = np.ascontiguousarray(pw.reshape(2, 4, 2, 128, 256).transpose(0, 3, 1, 2, 4)).reshape(2, 128, 2048)
    sw = np.asarray(sgu_w, f)
    wsT = np.ascontiguousarray(sw.transpose(0, 3, 1, 2)).reshape(2, 128, 512)
    idx = np.arange(128)
    maskT = ((idx[:, None] // 64) <= (idx[None, :] // 64)).astype(f)
    lng4 = np.asarray(ln_g, f)
    lnb4 = np.asarray(ln_b, f)
    sgb2 = np.asarray(sgu_ln_b, f)
    sgub = np.asarray(sgu_b, f).reshape(2, 512)

    def cols8(v):
        return np.asarray(v, f).reshape(8, 128).T

    base = np.zeros((128, NCOL), f)
    base[:, COL["one"]] = 1.0
    for i in range(2):
        base[:, COL["ps%d" % i]:COL["ps%d" % i] + 8] = cols8(pool_scale[i])
        for j in range(3):
            base[:, COL["cw%d" % i] + j * 8:COL["cw%d" % i] + j * 8 + 8] = cols8(np.asarray(sconv_w)[i, j])
        base[:, COL["cb%d" % i]:COL["cb%d" % i] + 8] = cols8(sconv_b[i])
        base[:, COL["sg%d" % i]:COL["sg%d" % i] + 8] = cols8(sgu_ln_g[i])
        for j in range(31):
            base[:, COL["dw%d" % i] + j:COL["dw%d" % i] + 248:31] = cols8(np.asarray(dconv_w)[i, j])
        base[:, COL["db%d" % i]:COL["db%d" % i] + 8] = cols8(dconv_b[i])
        base[:, COL["ng%d" % i]:COL["ng%d" % i] + 8] = cols8(dnorm_g[i])
        base[:, COL["nb%d" % i]:COL["nb%d" % i] + 8] = cols8(dnorm_b[i])
    gen = _pool_mats()
    maps = []
    for c in range(8):
        b, hf = c // 2, c % 2
        if hf == 0:
            xw = np.concatenate([np.zeros((HALO * 128, D), f), x[b, 0:4096]], axis=0)
        else:
            xw = x[b, 4096 - HALO * 128:8192]
        cm = gen.copy()
        hi, lo, pv = _pool_special(hf == 0)
        cm[PM_SPH:PM_SPH + 4] = hi
        cm[PM_SPL:PM_SPL + 4] = lo
        cm[PM_SPP:PM_SPP + 4] = pv
        cbf = np.ascontiguousarray(cm.transpose(1, 0, 2)).reshape(128, NCB * 128).astype(ml_dtypes.bfloat16)
        cp = base.copy()
        cp[:, COL["flag"]] = float(hf)
        maps.append({
            "xin": np.ascontiguousarray(xw), "consts_bf": cbf, "maskT": maskT, "colpack": cp,
            "ln_g": lng4, "ln_b": lnb4, "sgu_ln_b": sgb2, "sgu_b": sgub,
            "w_xa": w_xa, "w_ev": w_ev, "w_v": w_v, "w_od": w_od, "w_out": w_out,
            "pool_w": pw_l, "wsT": wsT,
        })
    return maps


def kernel(**inputs):
    nc = _get_program(4)
    maps = prepare_inputs(**inputs)
    res = run_bass_kernel_spmd(nc, maps, core_ids=list(range(8)))
    outp = np.empty((4, 8192, D), np.float32)
    for c in range(8):
        b, hf = c // 2, c % 2
        outp[b, hf * 4096:(hf + 1) * 4096] = res.results[c]["out"]
    return outp
```

```python
import numpy as np
import ml_dtypes
from contextlib import ExitStack
import concourse.bass as bass
import concourse.mybir as mybir
from concourse.bass_utils import run_bass_kernel_spmd

F32 = mybir.dt.float32
BF16 = mybir.dt.bfloat16
AF = mybir.ActivationFunctionType
ALU = mybir.AluOpType

D = 1024
NSUB = 34
HALO = 2
NTOK = NSUB * 128
TILES = [(0, 2)] + [(2 + 4 * i, 4) for i in range(8)]
ALPHA = float((2 * 4) ** 0.25)
EPS = 1e-5
POOL_W = (2, 4, 8, 16)
ENGS = ("pe", "act", "dve", "pool", "sp")
DEFER_C_ODD = True
DEFER_C_EVEN = False
LITE_HALO = True
HOOKS_AT0 = True
DIAG_ENG = "dve"
KDVE = 6


class Op:
    __slots__ = ("eng", "fn", "reads", "writes", "chan", "idx", "signal", "token", "waits")

    def __init__(self, eng, fn, reads, writes, chan):
        self.eng = eng
        self.fn = fn
        self.reads = reads
        self.writes = writes
        self.chan = chan
        self.signal = False
        self.token = None
        self.waits = []


class Prog:
    def __init__(self, nc):
        self.nc = nc
        self.ops = []

    def op(self, eng, fn, reads=(), writes=(), chan=None):
        o = Op(eng, fn, tuple(reads), tuple(writes), chan)
        o.idx = len(self.ops)
        self.ops.append(o)
        return o

    def dma(self, fn, reads=(), writes=(), chan=None, eng="sp"):
        assert chan is not None
        return self.op(eng, fn, reads, writes, chan)

    def finalize(self, stack):
        nc = self.nc
        ops = self.ops
        last_w = {}
        readers = {}
        deps = [None] * len(ops)
        chan_last = {}
        for o in ops:
            d = set()
            for k in o.reads:
                w = last_w.get(k)
                if w is not None:
                    d.add(w)
            for k in o.writes:
                w = last_w.get(k)
                if w is not None:
                    d.add(w)
                r = readers.get(k)
                if r:
                    d.update(r.values())
            if o.chan is not None:
                p = chan_last.get(o.chan)
                if p is not None:
                    d.add(p)
                chan_last[o.chan] = o.idx
            d.discard(o.idx)
            deps[o.idx] = d
            rk = ("c", o.chan) if o.chan is not None else o.eng
            for k in o.reads:
                readers.setdefault(k, {})[rk] = o.idx
            for k in o.writes:
                last_w[k] = o.idx
                readers[k] = {}
        for o in ops:
            for j in deps[o.idx]:
                oj = ops[j]
                if oj.chan is None and oj.eng == "pe" and o.eng == "pe" and o.chan is None:
                    continue
                oj.signal = True
        for o in ops:
            if o.chan is not None:
                o.signal = True
        sem_names = set()
        for o in ops:
            if o.signal:
                sem_names.add(("c", o.chan) if o.chan is not None else ("e", o.eng))
        sems = {}
        for n in sorted(sem_names):
            sems[n] = stack.enter_context(nc.semaphore("s_%s_%s" % n))
        counts = {n: 0 for n in sem_names}
        for o in ops:
            if o.signal:
                n = ("c", o.chan) if o.chan is not None else ("e", o.eng)
                counts[n] += 16 if o.chan is not None else 1
                o.token = (n, counts[n])
        self.max_counts = dict(counts)
        seen = {e: {} for e in ENGS}
        for o in ops:
            need = {}
            for j in deps[o.idx]:
                oj = ops[j]
                if oj.token is None:
                    continue
                n, v = oj.token
                if v > need.get(n, 0):
                    need[n] = v
            sd = seen[o.eng]
            for n, v in need.items():
                if sd.get(n, 0) >= v:
                    continue
                sd[n] = v
                o.waits.append((n, v))
        block = stack.enter_context(nc.Block())
        per_eng = {e: [o for o in ops if o.eng == e] for e in ENGS}

        def emit(engh, lst):
            for o in lst:
                for n, v in o.waits:
                    engh.wait_ge(sems[n], v)
                ins = o.fn(engh)
                if o.signal:
                    n, v = o.token
                    ins.then_inc(sems[n], 16 if o.chan is not None else 1)

        @block.tensor
        def _(e):
            emit(e, per_eng["pe"])

        @block.scalar
        def _(e):
            emit(e, per_eng["act"])

        @block.vector
        def _(e):
            emit(e, per_eng["dve"])

        @block.gpsimd
        def _(e):
            emit(e, per_eng["pool"])

        @block.sync
        def _(e):
            emit(e, per_eng["sp"])
            for n in sorted(sem_names):
                if n[0] == "c":
                    e.wait_ge(sems[n], counts[n])


def col_layout():
    lay = {}
    pos = 0

    def add(name, n):
        nonlocal pos
        lay[name] = pos
        pos += n

    add("flag", 1)
    add("one", 1)
    for i in range(2):
        add("ps%d" % i, 8)
        add("cw%d" % i, 24)
        add("cb%d" % i, 8)
    for i in range(2):
        add("sg%d" % i, 8)
        add("dw%d" % i, 248)
        add("db%d" % i, 8)
        add("ng%d" % i, 8)
        add("nb%d" % i, 8)
    return lay, pos


COL, NCOL = col_layout()
PM_CUR, PM_PREV, PM_SPH, PM_SPL, PM_SPP = 1, 5, 9, 13, 17
NCB = 21


def build_program(n_layers=4, tiles=None):
    TILES = tiles if tiles is not None else globals()["TILES"]
    NTOK = 128 * sum(t[1] for t in TILES)
    nc = bass.Bass("TRN2", target_bir_lowering=False)
    st = ExitStack()
    with st:
        def dram(name, shape, dt, kind):
            return nc.dram_tensor(name, list(shape), dt, kind=kind).ap()

        xin = dram("xin", [NTOK, D], F32, "ExternalInput")
        out = dram("out", [NTOK - HALO * 128, D], F32, "ExternalOutput")
        consts_bf_d = dram("consts_bf", [128, NCB * 128], BF16, "ExternalInput")
        maskT_d = dram("maskT", [128, 128], F32, "ExternalInput")
        colpack_d = dram("colpack", [128, NCOL], F32, "ExternalInput")
        lng_d = dram("ln_g", [4, D], F32, "ExternalInput")
        lnb_d = dram("ln_b", [4, D], F32, "ExternalInput")
        sgb_d = dram("sgu_ln_b", [2, D], F32, "ExternalInput")
        sgub_d = dram("sgu_b", [2, 512], F32, "ExternalInput")
        wxa_d = dram("w_xa", [2, 2, 128, 8 * 512], F32, "ExternalInput")
        wev_d = dram("w_ev", [2, 8, 128, 8 * 640], F32, "ExternalInput")
        wv_d = dram("w_v", [2, 2, 128, 8 * 512], F32, "ExternalInput")
        wod_d = dram("w_od", [2, 8, 128, 8 * 640], F32, "ExternalInput")
        wout_d = dram("w_out", [4, 128, 16 * 1024], F32, "ExternalInput")
        pw_d = dram("pool_w", [2, 128, 4 * 2 * 256], F32, "ExternalInput")
        wsT_d = dram("wsT", [2, 128, 512], F32, "ExternalInput")
        xs_a = dram("xs_a", [NTOK, D], F32, "Internal")
        xs_b = dram("xs_b", [NTOK, D], F32, "Internal")
        wxa_s = dram("wxa_s", [2, 2, 128, 8 * 512], BF16, "Internal")
        wev_s = dram("wev_s", [2, 8, 128, 8 * 640], BF16, "Internal")
        wv_s = dram("wv_s", [2, 2, 128, 8 * 512], BF16, "Internal")
        wod_s = dram("wod_s", [2, 8, 128, 8 * 640], BF16, "Internal")
        wout_s = dram("wout_s", [4, 128, 16 * 1024], BF16, "Internal")

        def sb(name, shape, dt):
            return st.enter_context(nc.sbuf_tensor(name, list(shape), dt))

        cbf = sb("cbf", [128, NCB, 128], BF16)
        ident = cbf[:, 0, :]
        onesD = sb("onesD", [128, 128], BF16)
        maskT = sb("maskT_s", [128, 128], F32)
        colp = sb("colp", [128, NCOL], F32)
        smalls = sb("smalls", [128, 8], F32)
        ones_row = sb("ones_row", [1, 128], F32)
        lng = sb("lng", [128, D], F32)
        lnb = sb("lnb", [128, D], F32)
        wout_bf = sb("wout_bf", [128, 16, 1024], BF16)
        pw_bf = sb("pw_bf", [128, 4, 2, 256], BF16)
        wsT_bf = sb("wsT_bf", [128, 4, 128], BF16)
        Bias = sb("Bias", [128, 8, 128], F32)
        dg = sb("dg", [128, 2, 16, 128], BF16)
        wbuf = sb("wbuf", [128, 2, 8, 640], BF16)
        xb16 = sb("xb16", [128, 4, D], BF16)
        xT = sb("xT", [128, 8, 512], BF16)
        concat = sb("concat", [128, 2, 16, 512], BF16)
        x32 = sb("x32", [128, D], F32)
        rbuf = sb("rbuf", [128, 2, D], F32)
        stt = sb("stt", [128, 4, 12], F32)
        mvt = sb("mvt", [128, 4, 4], F32)
        wt = sb("wt", [128, 248], F32)
        acc = sb("acc", [128, 2, 512], F32)
        UN = 0

        def carve(n):
            nonlocal UN
            o = UN
            UN += n + (n % 2)
            return o

        ev = {}
        ev["xa"] = carve(5 * 1024)
        ev["pT"] = carve(4 * 512)
        ev["sga"] = carve(4 * 512)
        ev["hsb"] = carve(2 * 512)
        ev["sgb"] = carve(2 * 512)
        ev["m"] = carve(2 * 512)
        ev["pst"] = carve(8 * 514)
        ev_end = UN
        UN = 0
        od = {}
        od["vn"] = carve(4 * 1024)
        od["t1"] = carve(2 * 1024)
        od["sgc"] = carve(2 * 512)
        od["t2"] = carve(2 * 512)
        od["th"] = carve(2 * 512)
        od["zst"] = carve(8 * 542)
        od["sgd"] = carve(8 * 512)
        od["czb"] = carve(8 * 512)
        od["sqb"] = carve(2 * 512)
        od["mean"] = carve(1024)
        od["rstd"] = carve(1024)
        od["dd"] = carve(2 * 1024)
        od["s1"] = carve(2 * 512)
        od_end = UN
        U = sb("U", [128, max(ev_end, od_end)], BF16)

        def ub(off, n):
            return U[:, off:off + n]

        def uf(off, n):
            return U[:, off:off + 2 * n].bitcast(F32)

        ps = st.enter_context(nc.psum_tensor("ps", [128, 8 * 512], F32))

        def bank(b):
            return ps[:, b * 512:(b + 1) * 512]

        print("SBUF bytes remaining:", nc.sbuf_bytes_remaining)

        P = Prog(nc)
        state = {"ring": 0, "ring_n": 8, "wu": 0, "castn": 0, "stat": 0, "xb": 0, "rb": 0}

        def nbank():
            b = state["ring"] % state["ring_n"]
            state["ring"] += 1
            return b

        def col(name, idx=0):
            c = COL[name] + idx
            return colp[:, c:c + 1]

        P.dma(lambda e: e.dma_start(out=cbf[:].rearrange("p a b -> p (a b)"), in_=consts_bf_d), writes=["cbf"], chan="k0")
        P.dma(lambda e: e.dma_start(out=maskT[:], in_=maskT_d), writes=["maskT"], chan="k1")
        P.dma(lambda e: e.dma_start(out=colp[:], in_=colpack_d), writes=["colp"], chan="k2")
        P.op("pool", lambda e: e.memset(onesD[:], 1.0 / 1024.0), writes=["onesD"])
        P.op("pool", lambda e: e.memset(smalls[:], -0.5), writes=["smalls"])
        P.op("pool", lambda e: e.memset(ones_row[:], 1.0), writes=["ones_row"])

        def cast_dma(src, dst, rowlen, key):
            n = state["castn"]
            state["castn"] += 1
            P.dma(lambda e: e.dma_start(out=dst, in_=src), writes=[key], chan="cast%d" % (n % 8), eng="pool")

        def cast_list(l):
            i = l // 2
            lst = []

            def add(src, dst, ncols, key):
                for c0 in range(0, ncols, 1024):
                    lst.append((src[:, c0:c0 + 1024], dst[:, c0:c0 + 1024], 1024, key + (c0 // 1024,)))

            if l % 2 == 0:
                for u in range(2):
                    add(wxa_d[i, u], wxa_s[i, u], 4096, ("wsc", l, "a", u))
                for u in range(8):
                    add(wev_d[i, u], wev_s[i, u], 5120, ("wsc", l, "c", u))
            else:
                for u in range(2):
                    add(wv_d[i, u], wv_s[i, u], 4096, ("wsc", l, "a", u))
                for u in range(8):
                    add(wod_d[i, u], wod_s[i, u], 5120, ("wsc", l, "c", u))
            for h in range(4):
                add(wout_d[l][:, h * 4096:(h + 1) * 4096], wout_s[l][:, h * 4096:(h + 1) * 4096], 4096, ("wosc", l, h))
            return lst

        def emit_casts(l):
            for a in cast_list(l):
                cast_dma(*a)

        def wunit_src(l, kind, u):
            i = l // 2
            if l % 2 == 0:
                return (wxa_s[i, u] if kind == "a" else wev_s[i, u])
            return (wv_s[i, u] if kind == "a" else wod_s[i, u])

        def load_wunit(l, kind, u, cols=None):
            k = state["wu"]
            state["wu"] += 1
            b = k % 2
            ncols = 512 if kind == "a" else 640
            src = wunit_src(l, kind, u).rearrange("p (k n) -> p k n", k=8)
            dst = wbuf[:, b, :, 0:ncols]
            if cols is not None:
                src = src[:, :, cols[0]:cols[1]]
                dst = wbuf[:, b, :, cols[0]:cols[1]]
            npieces = 4 if kind == "a" else 5
            P.dma(lambda e: e.dma_start(out=dst, in_=src), reads=[("wsc", l, kind, u, pc) for pc in range(npieces)], writes=[("wb", b)], chan="w%d" % b)
            return b

        def layer_in(l):
            return [xin, xs_a, xs_b, xs_a][l], ["in", "a", "b", "a"][l]

        def layer_out(l):
            if l == n_layers - 1:
                return out, "out"
            return [xs_a, xs_b, xs_a, None][l], ["a", "b", "a", None][l]

        def T_load(l, tile):
            s0, ns = tile
            src, skey = layer_in(l)
            for s in range(ns):
                S = s0 + s
                P.dma(lambda e, S=S, s=s: e.dma_start(out=xb16[:, s, :], in_=src[S * 128:(S + 1) * 128, :]),
                      reads=[("XD", skey, S)], writes=[("xb16", s)], chan="xin%d" % s, eng="pool")

        def T_phase(l, tile):
            s0, ns = tile
            for s in range(ns):
                b = nbank()
                pT = bank(b).bitcast(BF16)
                for j in range(8):
                    P.op("pe", lambda e, j=j, s=s, pT=pT: e.transpose(out=pT[:, j * 128:(j + 1) * 128], in_=xb16[:, s, j * 128:(j + 1) * 128], identity=ident),
                         reads=[("xb16", s), "cbf"], writes=[("ps", b)])
                P.op("dve", lambda e, s=s, pT=pT: e.tensor_copy(out=xT[:, :, s * 128:(s + 1) * 128], in_=pT.rearrange("p (j t) -> p j t", j=8)),
                     reads=[("ps", b)], writes=[("xT", s)])

        def proj_fm(b, wb_i, c0, N, ns):
            for kc in range(8):
                P.op("pe", lambda e, kc=kc: e.matmul(bank(b)[:, 0:N], lhsT=wbuf[:, wb_i, kc, c0:c0 + 128], rhs=xT[:, kc, 0:N], start=(kc == 0), stop=(kc == 7)),
                     reads=[("wb", wb_i)] + [("xT", s) for s in range(ns)], writes=[("ps", b)])

        def proj_tm(b, wb_i, s):
            for kc in range(8):
                P.op("pe", lambda e, kc=kc: e.matmul(bank(b)[:, 0:512], lhsT=xT[:, kc, s * 128:(s + 1) * 128], rhs=wbuf[:, wb_i, kc, 0:512], start=(kc == 0), stop=(kc == 7)),
                     reads=[("wb", wb_i), ("xT", s)], writes=[("ps", b)])

        def ln_chain(slot, srcs, keys):
            for h, a in enumerate(srcs):
                P.op("dve", lambda e, h=h, a=a: e.bn_stats(out=stt[:, slot, h * 6:(h + 1) * 6], in_=a),
                     reads=keys[h], writes=[("stt", slot, h)])
            P.op("dve", lambda e: e.bn_aggr(out=mvt[:, slot, 0:2], in_=stt[:, slot, :]),
                 reads=[("stt", slot, 0), ("stt", slot, 1)], writes=[("mv", slot)])
            P.op("dve", lambda e: e.tensor_scalar(out=mvt[:, slot, 1:2], in0=mvt[:, slot, 1:2], scalar1=EPS, scalar2=None, op0=ALU.add),
                 reads=[("mv", slot)], writes=[("mv", slot)])
            P.op("pool", lambda e: e.tensor_tensor(out=mvt[:, slot, 2:3], in0=mvt[:, slot, 1:2], in1=smalls[:, 0:1], op=ALU.pow),
                 reads=[("mv", slot), "smalls"], writes=[("rs", slot)])
            P.op("dve", lambda e: e.tensor_scalar(out=mvt[:, slot, 3:4], in0=mvt[:, slot, 0:1], scalar1=mvt[:, slot, 2:3], scalar2=-1.0, op0=ALU.mult, op1=ALU.mult),
                 reads=[("mv", slot), ("rs", slot)], writes=[("nm", slot)])

        def C_items(l, tile, ti):
            s0, ns = tile
            src, skey = layer_in(l)
            dst, dkey = layer_out(l)
            last = (l == n_layers - 1)
            cp = ti % 2

            def c_sub(s):
                S = s0 + s
                slot = state["stat"] % 4
                state["stat"] += 1
                rb = state["rb"] % 2
                state["rb"] += 1
                P.dma(lambda e: e.dma_start(out=x32[:], in_=src[S * 128:(S + 1) * 128, :]),
                      reads=[("XD", skey, S)], writes=["x32"], chan="x32")
                bs = [nbank(), nbank()]
                for h in range(2):
                    for kq in range(16):
                        P.op("pe", lambda e, h=h, kq=kq: e.matmul(bank(bs[h])[:, 0:512], lhsT=concat[:, cp, kq, s * 128:(s + 1) * 128], rhs=wout_bf[:, kq, h * 512:(h + 1) * 512], start=(kq == 0), stop=(kq == 15)),
                             reads=[("cc", cp, kq), "wout"], writes=[("ps", bs[h])])
                for h in range(2):
                    P.op("dve", lambda e, h=h: e.scalar_tensor_tensor(out=rbuf[:, rb, h * 512:(h + 1) * 512], in0=x32[:, h * 512:(h + 1) * 512], scalar=ALPHA, in1=bank(bs[h])[:, 0:512], op0=ALU.mult, op1=ALU.add),
                         reads=["x32", ("ps", bs[h])], writes=[("r", rb, h)])
                ln_chain(slot, [rbuf[:, rb, 0:512], rbuf[:, rb, 512:1024]], [[("r", rb, 0)], [("r", rb, 1)]])
                P.op("act", lambda e: e.activation(out=rbuf[:, rb, :], in_=rbuf[:, rb, :], func=AF.Identity, scale=mvt[:, slot, 2:3], bias=mvt[:, slot, 3:4]),
                     reads=[("r", rb, 0), ("r", rb, 1), ("rs", slot), ("nm", slot)], writes=[("r", rb, 0), ("r", rb, 1)])
                P.op("pool", lambda e: e.tensor_tensor(out=rbuf[:, rb, :], in0=rbuf[:, rb, :], in1=lng[:], op=ALU.mult),
                     reads=[("r", rb, 0), ("r", rb, 1), "lng"], writes=[("r", rb, 0), ("r", rb, 1)])
                P.op("pool", lambda e: e.tensor_tensor(out=rbuf[:, rb, :], in0=rbuf[:, rb, :], in1=lnb[:], op=ALU.add),
                     reads=[("r", rb, 0), ("r", rb, 1), "lnb"], writes=[("r", rb, 0), ("r", rb, 1)])
                row = (S - HALO) if last else S
                P.dma(lambda e: e.dma_start(out=dst[row * 128:(row + 1) * 128, :], in_=rbuf[:, rb, :]),
                      reads=[("r", rb, 0), ("r", rb, 1)], writes=[("XD", dkey, S)], chan="st%d" % rb, eng="pool")

            items = []
            for s in range(ns):
                if last and (s0 + s) < HALO:
                    continue
                items.append(lambda s=s: c_sub(s))
            return items

        def run_staged(stages, n, hooks, early=(), tick=None):
            ns_ = len(stages)
            hooks = list(hooks)
            nh = len(hooks)
            pts = {}
            if nh:
                for h in range(nh):
                    pts.setdefault(0 if HOOKS_AT0 else min(n - 1, 1 + (2 * h if nh <= 4 else h)), []).append(hooks[h])
            for it in range(n + ns_ - 1):
                for j, stg in enumerate(stages):
                    u = it - j
                    if 0 <= u < n:
                        stg(u)
                if it == 0:
                    for h in early:
                        h()
                if tick is not None:
                    tick()
                for h in pts.get(it, []):
                    h()

        def even_layer(l):
            i = l // 2
            xa = ub(ev["xa"], 5 * 1024).rearrange("p (s c) -> p s c", s=5)
            pT_ = ub(ev["pT"], 4 * 512).rearrange("p (g k n) -> p g k n", g=2, k=2)
            sga = ub(ev["sga"], 4 * 512).rearrange("p (g k n) -> p g k n", g=2, k=2)
            hsb = ub(ev["hsb"], 2 * 512).rearrange("p (a n) -> p a n", a=2)
            sgb = ub(ev["sgb"], 2 * 512).rearrange("p (a n) -> p a n", a=2)
            mm = ub(ev["m"], 2 * 512).rearrange("p (a n) -> p a n", a=2)
            pst = ub(ev["pst"], 8 * 514).rearrange("p (q n) -> p q n", q=8)
            dg3 = dg[:].rearrange("p a b c -> p (a b c)")[:, 0:8 * 3 * 128].rearrange("p (q j c) -> p q j c", q=8, j=3)
            P.dma(lambda e: e.dma_start(out=pw_bf[:].rearrange("p g k n -> p (g k n)"), in_=pw_d[i]), writes=["pw"], chan="pwc", eng="pool")
            for q in range(8):
                for j in range(3):
                    P.op("pool", lambda e, q=q, j=j: e.tensor_scalar(out=dg3[:, q, j, :], in0=ident, scalar1=col("cw%d" % i, j * 8 + q), scalar2=1.0, op0=ALU.mult, op1=ALU.mult),
                         reads=["cbf", "colp"], writes=["dg3"])
            P.op("pool", lambda e: e.memset(xa[:, 0, :], 0.0), writes=[("xa", 0, 0), ("xa", 0, 1)])
            P.op("pool", lambda e: e.memset(pst[:, :, 0:2], 0.0), writes=[("pst", q) for q in range(8)])
            state["ring_n"] = 8
            tstate = {}

            def load_next_c():
                if tstate["q"] < 8:
                    w = load_wunit(l, "c", tstate["q"])
                    tstate["q"] += 1
                    return w
                return None

            def pre(tile):
                s0, ns = tile
                wa = [load_wunit(l, "a", 0), load_wunit(l, "a", 1)]
                tstate["q"] = 0
                tstate["wq"] = {}
                for gg in range(2):
                    for s in range(ns):
                        b = nbank()
                        proj_tm(b, wa[gg], s)
                        P.op("act", lambda e, b=b, s=s, gg=gg: e.activation(out=xa[:, s + 1, gg * 512:(gg + 1) * 512], in_=bank(b)[:, 0:512], func=AF.Copy),
                             reads=[("ps", b)], writes=[("xa", s + 1, gg)])
                    tstate["wq"][gg] = load_next_c()

            def main(tile, ti, hooks, early=()):
                s0, ns = tile
                N = ns * 128
                cp = ti % 2
                wq = tstate["wq"]

                def stage1(q):
                    wb_i = wq[q]
                    g = q // 2
                    gp, qp = g % 2, q % 2
                    gg = q // 4
                    b_ga = nbank()
                    proj_fm(b_ga, wb_i, 0, N, ns)
                    P.op("act", lambda e: e.activation(out=sga[:, gp, qp, 0:N], in_=bank(b_ga)[:, 0:N], func=AF.Silu),
                         reads=[("ps", b_ga)], writes=[("sga", gp, qp)])
                    b_h = nbank()
                    proj_fm(b_h, wb_i, 128, N, ns)
                    P.op("act", lambda e: e.activation(out=hsb[:, qp, 0:N], in_=bank(b_h)[:, 0:N], func=AF.Copy),
                         reads=[("ps", b_h)], writes=[("hsb", qp)])
                    b_cg = nbank()
                    proj_fm(b_cg, wb_i, 384, N, ns)
                    P.op("dve", lambda e: e.tensor_tensor(out=pst[:, q, 2:2 + N], in0=bank(b_cg)[:, 0:N], in1=hsb[:, qp, 0:N], op=ALU.mult),
                         reads=[("ps", b_cg), ("hsb", qp)], writes=[("pst", q)])
                    b_gb = nbank()
                    proj_fm(b_gb, wb_i, 512, N, ns)
                    P.op("act", lambda e: e.activation(out=sgb[:, qp, 0:N], in_=bank(b_gb)[:, 0:N], func=AF.Silu),
                         reads=[("ps", b_gb)], writes=[("sgb", qp)])
                    b_bg = nbank()
                    proj_fm(b_bg, wb_i, 256, N, ns)
                    P.op("dve", lambda e: e.tensor_tensor(out=mm[:, qp, 0:N], in0=bank(b_bg)[:, 0:N], in1=sgb[:, qp, 0:N], op=ALU.mult),
                         reads=[("ps", b_bg), ("sgb", qp)], writes=[("m", qp)])
                    if q + 2 < 8:
                        wq[q + 2] = load_next_c()
                    b_pl = nbank()
                    for s in range(ns):
                        S = s0 + s
                        if S == HALO:
                            mats = [(s + 1, PM_SPH + g), (s + 1, PM_SPL + g), (s, PM_SPP + g)]
                        else:
                            mats = [(s + 1, PM_CUR + g), (s, PM_PREV + g)]
                        for idx, (slot, pm) in enumerate(mats):
                            P.op("pe", lambda e, s=s, slot=slot, pm=pm, idx=idx, nm=len(mats): e.matmul(bank(b_pl)[:, s * 128:(s + 1) * 128], lhsT=xa[:, slot, q * 128:(q + 1) * 128], rhs=cbf[:, pm, :], start=(idx == 0), stop=(idx == nm - 1)),
                                 reads=[("xa", slot, gg), "cbf"], writes=[("ps", b_pl)])
                    P.op("act", lambda e: e.activation(out=pT_[:, gp, qp, 0:N], in_=bank(b_pl)[:, 0:N], func=AF.Copy),
                         reads=[("ps", b_pl)], writes=[("pT", gp, qp)])

                def stage2(q):
                    g = q // 2
                    gp, qp = g % 2, q % 2
                    b_cv = nbank()
                    for j in range(3):
                        P.op("pe", lambda e, j=j: e.matmul(bank(b_cv)[:, 0:N], lhsT=dg3[:, q, j, :], rhs=pst[:, q, j:j + N], start=(j == 0), stop=(j == 2)),
                             reads=["dg3", ("pst", q)], writes=[("ps", b_cv)])
                    P.op("dve", lambda e: e.scalar_tensor_tensor(out=concat[:, cp, 8 + q, 0:N], in0=bank(b_cv)[:, 0:N], scalar=col("cb%d" % i, q), in1=mm[:, qp, 0:N], op0=ALU.add, op1=ALU.mult),
                         reads=[("ps", b_cv), ("m", qp), "colp"], writes=[("cc", cp, 8 + q)])
                    fl = col("flag") if ti == 0 else col("one")
                    P.op("pool", lambda e: e.tensor_scalar(out=pst[:, q, 0:2], in0=pst[:, q, N:N + 2], scalar1=fl, scalar2=1.0, op0=ALU.mult, op1=ALU.mult),
                         reads=[("pst", q), "colp"], writes=[("pst", q)])
                    if q % 2 == 1:
                        for qo in (q - 1, q):
                            b_pw = nbank()
                            for k2 in range(2):
                                P.op("pe", lambda e, k2=k2, qo=qo, b_pw=b_pw: e.matmul(bank(b_pw)[:, 0:N], lhsT=pw_bf[:, g, k2, (qo % 2) * 128:(qo % 2 + 1) * 128], rhs=pT_[:, gp, k2, 0:N], start=(k2 == 0), stop=(k2 == 1)),
                                     reads=["pw", ("pT", gp, 0), ("pT", gp, 1)], writes=[("ps", b_pw)])
                            P.op("dve", lambda e, qo=qo, b_pw=b_pw: e.scalar_tensor_tensor(out=concat[:, cp, qo, 0:N], in0=bank(b_pw)[:, 0:N], scalar=col("ps%d" % i, qo), in1=sga[:, gp, qo % 2, 0:N], op0=ALU.mult, op1=ALU.mult),
                                 reads=[("ps", b_pw), ("sga", gp, qo % 2), "colp"], writes=[("cc", cp, qo)])

                run_staged([stage1, stage2], 8, hooks, early, tstate.get("tick"))
                P.op("pool", lambda e: e.tensor_copy(out=xa[:, 0, :], in_=xa[:, ns, :]),
                     reads=[("xa", ns, 0), ("xa", ns, 1)], writes=[("xa", 0, 0), ("xa", 0, 1)])

            def transition(tile, nxt):
                if nxt is not None:
                    pre(nxt)

            main.tstate = tstate
            return pre, main, transition

        def odd_layer(l):
            i = l // 2
            vn = ub(od["vn"], 4 * 1024).rearrange("p (s c) -> p s c", s=4)
            t1 = uf(od["t1"], 2 * 512).rearrange("p (a n) -> p a n", a=2)
            sgc = ub(od["sgc"], 2 * 512).rearrange("p (a n) -> p a n", a=2)
            t2 = ub(od["t2"], 2 * 512).rearrange("p (a n) -> p a n", a=2)
            th = ub(od["th"], 2 * 512).rearrange("p (a n) -> p a n", a=2)
            zst = ub(od["zst"], 8 * 542).rearrange("p (q n) -> p q n", q=8)
            sgd = ub(od["sgd"], 8 * 512).rearrange("p (q n) -> p q n", q=8)
            czb = ub(od["czb"], 8 * 512).rearrange("p (q n) -> p q n", q=8)
            sqb = ub(od["sqb"], 2 * 512).rearrange("p (a n) -> p a n", a=2)
            mean = uf(od["mean"], 512)
            rstd = uf(od["rstd"], 512)
            dd = uf(od["dd"], 2 * 512).rearrange("p (a n) -> p a n", a=2)
            s1 = ub(od["s1"], 2 * 512).rearrange("p (a n) -> p a n", a=2)
            MEANB, EX2B = 6, 7
            state["ring_n"] = 6
            wsf = rbuf[:, 0, 0:512]
            bsg = rbuf[:, 1, :]
            sgr = x32[0:1, 0:512]
            P.dma(lambda e: e.dma_start(out=wsf, in_=wsT_d[i]), writes=[("r", 0, 0)], chan="k0")
            P.dma(lambda e: e.dma_start(out=bsg, in_=sgb_d[i:i + 1, :].partition_broadcast(128)), writes=[("r", 1, 0), ("r", 1, 1)], chan="k1")
            P.dma(lambda e: e.dma_start(out=sgr, in_=sgub_d[i:i + 1, :]), writes=["x32"], chan="k2")
            for hd in range(4):
                P.op("dve", lambda e, hd=hd: e.tensor_tensor(out=wsf[:, hd * 128:(hd + 1) * 128], in0=wsf[:, hd * 128:(hd + 1) * 128], in1=maskT[:], op=ALU.mult),
                     reads=[("r", 0, 0), "maskT"], writes=[("r", 0, 0)])
            P.op("dve", lambda e: e.tensor_copy(out=wsT_bf[:].rearrange("p h n -> p (h n)"), in_=wsf), reads=[("r", 0, 0)], writes=["wsT"])
            for q in range(8):
                hd = q // 2
                b = nbank()
                P.op("pe", lambda e, q=q, hd=hd, b=b: e.matmul(bank(b)[:, 0:128], lhsT=bsg[:, q * 128:(q + 1) * 128], rhs=wsf[:, hd * 128:(hd + 1) * 128], start=True, stop=False),
                     reads=[("r", 1, 0), ("r", 1, 1), ("r", 0, 0)], writes=[("ps", b)])
                P.op("pe", lambda e, hd=hd, b=b: e.matmul(bank(b)[:, 0:128], lhsT=ones_row[0:1, :], rhs=sgr[0:1, hd * 128:(hd + 1) * 128], start=False, stop=True),
                     reads=["ones_row", "x32"], writes=[("ps", b)])
                P.op("act", lambda e, q=q, b=b: e.activation(out=Bias[:, q, :], in_=bank(b)[:, 0:128], func=AF.Copy),
                     reads=[("ps", b)], writes=["Bias"])
            P.op("pool", lambda e: e.memset(zst[:, :, 0:30], 0.0), writes=[("zst", q) for q in range(8)])
            dwc = COL["dw%d" % i]
            P.op("dve", lambda e: e.tensor_scalar(out=wt[:], in0=colp[:, dwc:dwc + 248], scalar1=0.5, scalar2=None, op0=ALU.mult),
                 reads=["colp"], writes=["wt"])

            def build_diag(q, hs):
                j0 = KDVE if hs == 0 else 16
                nj = (16 - KDVE) if hs == 0 else 15
                P.op(DIAG_ENG, lambda e: e.tensor_tensor(out=dg[:, hs, 0:nj, :], in0=ident.unsqueeze(1).to_broadcast([128, nj, 128]),
                                                      in1=wt[:, q * 31 + j0:q * 31 + j0 + nj].unsqueeze(2).to_broadcast([128, nj, 128]), op=ALU.mult),
                     reads=["cbf", "wt"], writes=[("dg", hs)])

            tstate = {}

            def load_next_c():
                if tstate["q"] < 8:
                    w = load_wunit(l, "c", tstate["q"])
                    tstate["q"] += 1
                    return w
                return None

            def pre_items(tile):
                s0, ns = tile

                def begin():
                    tstate["wa"] = [load_wunit(l, "a", 0), load_wunit(l, "a", 1)]
                    tstate["q"] = 0

                def c1_sub(s):
                    wa = tstate["wa"]
                    slot = state["stat"] % 4
                    state["stat"] += 1
                    bs = [nbank(), nbank()]
                    for gg in range(2):
                        proj_tm(bs[gg], wa[gg], s)
                    ln_chain(slot, [bank(bs[0])[:, 0:512], bank(bs[1])[:, 0:512]], [[("ps", bs[0])], [("ps", bs[1])]])
                    for gg in range(2):
                        P.op("act", lambda e, gg=gg: e.activation(out=vn[:, s, gg * 512:(gg + 1) * 512], in_=bank(bs[gg])[:, 0:512], func=AF.Identity, scale=mvt[:, slot, 2:3], bias=mvt[:, slot, 3:4]),
                             reads=[("ps", bs[gg]), ("rs", slot), ("nm", slot)], writes=[("vn", s, gg)])

                return [begin] + [(lambda s=s: c1_sub(s)) for s in range(ns)]

            def pre(tile):
                for f in pre_items(tile):
                    f()

            def main(tile, ti, hooks, early=()):
                s0, ns = tile
                N = ns * 128
                cp = ti % 2
                wq = {0: load_next_c()}
                wq[1] = load_next_c()

                def stage1(q):
                    wb_i = wq[q]
                    hd = q // 2
                    qp = q % 2
                    gg = q // 4
                    b_gc = nbank()
                    proj_fm(b_gc, wb_i, 0, N, ns)
                    P.op("act", lambda e: e.activation(out=sgc[:, qp, 0:N], in_=bank(b_gc)[:, 0:N], func=AF.Silu),
                         reads=[("ps", b_gc)], writes=[("sgc", qp)])
                    b_bl = nbank()
                    proj_fm(b_bl, wb_i, 384, N, ns)
                    P.op("act", lambda e: e.activation(out=th[:, qp, 0:N], in_=bank(b_bl)[:, 0:N], func=AF.Tanh, scale=0.5),
                         reads=[("ps", b_bl)], writes=[("th", qp)])
                    b_a = nbank()
                    proj_fm(b_a, wb_i, 256, N, ns)
                    P.op("dve", lambda e: e.scalar_tensor_tensor(out=zst[:, q, 30:30 + N], in0=th[:, qp, 0:N], scalar=1.0, in1=bank(b_a)[:, 0:N], op0=ALU.add, op1=ALU.mult),
                         reads=[("th", qp), ("ps", b_a)], writes=[("zst", q)])
                    b_sg = nbank()
                    for s in range(ns):
                        P.op("pe", lambda e, s=s: e.matmul(bank(b_sg)[:, s * 128:(s + 1) * 128], lhsT=vn[:, s, q * 128:(q + 1) * 128], rhs=wsT_bf[:, hd, :], start=True, stop=True),
                             reads=[("vn", s, gg), "wsT"], writes=[("ps", b_sg)])
                    P.op("dve", lambda e: e.scalar_tensor_tensor(out=t1[:, qp, 0:N].rearrange("p (s n) -> p s n", s=ns), in0=bank(b_sg)[:, 0:N].rearrange("p (s n) -> p s n", s=ns), scalar=col("sg%d" % i, q), in1=Bias[:, q:q + 1, :].to_broadcast([128, ns, 128]), op0=ALU.mult, op1=ALU.add),
                         reads=[("ps", b_sg), "Bias", "colp"], writes=[("t1", qp)])
                    b_u = nbank()
                    proj_fm(b_u, wb_i, 128, N, ns)
                    P.op("dve", lambda e: e.tensor_tensor(out=t2[:, qp, 0:N], in0=bank(b_u)[:, 0:N], in1=t1[:, qp, 0:N], op=ALU.mult),
                         reads=[("ps", b_u), ("t1", qp)], writes=[("t2", qp)])
                    P.op("dve", lambda e: e.tensor_tensor(out=concat[:, cp, q, 0:N], in0=t2[:, qp, 0:N], in1=sgc[:, qp, 0:N], op=ALU.mult),
                         reads=[("t2", qp), ("sgc", qp)], writes=[("cc", cp, q)])
                    b_gd = nbank()
                    proj_fm(b_gd, wb_i, 512, N, ns)
                    P.op("act", lambda e: e.activation(out=sgd[:, q, 0:N], in_=bank(b_gd)[:, 0:N], func=AF.Silu),
                         reads=[("ps", b_gd)], writes=[("sgd", q)])
                    if q + 2 < 8:
                        wq[q + 2] = load_next_c()
                    if q == 0:
                        build_diag(0, 0)
                        build_diag(0, 1)

                def stage2(q):
                    qp = q % 2
                    for j in range(KDVE):
                        wc = wt[:, q * 31 + j:q * 31 + j + 1]
                        if j == 0:
                            P.op("dve", lambda e, j=j, wc=wc: e.tensor_scalar(out=acc[:, qp, 0:N], in0=zst[:, q, j:j + N], scalar1=wc, scalar2=None, op0=ALU.mult),
                                 reads=[("zst", q), "wt"], writes=[("acc", qp)])
                        else:
                            P.op("dve", lambda e, j=j, wc=wc: e.scalar_tensor_tensor(out=acc[:, qp, 0:N], in0=zst[:, q, j:j + N], scalar=wc, in1=acc[:, qp, 0:N], op0=ALU.mult, op1=ALU.add),
                                 reads=[("zst", q), "wt", ("acc", qp)], writes=[("acc", qp)])
                    b_cv = nbank()
                    for hs in range(2):
                        j0 = KDVE if hs == 0 else 16
                        nj = (16 - KDVE) if hs == 0 else 15
                        for jj in range(nj):
                            j = j0 + jj
                            P.op("pe", lambda e, jj=jj, j=j, hs=hs: e.matmul(bank(b_cv)[:, 0:N], lhsT=dg[:, hs, jj, :], rhs=zst[:, q, j:j + N], start=(j == KDVE), stop=(j == 30)),
                                 reads=[("dg", hs), ("zst", q)], writes=[("ps", b_cv)])
                        if q + 1 < 8:
                            build_diag(q + 1, hs)
                    if KDVE > 0:
                        P.op("dve", lambda e: e.scalar_tensor_tensor(out=czb[:, q, 0:N], in0=bank(b_cv)[:, 0:N], scalar=col("db%d" % i, q), in1=acc[:, qp, 0:N], op0=ALU.add, op1=ALU.add),
                             reads=[("ps", b_cv), "colp", ("acc", qp)], writes=[("czb", q)])
                        P.op("act", lambda e: e.activation(out=sqb[:, qp, 0:N], in_=czb[:, q, 0:N], func=AF.Square),
                             reads=[("czb", q)], writes=[("sqb", qp)])
                    else:
                        P.op("act", lambda e: e.activation(out=czb[:, q, 0:N], in_=bank(b_cv)[:, 0:N], func=AF.Identity, bias=col("db%d" % i, q)),
                             reads=[("ps", b_cv), "colp"], writes=[("czb", q)])
                        P.op("act", lambda e: e.activation(out=sqb[:, qp, 0:N], in_=bank(b_cv)[:, 0:N], func=AF.Square, bias=col("db%d" % i, q)),
                             reads=[("ps", b_cv), "colp"], writes=[("sqb", qp)])
                    fl = col("flag") if ti == 0 else col("one")
                    P.op("pool", lambda e: e.tensor_scalar(out=zst[:, q, 0:30], in0=zst[:, q, N:N + 30], scalar1=fl, scalar2=1.0, op0=ALU.mult, op1=ALU.mult),
                         reads=[("zst", q), "colp"], writes=[("zst", q)])

                def stage3(q):
                    qp = q % 2
                    P.op("pe", lambda e: e.matmul(bank(MEANB)[:, 0:N], lhsT=onesD[:], rhs=czb[:, q, 0:N], start=(q == 0), stop=(q == 7)),
                         reads=["onesD", ("czb", q)], writes=[("ps", MEANB)])
                    P.op("pe", lambda e: e.matmul(bank(EX2B)[:, 0:N], lhsT=onesD[:], rhs=sqb[:, qp, 0:N], start=(q == 0), stop=(q == 7)),
                         reads=["onesD", ("sqb", qp)], writes=[("ps", EX2B)])

                run_staged([stage1, stage2, stage3], 8, hooks, early, tstate.get("tick"))

            def main_lite(tile, ti, early=()):
                s0, ns = tile
                N = ns * 128
                sl = ns - 1
                tstate["q"] = 0

                def load_part():
                    if tstate["q"] < 8:
                        w = load_wunit(l, "c", tstate["q"], cols=(256, 512))
                        tstate["q"] += 1
                        return w
                    return None

                wq = {0: load_part()}
                wq[1] = load_part()
                for h in early:
                    h()
                for q in range(8):
                    def unit(q=q):
                        wb_i = wq[q]
                        qp = q % 2
                        b_bl = nbank()
                        for kc in range(8):
                            P.op("pe", lambda e, kc=kc: e.matmul(bank(b_bl)[:, 0:128], lhsT=wbuf[:, wb_i, kc, 384:512], rhs=xT[:, kc, sl * 128:(sl + 1) * 128], start=(kc == 0), stop=(kc == 7)),
                                 reads=[("wb", wb_i), ("xT", sl)], writes=[("ps", b_bl)])
                        P.op("act", lambda e: e.activation(out=th[:, qp, 0:128], in_=bank(b_bl)[:, 0:128], func=AF.Tanh, scale=0.5),
                             reads=[("ps", b_bl)], writes=[("th", qp)])
                        b_a = nbank()
                        for kc in range(8):
                            P.op("pe", lambda e, kc=kc: e.matmul(bank(b_a)[:, 0:128], lhsT=wbuf[:, wb_i, kc, 256:384], rhs=xT[:, kc, sl * 128:(sl + 1) * 128], start=(kc == 0), stop=(kc == 7)),
                                 reads=[("wb", wb_i), ("xT", sl)], writes=[("ps", b_a)])
                        P.op("dve", lambda e: e.scalar_tensor_tensor(out=zst[:, q, 30 + sl * 128:30 + N], in0=th[:, qp, 0:128], scalar=1.0, in1=bank(b_a)[:, 0:128], op0=ALU.add, op1=ALU.mult),
                             reads=[("th", qp), ("ps", b_a)], writes=[("zst", q)])
                        fl = col("flag") if ti == 0 else col("one")
                        P.op("pool", lambda e: e.tensor_scalar(out=zst[:, q, 0:30], in0=zst[:, q, N:N + 30], scalar1=fl, scalar2=1.0, op0=ALU.mult, op1=ALU.mult),
                             reads=[("zst", q), "colp"], writes=[("zst", q)])
                        if q + 2 < 8:
                            wq[q + 2] = load_part()
                    unit()

            def s3d_items(tile, ti):
                s0, ns = tile
                N = ns * 128
                cp = ti % 2

                def fin():
                    P.op("act", lambda e: e.activation(out=mean[:, 0:N], in_=bank(MEANB)[:, 0:N], func=AF.Copy),
                         reads=[("ps", MEANB)], writes=["mean"])
                    m2 = dd[:, 0, 0:N]
                    P.op("dve", lambda e: e.tensor_tensor(out=m2, in0=mean[:, 0:N], in1=mean[:, 0:N], op=ALU.mult),
                         reads=["mean"], writes=[("dd", 0)])
                    P.op("dve", lambda e: e.scalar_tensor_tensor(out=rstd[:, 0:N], in0=bank(EX2B)[:, 0:N], scalar=EPS, in1=m2, op0=ALU.add, op1=ALU.subtract),
                         reads=[("ps", EX2B), ("dd", 0)], writes=["rstd"])
                    P.op("act", lambda e: e.activation(out=rstd[:, 0:N], in_=rstd[:, 0:N], func=AF.Sqrt),
                         reads=["rstd"], writes=["rstd"])
                    P.op("dve", lambda e: e.reciprocal(out=rstd[:, 0:N], in_=rstd[:, 0:N]),
                         reads=["rstd"], writes=["rstd"])

                def s3d_a(q):
                    qp = q % 2
                    P.op("dve", lambda e: e.tensor_tensor(out=dd[:, qp, 0:N], in0=czb[:, q, 0:N], in1=mean[:, 0:N], op=ALU.subtract),
                         reads=[("czb", q), "mean"], writes=[("dd", qp)])
                    P.op("dve", lambda e: e.tensor_tensor(out=dd[:, qp, 0:N], in0=dd[:, qp, 0:N], in1=rstd[:, 0:N], op=ALU.mult),
                         reads=[("dd", qp), "rstd"], writes=[("dd", qp)])
                    P.op("act", lambda e: e.activation(out=s1[:, qp, 0:N], in_=dd[:, qp, 0:N], func=AF.Silu, scale=col("ng%d" % i, q), bias=col("nb%d" % i, q)),
                         reads=[("dd", qp), "colp"], writes=[("s1", qp)])

                def s3d_b(q):
                    qp = q % 2
                    P.op("dve", lambda e: e.tensor_tensor(out=concat[:, cp, 8 + q, 0:N], in0=s1[:, qp, 0:N], in1=sgd[:, q, 0:N], op=ALU.mult),
                         reads=[("s1", qp), ("sgd", q)], writes=[("cc", cp, 8 + q)])

                def s3d_step(k):
                    if k < 8:
                        s3d_a(k)
                    if k >= 1:
                        s3d_b(k - 1)

                return [fin] + [(lambda k=k: s3d_step(k)) for k in range(9)]

            def transition(tile, nxt, ti, lite=False):
                bq = pre_items(nxt) if nxt is not None else []
                if lite:
                    while bq:
                        bq.pop(0)()
                    return
                a = s3d_items(tile, ti)
                a.pop(0)()
                if bq:
                    bq.pop(0)()
                if bq:
                    bq.pop(0)()
                for idx in range(9):
                    a.pop(0)()
                    if idx % 2 == 1 and bq:
                        bq.pop(0)()
                while bq:
                    bq.pop(0)()

            main.lite = main_lite
            main.tstate = tstate
            return pre, main, transition

        first_casts = cast_list(0)
        T_load(0, TILES[0])
        for _ in range(4 + 4 + 5 * 4):
            cast_dma(*first_casts.pop(0))
        for l in range(n_layers):
            if l > 0:
                P.op("pool", lambda e: e.memset(smalls[:, 4:5], 0.0), reads=[], writes=["LAYER"])
            lstart = len(P.ops)
            is_even = (l % 2 == 0)
            pre, main, transition = even_layer(l) if is_even else odd_layer(l)
            pend = cast_list(l + 1) if l + 1 < n_layers else []
            if l == 0:
                while first_casts:
                    cast_dma(*first_casts.pop(0))
            else:
                T_load(l, TILES[0])
            P.dma(lambda e, l=l: e.dma_start(out=lng[:], in_=lng_d[l:l + 1, :].partition_broadcast(128)), writes=["lng"], chan="k0")
            P.dma(lambda e, l=l: e.dma_start(out=lnb[:], in_=lnb_d[l:l + 1, :].partition_broadcast(128)), writes=["lnb"], chan="k1")
            for h in range(4):
                P.dma(lambda e, l=l, h=h: e.dma_start(out=wout_bf[:, h * 4:(h + 1) * 4, :].rearrange("p k n -> p (k n)"), in_=wout_s[l][:, h * 4096:(h + 1) * 4096]),
                      reads=[("wosc", l, h, pc) for pc in range(4)], writes=["wout"], chan="wo%d" % h, eng="pool")
            T_phase(l, TILES[0])
            lite0 = LITE_HALO and (not is_even) and (l == n_layers - 1)
            if not lite0:
                pre(TILES[0])
            pendC = []
            per_tile = -(-len(pend) // len(TILES))
            for ti, tile in enumerate(TILES):
                budget = {"n": per_tile}

                def tick(budget=budget):
                    if pend and budget["n"] > 0:
                        cast_dma(*pend.pop(0))
                        budget["n"] -= 1

                main.tstate["tick"] = tick
                nxt = TILES[ti + 1] if ti + 1 < len(TILES) else None
                early = [(lambda nxt=nxt: T_load(l, nxt))] if nxt is not None else []
                lite = lite0 and ti == 0
                if lite:
                    main.lite(tile, ti, early)
                else:
                    main(tile, ti, pendC, early)
                if nxt is not None:
                    T_phase(l, nxt)
                if is_even:
                    transition(tile, nxt)
                else:
                    transition(tile, nxt, ti, lite)
                pendC = C_items(l, tile, ti)
                if (not is_even and not DEFER_C_ODD) or (is_even and not DEFER_C_EVEN):
                    for c in pendC:
                        c()
                    pendC = []
            for c in pendC:
                c()
            while pend:
                cast_dma(*pend.pop(0))
            for o in P.ops[lstart:]:
                o.reads = o.reads + ("LAYER",)
        print("ops:", len(P.ops), {e: sum(1 for o in P.ops if o.eng == e) for e in ENGS})
        P.finalize(st)
        print("sem max counts:", {k: v for k, v in P.max_counts.items() if k[0] == "e"})
    return nc


def _pool_mats():
    t = np.arange(128)
    mats = np.zeros((NCB, 128, 128), np.float32)
    mats[0] = np.eye(128)
    s_ = t[:, None]
    t_ = t[None, :]
    for g, w in enumerate(POOL_W):
        band = ((t_ - s_) >= 0) & ((t_ - s_) <= w - 1)
        mats[PM_CUR + g] = band / w - np.eye(128)
        mats[PM_PREV + g] = ((t_ + 128 - s_) <= w - 1) / w
    return mats


def _pool_special(first_half):
    g_ = _pool_mats()
    hi = np.zeros((4, 128, 128), np.float32)
    lo = np.zeros((4, 128, 128), np.float32)
    pv = np.zeros((4, 128, 128), np.float32)
    t = np.arange(128)
    s_ = t[:, None]
    t_ = t[None, :]
    for g, w in enumerate(POOL_W):
        if first_half:
            band = ((t_ - s_) >= 0) & ((t_ - s_) <= w - 1)
            inv = (1.0 / np.minimum(t + 1, w)).astype(np.float32)[None, :]
            full = band * inv
            h = full.astype(ml_dtypes.bfloat16).astype(np.float32)
            hi[g] = h - np.eye(128)
            lo[g] = full - h
        else:
            hi[g] = g_[PM_CUR + g]
            pv[g] = g_[PM_PREV + g]
    return hi, lo, pv


_CACHE = {}


def _get_program(n_layers=4):
    if n_layers not in _CACHE:
        _CACHE[n_layers] = build_program(n_layers)
    return _CACHE[n_layers]


def _unit_layout(w, col_groups):
    outs = []
    for cols in col_groups:
        sub = w[:, cols]
        n = sub.shape[1]
        outs.append(np.ascontiguousarray(sub.reshape(8, 128, n).transpose(1, 0, 2)).reshape(128, 8 * n))
    return np.stack(outs)


def prepare_inputs(x, ln_g, ln_b, w_in_even, w_out_even, pool_w, pool_scale, sconv_w, sconv_b,
                   w_in_odd, w_out_odd, sgu_ln_g, sgu_ln_b, sgu_w, sgu_b,
                   dconv_w, dconv_b, dnorm_g, dnorm_b):
    f = np.float32
    x = np.asarray(x, f)
    ar = np.arange
    xa_groups = [ar(0, 512), ar(512, 1024)]
    ev_groups = [np.concatenate([1024 + q * 128 + ar(128), 2048 + q * 128 + ar(128), 3072 + q * 128 + ar(128),
                                 4096 + q * 128 + ar(128), 5120 + q * 128 + ar(128)]) for q in range(8)]
    v_groups = [1024 + ar(0, 512), 1024 + ar(512, 1024)]
    od_groups = [np.concatenate([2048 + q * 128 + ar(128), 0 + q * 128 + ar(128), 3072 + q * 128 + ar(128),
                                 4096 + q * 128 + ar(128), 5120 + q * 128 + ar(128)]) for q in range(8)]
    w_in_even = np.asarray(w_in_even, f)
    w_in_odd = np.asarray(w_in_odd, f)
    w_xa = np.stack([_unit_layout(w_in_even[i], xa_groups) for i in range(2)])
    w_ev = np.stack([_unit_layout(w_in_even[i], ev_groups) for i in range(2)])
    w_v = np.stack([_unit_layout(w_in_odd[i], v_groups) for i in range(2)])
    w_od = np.stack([_unit_layout(w_in_odd[i], od_groups) for i in range(2)])
    wouts = [np.asarray(w_out_even, f)[0], np.asarray(w_out_odd, f)[0], np.asarray(w_out_even, f)[1], np.asarray(w_out_odd, f)[1]]
    w_out = np.stack([np.ascontiguousarray(w.reshape(16, 128, 1024).transpose(1, 0, 2)).reshape(128, 16 * 1024) for w in wouts])
    pw = np.asarray(pool_w, f)
    pw_l = np.ascontiguousarray(pw.reshape(2, 4, 2, 128, 256).transpose(0, 3, 1, 2, 4)).reshape(2, 128, 2048)
    sw = np.asarray(sgu_w, f)
    wsT = np.ascontiguousarray(sw.transpose(0, 3, 1, 2)).reshape(2, 128, 512)
    idx = np.arange(128)
    maskT = ((idx[:, None] // 64) <= (idx[None, :] // 64)).astype(f)
    lng4 = np.asarray(ln_g, f)
    lnb4 = np.asarray(ln_b, f)
    sgb2 = np.asarray(sgu_ln_b, f)
    sgub = np.asarray(sgu_b, f).reshape(2, 512)

    def cols8(v):
        return np.asarray(v, f).reshape(8, 128).T

    base = np.zeros((128, NCOL), f)
    base[:, COL["one"]] = 1.0
    for i in range(2):
        base[:, COL["ps%d" % i]:COL["ps%d" % i] + 8] = cols8(pool_scale[i])
        for j in range(3):
            base[:, COL["cw%d" % i] + j * 8:COL["cw%d" % i] + j * 8 + 8] = cols8(np.asarray(sconv_w)[i, j])
        base[:, COL["cb%d" % i]:COL["cb%d" % i] + 8] = cols8(sconv_b[i])
        base[:, COL["sg%d" % i]:COL["sg%d" % i] + 8] = cols8(sgu_ln_g[i])
        for j in range(31):
            base[:, COL["dw%d" % i] + j:COL["dw%d" % i] + 248:31] = cols8(np.asarray(dconv_w)[i, j])
        base[:, COL["db%d" % i]:COL["db%d" % i] + 8] = cols8(dconv_b[i])
        base[:, COL["ng%d" % i]:COL["ng%d" % i] + 8] = cols8(dnorm_g[i])
        base[:, COL["nb%d" % i]:COL["nb%d" % i] + 8] = cols8(dnorm_b[i])
    gen = _pool_mats()
    maps = []
    for c in range(8):
        b, hf = c // 2, c % 2
        if hf == 0:
            xw = np.concatenate([np.zeros((HALO * 128, D), f), x[b, 0:4096]], axis=0)
        else:
            xw = x[b, 4096 - HALO * 128:8192]
        cm = gen.copy()
        hi, lo, pv = _pool_special(hf == 0)
        cm[PM_SPH:PM_SPH + 4] = hi
        cm[PM_SPL:PM_SPL + 4] = lo
        cm[PM_SPP:PM_SPP + 4] = pv
        cbf = np.ascontiguousarray(cm.transpose(1, 0, 2)).reshape(128, NCB * 128).astype(ml_dtypes.bfloat16)
        cp = base.copy()
        cp[:, COL["flag"]] = float(hf)
        maps.append({
            "xin": np.ascontiguousarray(xw), "consts_bf": cbf, "maskT": maskT, "colpack": cp,
            "ln_g": lng4, "ln_b": lnb4, "sgu_ln_b": sgb2, "sgu_b": sgub,
            "w_xa": w_xa, "w_ev": w_ev, "w_v": w_v, "w_od": w_od, "w_out": w_out,
            "pool_w": pw_l, "wsT": wsT,
        })
    return maps


def kernel(**inputs):
    nc = _get_program(4)
    maps = prepare_inputs(**inputs)
    res = run_bass_kernel_spmd(nc, maps, core_ids=list(range(8)))
    outp = np.empty((4, 8192, D), np.float32)
    for c in range(8):
        b, hf = c // 2, c % 2
        outp[b, hf * 4096:(hf + 1) * 4096] = res.results[c]["out"]
    return outp
```
